# Optimizing a Trainium2 kernel written in Bass

```python
import jax, jax.numpy as jnp
from jax import lax
import numpy as np

D_MODEL = 1024
BATCH = 8
SEQ = 2048
DEPTH = 1
DEC_BATCH = 128
DEC_SEQ = 8
PAST_LEN = 16384
PAGE_SIZE = 128

RET_HEADS = 4
RET_DK = 128
RET_DV = 128
RET_W = RET_HEADS * RET_DV
GDN_HEADS = 4
GDN_DK = 128
GDN_DV = 128
GDN_W = GDN_HEADS * GDN_DV
MIX_W = RET_W + GDN_W
GDN_CONV_CH = 2 * GDN_HEADS * GDN_DK + GDN_HEADS * GDN_DV
CONV_W = 4
CHUNK = 64
D_FF = 4 * D_MODEL
ROPE_BASE = 10000.0
EPS = 1e-6
IN_SIZES = (RET_HEADS * RET_DK, RET_HEADS * RET_DK, RET_W, RET_W, GDN_CONV_CH, GDN_W, GDN_HEADS, GDN_HEADS)
IN_W = 2 * RET_HEADS * RET_DK + 2 * RET_W + GDN_CONV_CH + GDN_W + 2 * GDN_HEADS

kernel_name = "hybrid_retention_gated_delta_step"


def _rmsnorm(x, w):
    x32 = x.astype(jnp.float32)
    y = x32 * lax.rsqrt(jnp.mean(x32 * x32, axis=-1, keepdims=True) + EPS)
    return (y * w.astype(jnp.float32)).astype(x.dtype)


def _l2norm(x):
    return x * lax.rsqrt(jnp.sum(x * x, axis=-1, keepdims=True) + EPS)


def _rope(x, pos):
    d = x.shape[-1]
    inv_freq = ROPE_BASE ** (-jnp.arange(0, d, 2, dtype=jnp.float32) / d)
    ang = pos[:, None] * inv_freq[None, :]
    cos = jnp.cos(ang)[None, :, None, :]
    sin = jnp.sin(ang)[None, :, None, :]
    x1, x2 = x[..., : d // 2], x[..., d // 2:]
    return jnp.concatenate([x1 * cos - x2 * sin, x1 * sin + x2 * cos], axis=-1)


def _to_chunks(a, C):
    L = a.shape[2]
    N = -(-L // C)
    pad = N * C - L
    a = jnp.pad(a, [(0, 0), (0, 0), (0, pad)] + [(0, 0)] * (a.ndim - 3))
    a = a.reshape(a.shape[:2] + (N, C) + a.shape[3:])
    return jnp.moveaxis(a, 2, 0)


def _from_chunks(o, L):
    o = jnp.moveaxis(o, 0, 2)
    B, H, N, C, d = o.shape
    return o.reshape(B, H, N * C, d)[:, :, :L]


def _retention(q, k, v, logd, S0):
    L = q.shape[2]
    C = min(CHUNK, L)
    causal = jnp.tril(jnp.ones((C, C), dtype=bool))

    def step(S, inp):
        qi, ki, vi, gi = inp
        cum = jnp.cumsum(gi, axis=-1)
        dmat = jnp.exp(jnp.where(causal, cum[..., :, None] - cum[..., None, :], -jnp.inf))
        scores = jnp.einsum('bhid,bhjd->bhij', qi, ki) * dmat
        o = (jnp.einsum('bhcd,bhde->bhce', qi * jnp.exp(cum)[..., None], S)
             + jnp.einsum('bhij,bhje->bhie', scores, vi))
        tail = jnp.exp(cum[..., -1:] - cum)
        S = (jnp.exp(cum[..., -1])[..., None, None] * S
             + jnp.einsum('bhcd,bhce->bhde', ki * tail[..., None], vi))
        return S, o

    S, o = lax.scan(step, S0, (_to_chunks(q, C), _to_chunks(k, C), _to_chunks(v, C), _to_chunks(logd, C)))
    return _from_chunks(o, L), S


def _gated_delta(q, k, v, g, beta, S0):
    L = q.shape[2]
    C = min(CHUNK, L)
    causal = jnp.tril(jnp.ones((C, C), dtype=bool))
    strict = jnp.tril(jnp.ones((C, C), dtype=bool), -1)
    eye = jnp.eye(C, dtype=jnp.float32)
    dk = q.shape[-1]

    def step(S, inp):
        qi, ki, vi, gi, bi = inp
        cum = jnp.cumsum(gi, axis=-1)
        dmat = jnp.exp(jnp.where(causal, cum[..., :, None] - cum[..., None, :], -jnp.inf))
        kk = jnp.einsum('bhid,bhjd->bhij', ki, ki)
        A = jnp.where(strict, bi[..., :, None] * kk * dmat, 0.0)
        rhs = jnp.concatenate([ki * (bi * jnp.exp(cum))[..., None], vi * bi[..., None]], axis=-1)
        sol = lax.linalg.triangular_solve(eye + A, rhs, left_side=True, lower=True, unit_diagonal=True)
        w, u = sol[..., :dk], sol[..., dk:]
        v_new = u - jnp.einsum('bhcd,bhde->bhce', w, S)
        scores = jnp.einsum('bhid,bhjd->bhij', qi, ki) * dmat
        o = (jnp.einsum('bhcd,bhde->bhce', qi * jnp.exp(cum)[..., None], S)
             + jnp.einsum('bhij,bhje->bhie', scores, v_new))
        tail = jnp.exp(cum[..., -1:] - cum)
        S = (jnp.exp(cum[..., -1])[..., None, None] * S
             + jnp.einsum('bhcd,bhce->bhde', ki * tail[..., None], v_new))
        return S, o

    xs = (_to_chunks(q, C), _to_chunks(k, C), _to_chunks(v, C), _to_chunks(g, C), _to_chunks(beta, C))
    S, o = lax.scan(step, S0, xs)
    return _from_chunks(o, L), S


def _hybrid_mixer(xn, pos_offset, s_ret, s_gdn, s_conv, w_in, conv_w, A_log, dt_bias,
                  ret_norm_w, gdn_norm_w, w_out):
    f32 = jnp.float32
    B, L, _ = xn.shape
    proj = jnp.einsum('bld,de->ble', xn, w_in).astype(f32)
    offs = np.cumsum(IN_SIZES)[:-1].tolist()
    q_r, k_r, v_r, gate_r, qkv_g, z_g, a_g, b_g = jnp.split(proj, offs, axis=-1)

    pos = jnp.arange(L, dtype=f32) + jnp.asarray(pos_offset, f32)
    q_r = _rope(q_r.reshape(B, L, RET_HEADS, RET_DK), pos)
    k_r = _rope(k_r.reshape(B, L, RET_HEADS, RET_DK), pos) * (RET_DK ** -0.5)
    v_r = v_r.reshape(B, L, RET_HEADS, RET_DV)
    log_gamma = jnp.log(1.0 - 2.0 ** (-5.0 - jnp.arange(RET_HEADS, dtype=f32)))
    logd = jnp.broadcast_to(log_gamma[None, :, None], (B, RET_HEADS, L))
    o_r, S_r = _retention(q_r.transpose(0, 2, 1, 3), k_r.transpose(0, 2, 1, 3),
                          v_r.transpose(0, 2, 1, 3), logd, s_ret.astype(f32))
    o_r = _rmsnorm(o_r.transpose(0, 2, 1, 3), ret_norm_w).reshape(B, L, RET_W) * jax.nn.silu(gate_r)

    full = jnp.concatenate([s_conv.astype(f32), qkv_g], axis=1)
    cw = conv_w.astype(f32)
    conv = full[:, 0:L] * cw[0]
    for t in range(1, CONV_W):
        conv = conv + full[:, t:t + L] * cw[t]
    conv = jax.nn.silu(conv)
    new_conv = full[:, -(CONV_W - 1):]
    nqk = GDN_HEADS * GDN_DK
    q_g = _l2norm(conv[..., :nqk].reshape(B, L, GDN_HEADS, GDN_DK)) * (GDN_DK ** -0.5)
    k_g = _l2norm(conv[..., nqk:2 * nqk].reshape(B, L, GDN_HEADS, GDN_DK))
    v_g = conv[..., 2 * nqk:].reshape(B, L, GDN_HEADS, GDN_DV)
    g = -jnp.exp(A_log.astype(f32)) * jax.nn.softplus(a_g + dt_bias.astype(f32))
    beta = jax.nn.sigmoid(b_g)
    o_g, S_g = _gated_delta(q_g.transpose(0, 2, 1, 3), k_g.transpose(0, 2, 1, 3),
                            v_g.transpose(0, 2, 1, 3), g.transpose(0, 2, 1),
                            beta.transpose(0, 2, 1), s_gdn.astype(f32))
    o_g = _rmsnorm(o_g.transpose(0, 2, 1, 3), gdn_norm_w).reshape(B, L, GDN_W) * jax.nn.silu(z_g)

    mix = jnp.concatenate([o_r, o_g], axis=-1).astype(w_out.dtype)
    y = jnp.einsum('ble,ed->bld', mix, w_out).astype(xn.dtype)
    return y, S_r, S_g, new_conv


def _layer(x, pos_offset, s_ret, s_gdn, s_conv, pre_mix_w, w_in, conv_w, A_log, dt_bias,
           ret_norm_w, gdn_norm_w, w_out, post_mix_w, pre_mlp_w, w_up, w_down, post_mlp_w):
    h = _rmsnorm(x, pre_mix_w)
    m, S_r, S_g, new_conv = _hybrid_mixer(h, pos_offset, s_ret, s_gdn, s_conv, w_in, conv_w, A_log,
                                          dt_bias, ret_norm_w, gdn_norm_w, w_out)
    x = x + _rmsnorm(m, post_mix_w)
    h = _rmsnorm(x, pre_mlp_w)
    f = jnp.square(jax.nn.relu(jnp.einsum('bld,df->blf', h, w_up)))
    f = jnp.einsum('blf,fd->bld', f, w_down)
    x = x + _rmsnorm(f, post_mlp_w)
    return x, S_r, S_g, new_conv


def setup_inputs(seed: int = 0) -> dict:
    key = jax.random.key(seed)
    ks = jax.random.split(key, 20)
    f32 = jnp.float32
    nrm = lambda k, s, sc: jax.random.normal(k, s, f32) * sc
    gain = lambda k, s: 1.0 + 0.02 * jax.random.normal(k, s, f32)
    dt = jnp.exp(jax.random.uniform(ks[17], (DEPTH, GDN_HEADS), f32, np.log(1e-3), np.log(1e-1)))
    return {
        "x_prompt": nrm(ks[0], (BATCH, SEQ, D_MODEL), 1.0),
        "x_sample": nrm(ks[1], (DEC_BATCH, DEC_SEQ, D_MODEL), 1.0),
        "state_ret": nrm(ks[2], (DEPTH, DEC_BATCH, RET_HEADS, RET_DK, RET_DV), 0.05),
        "state_gdn": nrm(ks[3], (DEPTH, DEC_BATCH, GDN_HEADS, GDN_DK, GDN_DV), 0.05),
        "state_conv": nrm(ks[4], (DEPTH, DEC_BATCH, CONV_W - 1, GDN_CONV_CH), 1.0),
        "pre_mix_w": gain(ks[5], (DEPTH, D_MODEL)),
        "w_in": nrm(ks[6], (DEPTH, D_MODEL, IN_W), D_MODEL ** -0.5),
        "conv_w": nrm(ks[7], (DEPTH, CONV_W, GDN_CONV_CH), CONV_W ** -0.5),
        "A_log": jnp.log(jax.random.uniform(ks[8], (DEPTH, GDN_HEADS), f32, 1.0, 16.0)),
        "dt_bias": dt + jnp.log(-jnp.expm1(-dt)),
        "ret_norm_w": gain(ks[9], (DEPTH, RET_DV)),
        "gdn_norm_w": gain(ks[10], (DEPTH, GDN_DV)),
        "w_out": nrm(ks[11], (DEPTH, MIX_W, D_MODEL), MIX_W ** -0.5),
        "post_mix_w": gain(ks[12], (DEPTH, D_MODEL)),
        "pre_mlp_w": gain(ks[13], (DEPTH, D_MODEL)),
        "w_up": nrm(ks[14], (DEPTH, D_MODEL, D_FF), D_MODEL ** -0.5),
        "w_down": nrm(ks[15], (DEPTH, D_FF, D_MODEL), D_FF ** -0.5),
        "post_mlp_w": gain(ks[16], (DEPTH, D_MODEL)),
    }


def reference(x_prompt, x_sample, state_ret, state_gdn, state_conv, pre_mix_w, w_in, conv_w, A_log,
              dt_bias, ret_norm_w, gdn_norm_w, w_out, post_mix_w, pre_mlp_w, w_up, w_down, post_mlp_w):
    f32 = jnp.float32
    B = x_prompt.shape[0]
    yp, ys = x_prompt, x_sample
    rp, gp, cp, rs, gs, cs = [], [], [], [], [], []
    for l in range(DEPTH):
        params = (pre_mix_w[l], w_in[l], conv_w[l], A_log[l], dt_bias[l], ret_norm_w[l], gdn_norm_w[l],
                  w_out[l], post_mix_w[l], pre_mlp_w[l], w_up[l], w_down[l], post_mlp_w[l])
        z_r = jnp.zeros((B, RET_HEADS, RET_DK, RET_DV), f32)
        z_g = jnp.zeros((B, GDN_HEADS, GDN_DK, GDN_DV), f32)
        z_c = jnp.zeros((B, CONV_W - 1, GDN_CONV_CH), f32)
        yp, s1, s2, s3 = _layer(yp, 0, z_r, z_g, z_c, *params)
        ys, t1, t2, t3 = _layer(ys, PAST_LEN, state_ret[l], state_gdn[l], state_conv[l], *params)
        rp.append(s1.astype(state_ret.dtype)); gp.append(s2.astype(state_gdn.dtype)); cp.append(s3.astype(state_conv.dtype))
        rs.append(t1.astype(state_ret.dtype)); gs.append(t2.astype(state_gdn.dtype)); cs.append(t3.astype(state_conv.dtype))
    return (yp, ys, jnp.stack(rp), jnp.stack(gp), jnp.stack(cp), jnp.stack(rs), jnp.stack(gs), jnp.stack(cs))
```

```python
from contextlib import ExitStack
import numpy as np
import concourse.bass as bass
import concourse.mybir as mybir
from concourse.bass_utils import run_bass_kernel_spmd

F32 = mybir.dt.float32
BF16 = mybir.dt.bfloat16
ALU = mybir.AluOpType
AF = mybir.ActivationFunctionType

D = 1024
NT = 17
INW = 4104
EPS = 1e-6
ENGS = ["pe", "act", "dve", "pool", "sp"]


class Prog:
    def __init__(self):
        self.ops = []
        self.cnt = {e: 0 for e in ENGS}
        self.obs = {e: {} for e in ENGS}
        self.bufs = {}
        self.dsem = {}
        self.pnext = {}
        self.rec = None
        self.collect = None
        self.alias = {"S%d" % i: ("S%da" % i, "S%db" % i) for i in range(15, 20)}

    def _b(self, k):
        if k not in self.bufs:
            self.bufs[k] = {"w": None, "r": []}
        return self.bufs[k]

    def _need(self, eng, tok, waits):
        if tok is None:
            return
        sk, v = tok
        if eng == "pe" and sk == "E:pe":
            return
        if self.obs[eng].get(sk, 0) >= v:
            return
        self.obs[eng][sk] = v
        waits.append((sk, v))

    def _exp(self, keys):
        out = []
        for k in keys:
            out.extend(self.alias.get(k, (k,)))
        return out

    def op(self, eng, fn, r=(), w=(), dma=None):
        if self.rec is not None:
            self.rec.append((eng, fn, tuple(r), tuple(w), dma))
            return
        r = self._exp(r)
        w = self._exp(w)
        if self.collect is not None:
            self.collect.append((eng, fn, tuple(r), tuple(w), dma))
            return
        waits = []
        for k in r:
            self._need(eng, self._b(k)["w"], waits)
            if k.startswith("pb"):
                for t in self._b(k)["r"]:
                    if t[0] != "E:" + eng:
                        self._need(eng, t, waits)
        for k in w:
            b = self._b(k)
            self._need(eng, b["w"], waits)
            for t in b["r"]:
                self._need(eng, t, waits)
        if dma is None:
            self.cnt[eng] += 1
            tok = ("E:" + eng, self.cnt[eng])
            inc = ("E:" + eng, 1)
        else:
            npool = 16 if eng == "sp" else 8
            i = self.pnext.get(eng, 0)
            self.pnext[eng] = (i + 1) % npool
            sk = "D:%s%02d" % (eng, i)
            prev = self.dsem.get(sk, 0)
            if prev > 0:
                self._need(eng, (sk, prev), waits)
            self.dsem[sk] = prev + 16
            tok = (sk, self.dsem[sk])
            inc = (sk, 16)
        for k in r:
            self._b(k)["r"].append(tok)
        for k in w:
            b = self._b(k)
            b["w"] = tok
            b["r"] = []
        self.ops.append((eng, fn, waits, inc))

    def regroup(self, keys, dma):
        return

    def record(self, f):
        self.rec = []
        f()
        ops, self.rec = self.rec, None
        return ops

    def play(self, ops):
        for (eng, fn, r, w, dma) in ops:
            self.op(eng, fn, r, w, dma)

    def barrier_keys(self, keys):
        if self.collect is not None:
            self.collect.append(("barrier", None, tuple(keys), (), None))
            return
        toks = [("E:" + e, self.cnt[e]) for e in ENGS if self.cnt[e] > 0] + list(self.dsem.items())
        for k in self._exp(keys):
            self._b(k)["r"] = list(toks)

    def emit(self, nc, stack):
        sems = {}
        keys = ["E:" + e for e in ENGS] + sorted(self.dsem.keys())
        for i, k in enumerate(keys):
            sems[k] = stack.enter_context(nc.semaphore("s%d" % i))
        final = list(self.dsem.items())
        block = stack.enter_context(nc.Block())

        def run(engname):
            def body(e):
                for (eng, fn, waits, inc) in self.ops:
                    if eng != engname:
                        continue
                    for (sk, v) in waits:
                        e.wait_ge(sems[sk], v)
                    fn(e).then_inc(sems[inc[0]], inc[1])
                if engname == "sp":
                    for (sk, v) in final:
                        e.wait_ge(sems[sk], v)
            return body

        block.tensor(run("pe"))
        block.scalar(run("act"))
        block.vector(run("dve"))
        block.gpsimd(run("pool"))
        block.sync(run("sp"))


class _Mock:
    def __init__(self):
        self.name = None
        self.args = ()
        self.kw = {}

    def __getattr__(self, name):
        def f(*a, **kw):
            self.name, self.args, self.kw = name, a, kw
            return self
        return f


def _free(ap):
    n = 1
    for d in ap.shape[1:]:
        n *= int(d)
    return n


def _op_cost(eng, fn, dma):
    m = _Mock()
    try:
        fn(m)
    except Exception:
        return 0.3, 0.3
    out = m.kw.get("out", m.args[0] if m.args else None)
    n = _free(out) if out is not None else 512
    if dma is not None:
        nbytes = n * int(out.shape[0]) * (2 if out.dtype == BF16 else 4)
        lat = 2.0 + nbytes / 150e3
        if eng == "pool":
            return (5.0 if nbytes > 500000 else 0.3), lat
        return (1.5 if nbytes > 200000 else 0.3), lat
    if eng == "pe":
        if m.name == "transpose":
            src = m.args[1] if len(m.args) > 1 else m.kw.get("in_")
            c = 0.07 + (2.0 if src.dtype == F32 else 1.0) * n / 2400.0
        else:
            lhsT = m.kw.get("lhsT")
            c = 0.06 + (4.0 if lhsT.dtype == F32 else 1.0) * n / 2400.0
        return c, c + 0.25
    if eng == "act":
        c = 0.22 + n / 1000.0
    elif eng == "dve":
        c = 0.15 + n / 960.0
    else:
        c = 0.3 + n / 450.0
    return c, c + 0.1


_SCHED_DEBUG = False
_SCHED_WINDOW = 3000
_SCHED_EPS = 0.0
_SEG_EPS = (0.1, 0.05)


def _schedule(ops, window=None):
    window = window or _SCHED_WINDOW
    n = len(ops)
    deps = [set() for _ in range(n)]
    lastw = {}
    readers = {}
    for i, (eng, fn, r, w, dma) in enumerate(ops):
        for k in r:
            if k in lastw:
                deps[i].add(lastw[k])
        for k in w:
            if k in lastw:
                deps[i].add(lastw[k])
            for j in readers.get(k, ()):
                deps[i].add(j)
        for k in r:
            readers.setdefault(k, []).append(i)
        for k in w:
            lastw[k] = i
            readers[k] = []
        deps[i].discard(i)
    users = [[] for _ in range(n)]
    ndep = [len(d) for d in deps]
    for i, d in enumerate(deps):
        for j in d:
            users[j].append(i)
    cost = [_op_cost(e, f, d) for (e, f, r, w, d) in ops]
    finish = [0.0] * n
    ready = [0.0] * n
    avail = {e: [] for e in ENGS}
    done = [False] * n
    efree = {e: 0.0 for e in ENGS}
    lo = 0
    hi = 0

    def admit(upto):
        nonlocal hi
        while hi < min(upto, n):
            if ndep[hi] == 0:
                avail[ops[hi][0]].append(hi)
            hi += 1
    blev = [0.0] * n
    if _SCHED_EPS > 0:
        for i in range(n - 1, -1, -1):
            m = 0.0
            for u in users[i]:
                if blev[u] > m:
                    m = blev[u]
            blev[i] = m + cost[i][1]
    order = []
    bind = [None] * n
    elast = {e: None for e in ENGS}
    rsrc = [None] * n
    admit(window)
    while len(order) < n:
        best = None
        for e in ENGS:
            be = None
            if _SCHED_EPS > 0 and avail[e]:
                smin = min(max(ready[i], efree[e]) for i in avail[e])
                for i in avail[e]:
                    st_ = max(ready[i], efree[e])
                    if st_ <= smin + _SCHED_EPS:
                        key2 = (-blev[i], i)
                        if be is None or key2 < be[2]:
                            be = ((st_, i), i, key2)
            else:
                for i in avail[e]:
                    st_ = max(ready[i], efree[e])
                    key = (st_, i)
                    if be is None or key < be[0]:
                        be = (key, i)
            if be is not None and (best is None or be[0] < best[0]):
                best = be
        if best is None:
            admit(hi + window)
            continue
        (st_, i) = best[0]
        e = ops[i][0]
        avail[e].remove(i)
        busy, lat = cost[i]
        bind[i] = elast[e] if (efree[e] >= ready[i] and elast[e] is not None) else rsrc[i]
        elast[e] = i
        efree[e] = st_ + busy
        finish[i] = st_ + lat
        done[i] = True
        order.append(i)
        for u in users[i]:
            ndep[u] -= 1
            if finish[i] >= ready[u]:
                rsrc[u] = i
            ready[u] = max(ready[u], finish[i])
            if ndep[u] == 0 and u < hi:
                avail[ops[u][0]].append(u)
        while lo < n and done[lo]:
            lo += 1
        admit(lo + window)
    if _SCHED_DEBUG:
        import collections
        agg = collections.Counter()
        i = max(range(n), key=lambda j: finish[j]) if n else None
        while i is not None:
            j = bind[i]
            t_prev = finish[j] if j is not None else 0.0
            agg[(ops[i][0], ops[i][1].__code__.co_firstlineno)] += finish[i] - t_prev
            i = j
        print('CRIT', [(k, round(v, 1)) for k, v in agg.most_common(40)])
        print('SCHED segment n=%d makespan=%.1f us busy=%s' % (n, max(finish) if n else 0.0, {e: round(sum(cost[i][0] for i in range(n) if ops[i][0] == e), 1) for e in ENGS}))
    return [ops[i] for i in order]


def hs(h):
    return slice(h * 128, (h + 1) * 128)


def build_nc(tiles1=None, tiles2=None, dbg=False, skip=(), stage=99, sched=True):
    tiles1 = list(range(NT)) if tiles1 is None else tiles1
    tiles2 = list(range(NT)) if tiles2 is None else tiles2
    nc = bass.Bass("TRN2", target_bir_lowering=False)

    def din(name, shape, dt=F32):
        return nc.dram_tensor(name, list(shape), dt, kind="ExternalInput").ap()

    def dout(name, shape):
        return nc.dram_tensor(name, list(shape), F32, kind="ExternalOutput").ap()

    x_d = din("x", [NT * 128, D])
    sret_d = din("sret", [16, 4, 128, 128])
    sgdn_d = din("sgdn", [16, 4, 128, 128])
    sconv_d = din("sconv", [48, 1536])
    win_d = din("w_in", [D, INW])
    wout_d = din("w_out", [D, D])
    wup_d = din("w_up", [D, 4096])
    wdn_d = din("w_down", [4096, D])
    cols_d = din("cols", [128, 32])
    rows_d = din("rows", [128, 2048])
    cw_d = din("cw", [128, 48])
    ad_d = din("ad", [128, 8])
    rope_d = din("rope", [NT * 128, 256])
    cm_d = din("cmask", [128, 2 * 4 * 128 + 2 * 2 * 512 + 128 + 128 + 8 + 16])
    y_d = dout("y", [NT * 128, D])
    sretp_d = dout("sret_p", [4, 128, 128])
    sgdnp_d = dout("sgdn_p", [4, 128, 128])
    convp_d = dout("conv_p", [3, 1536])
    srets_d = dout("sret_s", [16, 4, 128, 128])
    sgdns_d = dout("sgdn_s", [16, 4, 128, 128])
    convs_d = dout("conv_s", [48, 1536])
    x1_d = (nc.dram_tensor("x1scr", [NT * 128, D], F32, kind="ExternalOutput").ap() if dbg
            else nc.dram_tensor("x1scr", [NT * 128, D], F32).ap())

    wup16_d = nc.dram_tensor("wup16", [D, 4096], BF16).ap()
    wdn16_d = nc.dram_tensor("wdn16", [4096, D], BF16).ap()

    P = Prog()
    P.collect = []
    st = ExitStack()
    with st:
        def sb(name, shape, dt=F32):
            return st.enter_context(nc.sbuf_tensor("sb_" + name, list(shape), dt))

        WA = sb("WA", [128, 41024], BF16)
        WB = sb("WB", [128, 12288], F32)
        WBb = WB.bitcast(BF16)
        cmask = sb("cmask", [128, cm_d.shape[1]])
        cols = sb("cols", [128, 32])
        rows = sb("rows", [128, 1024])
        gsilB = sb("gsilB", [128, 1024])
        cwt = sb("cwt", [128, 48])
        adt = sb("adt", [128, 8])
        negA = sb("negA", [128, 4])
        Sp = {"r": sb("SpR", [128, 512]), "g": sb("SpG", [128, 512])}
        Sp16 = {"r": sb("Sp16R", [128, 512], BF16), "g": sb("Sp16G", [128, 512], BF16)}
        ident16 = sb("ident16", [128, 128], BF16)
        ones16 = sb("ones16", [128, 128], BF16)
        xt = sb("xt", [128, 1024])
        xtB = sb("xtB", [128, 1024])
        qrkr16 = sb("qrkr16", [128, 1024], BF16)
        gsil = sb("gsil", [128, 1024])
        v_r = sb("v_r", [128, 512])
        convbuf = sb("convbuf", [128, 12 * 176])
        mixT = sb("mixT", [128, 1024], BF16)
        ropet = sb("ropet", [128, 256])
        xnT = sb("xnT", [128, 1024], BF16)
        x1t = sb("x1t", [128, 1024])
        SS = sb("SS", [128, 16 * 128])
        smallc = sb("smallc", [128, 168])
        smallo = sb("smallo", [128, 8])
        smalla = sb("smalla", [128, 16])
        smallg = sb("smallg", [128, 32])
        small2 = sb("small2", [128, 32])
        xgB = sb("xgB", [128, 2048])
        pb = [st.enter_context(nc.psum_tensor("pb%d" % i, [128, 512], F32)) for i in range(8)]
        PB = ["pb%d" % i for i in range(8)]

        o = 0
        U = {}; MG = {}; CMI = {}; CMS = {}; DTR = {}; QSC = {}
        for ti, typ in enumerate("PS"):
            U[typ] = cmask[:, o:o + 128]; o += 128
            MG[typ] = cmask[:, o:o + 128]; o += 128
            CMI[typ] = cmask[:, o:o + 128]; o += 128
            CMS[typ] = cmask[:, o:o + 128]; o += 128
        for typ in "PS":
            DTR[typ] = cmask[:, o:o + 512]; o += 512
            QSC[typ] = cmask[:, o:o + 512]; o += 512
        ident = cmask[:, o:o + 128]; o += 128
        ones = cmask[:, o:o + 128]; o += 128
        KTC = {"P": cmask[:, o:o + 4], "S": cmask[:, o + 4:o + 8]}; o += 8
        segm = cmask[:, o:o + 16]; o += 16

        def SL(i, n=1):
            return WB[:, i * 512:(i + n) * 512]

        def SK(i, n=1):
            return ["S%d" % j for j in range(i, i + n)]

        xn32 = WB[:, 10240:11264]
        XN32 = ["xn32"]

        GAM = [1.0 - 2.0 ** (-5.0 - h) for h in range(4)]

        P.op("sp", lambda e: e.dma_start(out=cmask[:], in_=cm_d), w=["cmask"], dma="cmask")
        P.op("sp", lambda e: e.dma_start(out=cols[:], in_=cols_d), w=["cols"], dma="cols")
        P.op("sp", lambda e: e.dma_start(out=rows[:], in_=rows_d[:, 0:1024]), w=["rows"], dma="rows")
        P.op("sp", lambda e: e.dma_start(out=cwt[:], in_=cw_d), w=["cwt"], dma="cwt")
        P.op("sp", lambda e: e.dma_start(out=adt[:], in_=ad_d), w=["adt"], dma="adt")
        for k in range(0 if 'wload1' not in skip else 99, 8):
            P.op("pool", lambda e, k=k: e.dma_start(out=WA[:, k * INW:(k + 1) * INW], in_=win_d[k * 128:(k + 1) * 128, :]),
                 w=["win%d" % k], dma="win")
        for k in range(0 if 'wload1' not in skip else 99, 8):
            P.op("pool", lambda e, k=k: e.dma_start(out=WA[:, 32832 + k * 1024:32832 + (k + 1) * 1024], in_=wout_d[k * 128:(k + 1) * 128, :]),
                 w=["wout%d" % k], dma="wout")
        WIN = ["win%d" % k for k in range(8)]
        WOUT = ["wout%d" % k for k in range(8)]
        if "wload1" not in skip:
            P.regroup(WIN, "win")
            P.regroup(WOUT, "wout")
        P.op("act", lambda e: e.activation(out=negA[:], in_=adt[:, 0:4], func=AF.Exp), r=["adt"], w=["negA"])
        P.op("dve", lambda e: e.tensor_scalar(out=negA[:], in0=negA[:], scalar1=-1.0, scalar2=None, op0=ALU.mult), r=["negA"], w=["negA"])
        P.op("dve", lambda e: e.memset(Sp["r"][:], 0.0), w=["SpR"])
        P.op("dve", lambda e: e.memset(Sp["g"][:], 0.0), w=["SpG"])
        P.op("dve", lambda e: e.memset(Sp16["r"][:], 0.0), w=["Sp16R"])
        P.op("dve", lambda e: e.memset(Sp16["g"][:], 0.0), w=["Sp16G"])
        P.op("act", lambda e: e.activation(out=ident16[:], in_=ident, func=AF.Copy), r=["cmask"], w=["c16"])
        P.op("act", lambda e: e.activation(out=ones16[:], in_=ones, func=AF.Copy), r=["cmask"], w=["c16"])
        P.op("pool", lambda e: e.memset(convbuf[:], 0.0), w=["convbuf"])

        def win(k, c0, c1):
            return WA[:, k * INW + c0:k * INW + c1]

        def rms_cols(src_ap, srckeys, n_inv, junk_ap, junkkeys, outcol, tmpc):
            P.op("act", lambda e: e.activation(out=junk_ap, in_=src_ap, func=AF.Square, accum_out=smalla[:, tmpc:tmpc + 1]),
                 r=srckeys, w=junkkeys + ["smalla"])
            P.op("act", lambda e: e.activation(out=smalla[:, tmpc + 1:tmpc + 2], in_=smalla[:, tmpc:tmpc + 1], func=AF.Ln, scale=n_inv, bias=EPS),
                 r=["smalla"], w=["smalla"])
            P.op("act", lambda e: e.activation(out=smalla[:, outcol:outcol + 1], in_=smalla[:, tmpc + 1:tmpc + 2], func=AF.Exp, scale=-0.5),
                 r=["smalla"], w=["smalla"])

        def norm_transpose(src, srckeys, wc0, dstT, dstkeys):
            rms_cols(src[:], srckeys, 1.0 / D, xn32, XN32, 8, 0)
            P.op("dve", lambda e: e.tensor_scalar(out=xn32, in0=src[:], scalar1=smalla[:, 8:9], scalar2=None, op0=ALU.mult),
                 r=srckeys + ["smalla"], w=XN32)
            for k in range(8):
                P.op("pe", lambda e, k=k: e.transpose(pb[k // 4][:, hs(k % 4)], xn32[:, hs(k)], ident),
                     r=XN32 + ["cmask"], w=[PB[k // 4]])
            for hf in range(2):
                P.op("dve", lambda e, hf=hf: e.tensor_tensor(
                    out=dstT[:, hf * 512:(hf + 1) * 512].rearrange("p (k t) -> p k t", k=4),
                    in0=pb[hf][:].rearrange("p (k t) -> p k t", k=4),
                    in1=cols[:, wc0 + hf * 4:wc0 + hf * 4 + 4].unsqueeze(2).to_broadcast([128, 4, 128]), op=ALU.mult),
                    r=[PB[hf], "cols"], w=dstkeys)

        SSB = [SS, xgB]

        def load_states(kind, h, par):
            src = sret_d if kind == "r" else sgdn_d
            P.op("sp", lambda e: e.dma_start(out=SSB[par][:].rearrange("p (s e) -> p s e", s=16), in_=src[:, h].rearrange("s d e -> d s e")),
                 w=["SS%d" % par], dma="SS%d" % par)

        def state_mm(typ, kind, h, bank, rhs_tile, rhskeys, par=0):
            SS = SSB[par]
            if typ == "P":
                P.op("pe", lambda e: e.matmul(pb[bank][:, hs(h)], lhsT=Sp16[kind][:, hs(h)], rhs=rhs_tile[:, hs(h)], start=False, stop=True),
                     r=["Sp16" + kind.upper()] + rhskeys, w=[PB[bank]])
            else:
                for s in range(16):
                    c0 = h * 128 + s * 8
                    P.op("pe", lambda e, s=s, c0=c0: e.matmul(pb[bank][:, c0:c0 + 8], lhsT=SS[:, s * 128:(s + 1) * 128], rhs=rhs_tile[:, c0:c0 + 8], start=False, stop=(s == 15)),
                         r=["SS%d" % par] + rhskeys, w=[PB[bank]])

        def state_update(typ, kind, h, kt_tile, ktkeys, v_tile, vkeys, dec_f, dec_ap, out_d, par=0, ks=9):
            SS = SSB[par]
            if typ == "P":
                P.op("pe", lambda e: e.matmul(pb[4][:, hs(h)], lhsT=kt_tile[:, hs(h)], rhs=v_tile[:, hs(h)], start=True, stop=True),
                     r=ktkeys + vkeys, w=[PB[4]])
                key = "Sp" + kind.upper()
                sc = dec_f if dec_ap is None else dec_ap[:, h:h + 1]
                rk = [PB[4], key] + ([] if dec_ap is None else ["smallc"])
                P.op("dve", lambda e: e.scalar_tensor_tensor(out=Sp[kind][:, hs(h)], in0=Sp[kind][:, hs(h)], scalar=sc, in1=pb[4][:, hs(h)], op0=ALU.mult, op1=ALU.add),
                     r=rk, w=[key])
                P.op("pool", lambda e: e.tensor_copy(out=Sp16[kind][:, hs(h)], in_=Sp[kind][:, hs(h)]), r=[key], w=["Sp16" + kind.upper()])
            else:
                kexp = WB[:, ks * 512:(ks + 4) * 512]
                KEXP = SK(ks, 4)
                SoS = WB[:, 15 * 512:19 * 512]
                SOS = SK(15, 4)
                P.op("dve", lambda e: e.tensor_tensor(out=kexp.rearrange("p (s d) -> p s d", s=16),
                                                     in0=kt_tile[:, hs(h)].unsqueeze(1).to_broadcast([128, 16, 128]),
                                                     in1=segm.unsqueeze(2).to_broadcast([128, 16, 128]), op=ALU.mult),
                     r=ktkeys + ["cmask"], w=KEXP)
                for g4 in range(4):
                    bank = 4 + (g4 % 2)
                    for q in range(4):
                        s = g4 * 4 + q
                        P.op("pe", lambda e, s=s, q=q, bank=bank: e.matmul(pb[bank][:, hs(q)], lhsT=kexp[:, s * 128:(s + 1) * 128], rhs=v_tile[:, hs(h)], start=True, stop=True),
                             r=[KEXP[g4]] + vkeys, w=[PB[bank]])
                    if dec_ap is None:
                        P.op("dve", lambda e, g4=g4, bank=bank: e.scalar_tensor_tensor(out=SoS[:, g4 * 512:(g4 + 1) * 512], in0=SS[:, g4 * 512:(g4 + 1) * 512], scalar=dec_f, in1=pb[bank][:], op0=ALU.mult, op1=ALU.add),
                             r=[PB[bank], "SS%d" % par], w=[SOS[g4]])
                    for q in (range(4) if dec_ap is not None else ()):
                        s = g4 * 4 + q
                        sc = dec_f if dec_ap is None else dec_ap[:, s * 4 + h:s * 4 + h + 1]
                        rk = [PB[bank], "SS%d" % par] + ([] if dec_ap is None else ["smallc"])
                        P.op("dve", lambda e, s=s, q=q, sc=sc, bank=bank: e.scalar_tensor_tensor(out=SoS[:, s * 128:(s + 1) * 128], in0=SS[:, s * 128:(s + 1) * 128], scalar=sc, in1=pb[bank][:, hs(q)], op0=ALU.mult, op1=ALU.add),
                             r=rk, w=[SOS[g4]])
                    P.op("sp", lambda e, g4=g4: e.dma_start(out=out_d[g4 * 4:(g4 + 1) * 4, h].rearrange("s d e -> d s e"), in_=SoS[:, g4 * 512:(g4 + 1) * 512].rearrange("p (s e) -> p s e", s=4)),
                         r=[SOS[g4]], dma="sout")

        wup_done = [False]
        pending = [[]]

        cast_next = [0]

        def emit_casts(n):
            if 'wload2' in skip:
                return
            while n > 0 and cast_next[0] < 16:
                i = cast_next[0]
                cast_next[0] += 1
                n -= 1
                if i < 8:
                    P.op("pool", lambda e, i=i: e.dma_start(out=wup16_d[i * 128:(i + 1) * 128, :], in_=wup_d[i * 128:(i + 1) * 128, :]),
                         w=["wup16_%d" % i], dma="cast")
                else:
                    j = i - 8
                    P.op("pool", lambda e, j=j: e.dma_start(out=wdn16_d[j * 512:(j + 1) * 512, :], in_=wdn_d[j * 512:(j + 1) * 512, :]),
                         w=["wdn16_%d" % j], dma="cast")

        def emit_wup():
            wup_done[0] = True
            if 'wload2' in skip:
                return
            emit_casts(99)
            for k in range(8):
                P.op("sp", lambda e, k=k: e.dma_start(out=WA[:, k * 4096:(k + 1) * 4096], in_=wup16_d[k * 128:(k + 1) * 128, :]),
                     r=["wup16_%d" % k], w=WIN + ["wup%d" % k], dma="wup")

        def sweep1_tile(T):
            typ = "P" if T < 16 else "S"
            lp = (typ == "P")
            gsl = [gsil, gsilB][T % 2]
            GSLK = "gsil%d" % (T % 2)

            def V(ap):
                if not lp:
                    return ap
                v = ap.bitcast(BF16)
                return v[:, 0:v.shape[1] // 2]

            def TP(bank):
                return pb[bank][:].bitcast(BF16)[:, 0:512] if lp else pb[bank][:]
            identV = ident16[:] if lp else ident
            onesV = ones16[:] if lp else ones
            vr = V(v_r[:])
            CH32 = True

            def VC(ap):
                return ap if CH32 else V(ap)

            def TPC(bank):
                return pb[bank][:] if CH32 else TP(bank)
            identC = ident if CH32 else identV
            nseg = 1 if typ == "P" else 16
            L = 128 // nseg
            r0 = T * 128
            XT = [xt, xtB]
            xcur = XT[T % 2]
            XK = "xt%d" % (T % 2)

            def load_x(Tn):
                buf = XT[Tn % 2]
                P.op("sp", lambda e: e.dma_start(out=buf[:], in_=x_d[Tn * 128:(Tn + 1) * 128, :]), w=["xt%d" % (Tn % 2)], dma="xt")
            if T == tiles1[0]:
                load_x(T)
            P.op("sp", lambda e, r0=r0: e.dma_start(out=ropet[:], in_=rope_d[r0:r0 + 128, :]), w=["ropet"], dma="ropet")
            norm_transpose(xcur, [XK], 0, xnT, ["xnT"])

            if typ == "P":
                cb4 = convbuf[:, 0:12 * 131].rearrange("p (c s t) -> p c s t", c=12, s=1)
            else:
                cb4 = convbuf[:, 0:12 * 176].rearrange("p (c s t) -> p c s t", c=12, s=16)
                sct = WB[0:48, 15 * 512:18 * 512]
                P.op("sp", lambda e: e.dma_start(out=sct, in_=sconv_d), w=SK(15, 3), dma="sct")
                for c in range(12):
                    bank = 5 + c // 8
                    cc = c % 8
                    P.op("pe", lambda e, c=c, bank=bank, cc=cc: e.transpose(pb[bank][:, cc * 48:(cc + 1) * 48], sct[:, hs(c)], ident[0:48, 0:48]),
                         r=SK(15, 3) + ["cmask"], w=[PB[bank]])
                P.op("act", lambda e: e.activation(out=cb4[:, 0:8, :, 0:3], in_=pb[5][:, 0:384].rearrange("p (c s t) -> p c s t", c=8, s=16), func=AF.Copy),
                     r=[PB[5]], w=["convbuf"])
                P.op("act", lambda e: e.activation(out=cb4[:, 8:12, :, 0:3], in_=pb[6][:, 0:192].rearrange("p (c s t) -> p c s t", c=4, s=16), func=AF.Copy),
                     r=[PB[6]], w=["convbuf"])

            pend, pending[0] = pending[0], []
            TMB = (4, 0, 1) if lp else (2, 3, 4)
            NSLOT = 23
            pslot = [0]

            def play_pend():
                if pend:
                    i = pslot[0]
                    P.play(pend[(len(pend) * i) // NSLOT:(len(pend) * (i + 1)) // NSLOT])
                pslot[0] += 1
            for c in range(12):
                bank = 5 + c // 4
                for k in range(8):
                    P.op("pe", lambda e, c=c, k=k, bank=bank: e.matmul(pb[bank][:, hs(c % 4)], lhsT=win(k, 2048 + c * 128, 2048 + (c + 1) * 128), rhs=xnT[:, hs(k)], start=(k == 0), stop=(k == 7)),
                         r=[WIN[k], "xnT"], w=[PB[bank]])
                play_pend()
            for g in range(3):
                P.op("act", lambda e, g=g: e.activation(out=cb4[:, g * 4:(g + 1) * 4, :, 3:3 + L], in_=pb[5 + g][:].rearrange("p (c s t) -> p c s t", c=4, s=nseg), func=AF.Copy),
                     r=[PB[5 + g]], w=["convbuf"])
            c01 = WB[:, 0:1536]; c23 = WB[:, 1536:3072]; cq = WB[:, 3072:4608]
            C01 = SK(0, 3); C23 = SK(3, 3); CQ = SK(6, 3)
            cqv = V(SL(15)) if lp else cq[:, 1024:1536]
            def cview(buf):
                return buf.rearrange("p (c s t) -> p c s t", c=12, s=nseg)
            def cwb(t):
                return cwt[:].rearrange("p (c t) -> p c t", t=4)[:, :, t:t + 1].unsqueeze(3).to_broadcast([128, 12, nseg, L])

            def do_conv(parts=(0, 1)):
                if 0 in parts:
                    c9 = WB[:, 9 * 512:12 * 512]
                    C9 = SK(9, 3)
                    P.op("pool", lambda e: e.tensor_tensor(out=cview(c01), in0=cb4[:, :, :, 0:L], in1=cwb(0), op=ALU.mult), r=["convbuf", "cwt"], w=C01)
                    P.op("pool", lambda e: e.tensor_tensor(out=cview(c9), in0=cb4[:, :, :, 1:1 + L], in1=cwb(1), op=ALU.mult), r=["convbuf", "cwt"], w=C9)
                    P.op("dve", lambda e: e.tensor_tensor(out=cview(c23), in0=cb4[:, :, :, 2:2 + L], in1=cwb(2), op=ALU.mult), r=["convbuf", "cwt"], w=C23)
                    P.op("dve", lambda e: e.tensor_tensor(out=cview(cq), in0=cb4[:, :, :, 3:3 + L], in1=cwb(3), op=ALU.mult), r=["convbuf", "cwt"], w=CQ)
                    P.op("dve", lambda e: e.tensor_tensor(out=c23, in0=c23, in1=cq, op=ALU.add), r=C23 + CQ, w=C23)
                    P.op("dve", lambda e: e.tensor_tensor(out=c23, in0=c23, in1=c9, op=ALU.add), r=C23 + C9, w=C23)
                    P.op("dve", lambda e: e.tensor_tensor(out=c01, in0=c01, in1=c23, op=ALU.add), r=C01 + C23, w=C01)
                if 1 in parts:
                    P.op("act", lambda e: e.activation(out=cq[:, 0:1024], in_=c01[:, 0:1024], func=AF.Silu), r=C01, w=CQ)
                    P.op("act", lambda e: e.activation(out=cqv, in_=c01[:, 1024:1536], func=AF.Silu), r=C01, w=CQ + (["S15"] if lp else []))
                if typ == "P" and 0 in parts:
                    P.op("pool", lambda e: e.tensor_copy(out=cb4[:, :, :, 0:3], in_=cb4[:, :, :, 128:131]), r=["convbuf"], w=["convbuf"])
            if lp:
                do_conv((0,))
            for c in range(8):
                col0 = 1536 + c * 128 if c < 4 else 3584 + (c - 4) * 128
                for k in range(8):
                    P.op("pe", lambda e, c=c, k=k, col0=col0: e.matmul(pb[c // 4][:, hs(c % 4)], lhsT=win(k, col0, col0 + 128), rhs=xnT[:, hs(k)], start=(k == 0), stop=(k == 7)),
                         r=[WIN[k], "xnT"], w=[PB[c // 4]])
                play_pend()
            for hf in range(2):
                P.op("act", lambda e, hf=hf: e.activation(out=gsl[:, hf * 512:(hf + 1) * 512], in_=pb[hf][:], func=AF.Silu), r=[PB[hf]], w=[GSLK])
            for c in range(3):
                for k in range(8):
                    P.op("pe", lambda e, c=c, k=k: e.matmul(pb[TMB[c]][:], lhsT=xnT[:, hs(k)], rhs=win(k, c * 512, (c + 1) * 512), start=(k == 0), stop=(k == 7)),
                         r=[WIN[k], "xnT"], w=[PB[TMB[c]]])
                play_pend()
            P.op("act", lambda e: e.activation(out=vr, in_=pb[TMB[2]][:], func=AF.Copy), r=[PB[TMB[2]]], w=["v_r"])
            for k in range(8):
                P.op("pe", lambda e, k=k: e.matmul(pb[5][:, 0:8], lhsT=xnT[:, hs(k)], rhs=win(k, 4096, 4104), start=(k == 0), stop=(k == 7)),
                     r=[WIN[k], "xnT"], w=[PB[5]])
            g0 = 16 * (T % 2)
            SMG = "smg%d" % (T % 2)
            gcol = smallg[:, g0:g0 + 4]
            bcol = smallg[:, g0 + 4:g0 + 8]
            gtmp = smallg[:, g0 + 8:g0 + 12]
            P.op("dve", lambda e: e.tensor_tensor(out=gtmp, in0=pb[5][:, 0:4], in1=adt[:, 4:8], op=ALU.add), r=[PB[5], "adt"], w=[SMG])
            P.op("act", lambda e: e.activation(out=gtmp, in_=gtmp, func=AF.Exp), r=[SMG], w=[SMG])
            P.op("act", lambda e: e.activation(out=gtmp, in_=gtmp, func=AF.Ln, bias=1.0), r=[SMG], w=[SMG])
            P.op("dve", lambda e: e.tensor_tensor(out=gcol, in0=gtmp, in1=negA[:], op=ALU.mult), r=[SMG, "negA"], w=[SMG])
            P.op("act", lambda e: e.activation(out=bcol, in_=pb[5][:, 4:8], func=AF.Sigmoid), r=[PB[5]], w=[SMG])

            if T >= 15:
                rawtm = WB[:, 9 * 512:12 * 512]
                for c in range(3):
                    for k in range(8):
                        P.op("pe", lambda e, c=c, k=k: e.matmul(pb[5 + c][:], lhsT=xnT[:, hs(k)], rhs=win(k, 2048 + c * 512, 2048 + (c + 1) * 512), start=(k == 0), stop=(k == 7)),
                             r=[WIN[k], "xnT"], w=[PB[5 + c]])
                    P.op("act", lambda e, c=c: e.activation(out=rawtm[:, c * 512:(c + 1) * 512], in_=pb[5 + c][:], func=AF.Copy), r=[PB[5 + c]], w=SK(9, 3))
                if T == 15:
                    P.op("sp", lambda e: e.dma_start(out=convp_d, in_=rawtm[125:128, :]), r=SK(9, 3), dma="convo")
                else:
                    for s in range(16):
                        P.op("sp", lambda e, s=s: e.dma_start(out=convs_d[s * 3:s * 3 + 3, :], in_=rawtm[s * 8 + 5:s * 8 + 8, :]), r=SK(9, 3), dma="convo")

            if T == NT - 1:
                emit_wup()
            if lp:
                qr, kr = qrkr16[:, 0:512], qrkr16[:, 512:1024]
                QRK, KRK = "qrkr", "qrkr"
                tmp12, tmp34 = xn32[:, 0:512], xn32[:, 512:1024]
                T12K, T34K = "xn32", "xn32"
                H = lambda i, j: SL(i).bitcast(BF16)[:, j * 512:(j + 1) * 512]
                ktl, qT, kT, qTs, scm = H(3, 0), H(3, 1), H(4, 0), H(4, 1), H(5, 0)
                KTLK, QTK, KTK, QTSK, SCMK = "S3", "S3", "S4", "S4", "S5"
                tpq = pb[4][:].bitcast(BF16)[:, 0:512]
                tpk = pb[4][:].bitcast(BF16)[:, 512:1024]
                TPQB, TPKB, SCB = 4, 4, 4
            else:
                qr, kr, ktl, qT, kT, qTs, scm = [SL(i) for i in (0, 1, 4, 5, 6, 7, 8)]
                QRK, KRK, KTLK, QTK, KTK, QTSK, SCMK = "S0", "S1", "S4", "S5", "S6", "S7", "S8"
                tmp12, tmp34 = SL(2), SL(3)
                T12K, T34K = "S2", "S3"
                tpq, tpk = pb[0][:], pb[1][:]
                TPQB, TPKB, SCB = 0, 1, 2

            def do_rope():
                for (src, dst, dk, co) in ((TMB[0], qr, QRK, 0), (TMB[1], kr, KRK, 128)):
                    s4 = pb[src][:].rearrange("p (h two f) -> p h two f", h=4, two=2)
                    d4 = dst.rearrange("p (h two f) -> p h two f", h=4, two=2)
                    cosb = ropet[:, co:co + 64].unsqueeze(1).to_broadcast([128, 4, 64])
                    sinb = ropet[:, co + 64:co + 128].unsqueeze(1).to_broadcast([128, 4, 64])
                    t1 = tmp12[:, 0:256].rearrange("p (h f) -> p h f", h=4)
                    t2 = tmp12[:, 256:512].rearrange("p (h f) -> p h f", h=4)
                    t3 = tmp34[:, 0:256].rearrange("p (h f) -> p h f", h=4)
                    t4 = tmp34[:, 256:512].rearrange("p (h f) -> p h f", h=4)
                    P.op("dve", lambda e, s4=s4, t1=t1, cosb=cosb: e.tensor_tensor(out=t1, in0=s4[:, :, 0, :], in1=cosb, op=ALU.mult), r=[PB[src], "ropet"], w=[T12K])
                    P.op("dve", lambda e, s4=s4, t2=t2, sinb=sinb: e.tensor_tensor(out=t2, in0=s4[:, :, 1, :], in1=sinb, op=ALU.mult), r=[PB[src], "ropet"], w=[T12K])
                    P.op("dve", lambda e, s4=s4, t3=t3, sinb=sinb: e.tensor_tensor(out=t3, in0=s4[:, :, 0, :], in1=sinb, op=ALU.mult), r=[PB[src], "ropet"], w=[T34K])
                    P.op("dve", lambda e, s4=s4, t4=t4, cosb=cosb: e.tensor_tensor(out=t4, in0=s4[:, :, 1, :], in1=cosb, op=ALU.mult), r=[PB[src], "ropet"], w=[T34K])
                    P.op("pool", lambda e, d4=d4, t1=t1, t2=t2: e.tensor_tensor(out=d4[:, :, 0, :], in0=t1, in1=t2, op=ALU.subtract), r=[T12K], w=[dk])
                    P.op("pool", lambda e, d4=d4, t3=t3, t4=t4: e.tensor_tensor(out=d4[:, :, 1, :], in0=t3, in1=t4, op=ALU.add), r=[T34K], w=[dk])

            def do_ret():
                P.op("pool", lambda e: e.tensor_tensor(out=ktl.rearrange("p (h d) -> p h d", h=4), in0=kr.rearrange("p (h d) -> p h d", h=4),
                                                       in1=KTC[typ].unsqueeze(2).to_broadcast([128, 4, 128]), op=ALU.mult), r=[KRK, "cmask"], w=[KTLK])
                for h in range(4):
                    P.op("pe", lambda e, h=h: e.transpose(tpq[:, hs(h)], qr[:, hs(h)], identV), r=[QRK, "cmask", "c16"], w=[PB[TPQB]])
                    P.op("pe", lambda e, h=h: e.transpose(tpk[:, hs(h)], kr[:, hs(h)], identV), r=[KRK, "cmask", "c16"], w=[PB[TPKB]])
                P.op("act", lambda e: e.activation(out=qT, in_=tpq, func=AF.Copy), r=[PB[TPQB]], w=[QTK])
                P.op("dve", lambda e: e.tensor_tensor(out=qTs, in0=tpq, in1=QSC[typ], op=ALU.mult), r=[PB[TPQB], "cmask"], w=[QTSK])
                P.op("act", lambda e: e.activation(out=kT, in_=tpk, func=AF.Copy), r=[PB[TPKB]], w=[KTK])
                for h in range(4):
                    P.op("pe", lambda e, h=h: e.matmul(pb[SCB][:, hs(h)], lhsT=kT[:, hs(h)], rhs=qT[:, hs(h)], start=True, stop=True), r=[QTK, KTK], w=[PB[SCB]])
                P.op("dve", lambda e: e.tensor_tensor(out=scm, in0=pb[SCB][:], in1=DTR[typ], op=ALU.mult), r=[PB[SCB], "cmask"], w=[SCMK])
                if typ == "P":
                    for h in range(4):
                        P.op("pe", lambda e, h=h: e.matmul(pb[6][:, hs(h)], lhsT=vr[:, hs(h)], rhs=scm[:, hs(h)], start=True, stop=False), r=["v_r", SCMK], w=[PB[6]])
                        state_mm(typ, "r", h, 6, qTs, [QTSK])
                    for h in range(4):
                        state_update(typ, "r", h, ktl, [KTLK], vr, ["v_r"], GAM[h] ** L, None, srets_d)
                else:
                    load_states("r", 0, 0)
                    for h in range(4):
                        if h + 1 < 4:
                            load_states("r", h + 1, (h + 1) % 2)
                        P.op("pe", lambda e, h=h: e.matmul(pb[6][:, hs(h)], lhsT=vr[:, hs(h)], rhs=scm[:, hs(h)], start=True, stop=False), r=["v_r", SCMK], w=[PB[6]])
                        state_mm(typ, "r", h, 6, qTs, [QTSK], par=h % 2)
                        state_update(typ, "r", h, ktl, [KTLK], vr, ["v_r"], GAM[h] ** L, None, srets_d, par=h % 2, ks=9)

            if lp:
                do_conv((1,))
            do_rope()
            if lp:
                ret_ops = P.record(do_ret)
            else:
                do_ret()
                ret_ops = []

            if not lp:
                do_conv()
            sq = V(c01[:, 0:1024]); rinv = c23[:, 0:1024]
            P.op("act", lambda e: e.activation(out=sq, in_=cq[:, 0:1024], func=AF.Square), r=CQ, w=C01)
            for hf in range(2):
                P.op("pe", lambda e, hf=hf: e.matmul(pb[hf][:], lhsT=onesV, rhs=sq[:, hf * 512:(hf + 1) * 512], start=True, stop=True), r=C01 + ["cmask", "c16"], w=[PB[hf]])
                P.op("act", lambda e, hf=hf: e.activation(out=rinv[:, hf * 512:(hf + 1) * 512], in_=pb[hf][:], func=AF.Ln, bias=EPS), r=[PB[hf]], w=C23)
            P.op("act", lambda e: e.activation(out=rinv[:, 0:512], in_=rinv[:, 0:512], func=AF.Exp, scale=-0.5, bias=float(np.log(128.0 ** -0.5))), r=C23, w=C23)
            P.op("act", lambda e: e.activation(out=rinv[:, 512:1024], in_=rinv[:, 512:1024], func=AF.Exp, scale=-0.5), r=C23, w=C23)
            qkn = V(WB[:, 9 * 512:11 * 512]); QKN = SK(9, 2)
            qTn = qkn[:, 0:512]; kTn = qkn[:, 512:1024]
            P.op("dve", lambda e: e.tensor_tensor(out=qkn, in0=cq[:, 0:1024], in1=rinv, op=ALU.mult), r=CQ + C23, w=QKN)
            gU, dmatT, dmats, ecumB = SL(11), SL(12), SL(13), SL(14)
            P.op("pool", lambda e: e.tensor_tensor(out=gU.rearrange("p (h i) -> p h i", h=4), in0=U[typ].unsqueeze(1).to_broadcast([128, 4, 128]),
                                                   in1=gcol.unsqueeze(2).to_broadcast([128, 4, 128]), op=ALU.mult), r=["cmask", "smallc", SMG], w=["S11"])
            P.op("pe", lambda e: e.matmul(pb[4][:, 0:4], lhsT=U[typ], rhs=gcol, start=True, stop=True), r=["cmask", "smallc", SMG], w=[PB[4]])
            P.op("pe", lambda e: e.matmul(pb[4][:, 4:8], lhsT=MG[typ], rhs=gcol, start=True, stop=True), r=["cmask", "smallc", SMG], w=[PB[4]])
            if typ == "P":
                P.op("pe", lambda e: e.matmul(pb[4][:, 8:12], lhsT=ones, rhs=gcol, start=True, stop=True), r=["cmask", "smallc", SMG], w=[PB[4]])
            else:
                gseg = smallc[:, 100:164]
                P.op("dve", lambda e: e.tensor_tensor(out=gseg.rearrange("p (s h) -> p s h", s=16), in0=gcol.unsqueeze(1).to_broadcast([128, 16, 4]),
                                                     in1=segm.unsqueeze(2).to_broadcast([128, 16, 4]), op=ALU.mult), r=["smallc", "cmask", SMG], w=["smallc"])
                P.op("pe", lambda e: e.matmul(pb[4][:, 8:72], lhsT=ones, rhs=gseg, start=True, stop=True), r=["cmask", "smallc"], w=[PB[4]])
            ncol = 8 + 4 * nseg
            ecols = smallc[:, 28:28 + ncol]
            P.op("act", lambda e: e.activation(out=ecols, in_=pb[4][:, 0:ncol], func=AF.Exp), r=[PB[4]], w=["smallc"])
            edec = smallc[:, 36:36 + 4 * nseg]
            P.op("dve", lambda e: e.tensor_tensor(out=smallc[:, 9:13], in0=bcol, in1=smallc[:, 28:32], op=ALU.mult), r=["smallc", SMG], w=["smallc"])
            P.op("dve", lambda e: e.tensor_scalar(out=smallc[:, 2:6], in0=bcol, scalar1=-1.0, scalar2=None, op0=ALU.mult), r=["smallc", SMG], w=["smallc"])
            becol = smallc[:, 9:13]
            negb = smallc[:, 2:6]
            etail = smallc[:, 32:36]
            for h in range(4):
                P.op("pe", lambda e, h=h: e.matmul(pb[5][:, hs(h)], lhsT=MG[typ], rhs=gU[:, hs(h)], start=True, stop=True), r=["cmask", "S11"], w=[PB[5]])
                P.op("pe", lambda e, h=h: e.matmul(pb[0][:, hs(h)], lhsT=gU[:, hs(h)], rhs=MG[typ], start=True, stop=True), r=["cmask", "S11"], w=[PB[0]])
                P.op("pe", lambda e, h=h: e.matmul(pb[1][:, hs(h)], lhsT=ones, rhs=gU[:, hs(h)], start=True, stop=True), r=["cmask", "S11"], w=[PB[1]])
            P.op("act", lambda e: e.activation(out=dmatT, in_=pb[5][:], func=AF.Exp), r=[PB[5]], w=["S12"])
            P.op("act", lambda e: e.activation(out=dmats, in_=pb[0][:], func=AF.Exp), r=[PB[0]], w=["S13"])
            P.op("act", lambda e: e.activation(out=ecumB, in_=pb[1][:], func=AF.Exp), r=[PB[1]], w=["S14"])
            P.op("pool", lambda e: e.tensor_tensor(out=dmatT.rearrange("p (h i) -> p h i", h=4), in0=dmatT.rearrange("p (h i) -> p h i", h=4),
                                                   in1=CMI[typ].unsqueeze(1).to_broadcast([128, 4, 128]), op=ALU.mult), r=["S12", "cmask"], w=["S12"])
            P.op("pool", lambda e: e.tensor_tensor(out=dmats.rearrange("p (h i) -> p h i", h=4), in0=dmats.rearrange("p (h i) -> p h i", h=4),
                                                   in1=CMS[typ].unsqueeze(1).to_broadcast([128, 4, 128]), op=ALU.mult), r=["S13", "cmask"], w=["S13"])
            for h in range(4):
                P.op("pe", lambda e, h=h: e.transpose(TP(2)[:, hs(h)], kTn[:, hs(h)], identV), r=QKN + ["cmask", "c16"], w=[PB[2]])
                P.op("pe", lambda e, h=h: e.transpose(TP(3)[:, hs(h)], cqv[:, hs(h)], identV), r=CQ + ["cmask", "c16", "S15"], w=[PB[3]])
            kb, ktg, vb = VC(SL(0)), V(SL(1)), VC(SL(2))
            P.op("dve", lambda e: e.tensor_tensor(out=kb.rearrange("p (h d) -> p h d", h=4), in0=TP(2).rearrange("p (h d) -> p h d", h=4),
                                                  in1=becol.unsqueeze(2).to_broadcast([128, 4, 128]), op=ALU.mult), r=[PB[2], "smallc"] + C01, w=["S0"])
            P.op("dve", lambda e: e.tensor_tensor(out=ktg.rearrange("p (h d) -> p h d", h=4), in0=TP(2).rearrange("p (h d) -> p h d", h=4),
                                                  in1=etail.unsqueeze(2).to_broadcast([128, 4, 128]), op=ALU.mult), r=[PB[2], "smallc"], w=["S1"])
            P.op("dve", lambda e: e.tensor_tensor(out=vb.rearrange("p (h d) -> p h d", h=4), in0=TP(3).rearrange("p (h d) -> p h d", h=4),
                                                  in1=bcol.unsqueeze(2).to_broadcast([128, 4, 128]), op=ALU.mult), r=[PB[3], "smallc", SMG], w=["S2"])
            for h in range(4):
                P.op("pe", lambda e, h=h: e.matmul(pb[4][:, hs(h)], lhsT=kTn[:, hs(h)], rhs=kTn[:, hs(h)], start=True, stop=True), r=QKN, w=[PB[4]])
            NM = [(VC(SL(15)), VC(SL(16)), "S15", "S16"), (VC(SL(17)), VC(SL(18)), "S17", "S18")]
            Pc = VC(SL(19))
            Nc, Mc, NK, MK = NM[0]
            for h in range(4):
                P.op("dve", lambda e, h=h, Nc=Nc: e.scalar_tensor_tensor(out=Nc[:, hs(h)], in0=pb[4][:, hs(h)], scalar=negb[:, h:h + 1], in1=dmats[:, hs(h)], op0=ALU.mult, op1=ALU.mult),
                     r=[PB[4], "smallc", "S13"], w=[NK])
            for h in range(4):
                P.op("pe", lambda e, h=h, Nc=Nc: e.transpose(TPC(5)[:, hs(h)], Nc[:, hs(h)], identC), r=[NK, "cmask", "c16"], w=[PB[5]])
            P.op("act", lambda e, Mc=Mc: e.activation(out=Mc, in_=TPC(5), func=AF.Copy), r=[PB[5]], w=[MK])
            P.op("dve", lambda e: e.tensor_tensor(out=Pc.rearrange("p (h i) -> p h i", h=4), in0=TPC(5).rearrange("p (h i) -> p h i", h=4),
                                                  in1=ident.unsqueeze(1).to_broadcast([128, 4, 128]), op=ALU.add), r=[PB[5], "cmask"], w=["S19"])
            nlev = 7 if typ == "P" else 3
            if lp and T >= 1:
                emit_casts(2)
            HALF = [(0, (0, 1), (0, 1, 2)), (1, (2, 3), (3, 5, 7))]
            for lv in range(nlev):
                Nc, Mc, NK, MK = NM[lv % 2]
                Nn, Mn, NKn, MKn = NM[(lv + 1) % 2]
                last = (lv == nlev - 1)
                for (hi, heads, (bM, bN, bP)) in HALF:
                    sf = "ab"[hi]
                    for h in heads:
                        if not last:
                            P.op("pe", lambda e, h=h, Nc=Nc, Mc=Mc, bM=bM: e.matmul(pb[bM][:, hs(h)], lhsT=Nc[:, hs(h)], rhs=Mc[:, hs(h)], start=True, stop=True), r=[NK + sf, MK + sf], w=[PB[bM]])
                            P.op("pe", lambda e, h=h, Nc=Nc, Mc=Mc, bN=bN: e.matmul(pb[bN][:, hs(h)], lhsT=Mc[:, hs(h)], rhs=Nc[:, hs(h)], start=True, stop=True), r=[NK + sf, MK + sf], w=[PB[bN]])
                        if lv > 0:
                            P.op("pe", lambda e, h=h, Nc=Nc, bP=bP: e.matmul(pb[bP][:, hs(h)], lhsT=Nc[:, hs(h)], rhs=Pc[:, hs(h)], start=True, stop=True), r=[NK + sf, "S19" + sf], w=[PB[bP]])
                for (hi, heads, (bM, bN, bP)) in HALF:
                    sf = "ab"[hi]
                    cs = slice(heads[0] * 128, (heads[-1] + 1) * 128)
                    if not last:
                        P.op("act", lambda e, Mn=Mn, bM=bM, cs=cs: e.activation(out=Mn[:, cs], in_=pb[bM][:, cs], func=AF.Copy), r=[PB[bM]], w=[MKn + sf])
                        P.op("act", lambda e, Nn=Nn, bN=bN, cs=cs: e.activation(out=Nn[:, cs], in_=pb[bN][:, cs], func=AF.Copy), r=[PB[bN]], w=[NKn + sf])
                    if lv > 0:
                        P.op("dve", lambda e, bP=bP, cs=cs: e.tensor_tensor(out=Pc[:, cs], in0=Pc[:, cs], in1=pb[bP][:, cs], op=ALU.add), r=[PB[bP], "S19" + sf], w=["S19" + sf])
                if ret_ops:
                    n0 = (len(ret_ops) * lv) // nlev
                    n1 = (len(ret_ops) * (lv + 1)) // nlev
                    P.play(ret_ops[n0:n1])
            wTn, vnT, vnew = V(SL(3)), V(SL(4)), V(SL(5))
            for h in range(4):
                P.op("pe", lambda e, h=h: e.matmul(pb[0][:, hs(h)], lhsT=kb[:, hs(h)], rhs=Pc[:, hs(h)], start=True, stop=True), r=["S0", "S19"], w=[PB[0]])
            P.op("act", lambda e: e.activation(out=wTn, in_=pb[0][:], func=AF.Copy, scale=-1.0), r=[PB[0]] + C23, w=["S3"])
            scg, qTg = V(SL(11)), V(SL(13))
            if typ == "S":
                load_states("g", 0, 0)
            for h in range(4):
                P.op("pe", lambda e, h=h: e.matmul(pb[3][:, hs(h)], lhsT=kTn[:, hs(h)], rhs=qTn[:, hs(h)], start=True, stop=True), r=QKN, w=[PB[3]])
            P.op("dve", lambda e: e.tensor_tensor(out=scg, in0=pb[3][:], in1=dmatT, op=ALU.mult), r=[PB[3], "S12"], w=["S11"])
            P.op("pool", lambda e: e.tensor_tensor(out=qTg, in0=qTn, in1=ecumB, op=ALU.mult), r=QKN + ["S14"], w=["S13"])
            if typ == "P":
                for h in range(4):
                    P.op("pe", lambda e, h=h: e.matmul(pb[1][:, hs(h)], lhsT=vb[:, hs(h)], rhs=Pc[:, hs(h)], start=True, stop=False), r=["S2", "S19"], w=[PB[1]])
                    state_mm(typ, "g", h, 1, wTn, ["S3"])
                P.op("act", lambda e: e.activation(out=vnT, in_=pb[1][:], func=AF.Copy), r=[PB[1]], w=["S4"])
                for h in range(4):
                    P.op("pe", lambda e, h=h: e.transpose(TP(2)[:, hs(h)], vnT[:, hs(h)], identV), r=["S4", "cmask", "c16"], w=[PB[2]])
                P.op("act", lambda e: e.activation(out=vnew, in_=TP(2), func=AF.Copy), r=[PB[2]], w=["S5"])
                for h in range(4):
                    P.op("pe", lambda e, h=h: e.matmul(pb[7][:, hs(h)], lhsT=vnew[:, hs(h)], rhs=scg[:, hs(h)], start=True, stop=False), r=["S5", "S11"], w=[PB[7]])
                    state_mm(typ, "g", h, 7, qTg, ["S13"])
                for h in range(4):
                    state_update(typ, "g", h, ktg, ["S1"], vnew, ["S5"], None, edec, sgdns_d)
            else:
                for h in range(4):
                    par = h % 2
                    if h + 1 < 4:
                        load_states("g", h + 1, (h + 1) % 2)
                    P.op("pe", lambda e, h=h: e.matmul(pb[1][:, hs(h)], lhsT=vb[:, hs(h)], rhs=Pc[:, hs(h)], start=True, stop=False), r=["S2", "S19"], w=[PB[1]])
                    state_mm(typ, "g", h, 1, wTn, ["S3"], par=par)
                    P.op("act", lambda e, h=h: e.activation(out=vnT[:, hs(h)], in_=pb[1][:, hs(h)], func=AF.Copy), r=[PB[1]], w=["S4"])
                    P.op("pe", lambda e, h=h: e.transpose(TP(2)[:, hs(h)], vnT[:, hs(h)], identV), r=["S4", "cmask", "c16"], w=[PB[2]])
                    P.op("act", lambda e, h=h: e.activation(out=vnew[:, hs(h)], in_=TP(2)[:, hs(h)], func=AF.Copy), r=[PB[2]], w=["S5"])
                    P.op("pe", lambda e, h=h: e.matmul(pb[7][:, hs(h)], lhsT=vnew[:, hs(h)], rhs=scg[:, hs(h)], start=True, stop=False), r=["S5", "S11"], w=[PB[7]])
                    state_mm(typ, "g", h, 7, qTg, ["S13"], par=par)
                    state_update(typ, "g", h, ktg, ["S1"], vnew, ["S5"], None, edec, sgdns_d, par=par, ks=6)

            if T in tiles1 and tiles1.index(T) + 1 < len(tiles1):
                load_x(tiles1[tiles1.index(T) + 1])
            oS = WB[:, 13 * 512:15 * 512]; OSK = SK(13, 2)
            sqo = WB[:, 9 * 512:11 * 512]; SQO = SK(9, 2)
            rso = WB[:, 11 * 512:13 * 512]; RSO = SK(11, 2)
            gsC = WB[:, 18 * 512:20 * 512]; GSK = SK(18, 2)
            ptmp = WB[:, 16 * 512:18 * 512]; PTK = SK(16, 2)
            for hf in range(2):
                P.op("act", lambda e, hf=hf: e.activation(out=oS[:, hf * 512:(hf + 1) * 512], in_=pb[6 + hf][:], func=AF.Copy), r=[PB[6 + hf]], w=OSK)

            def do_out():
                for hf in range(2):
                    P.op("act", lambda e, hf=hf: e.activation(out=V(sqo)[:, hf * 512:(hf + 1) * 512], in_=oS[:, hf * 512:(hf + 1) * 512], func=AF.Square), r=OSK, w=SQO)
                    P.op("pe", lambda e, hf=hf: e.matmul(pb[2 + hf][:], lhsT=onesV, rhs=V(sqo)[:, hf * 512:(hf + 1) * 512], start=True, stop=True), r=SQO + ["cmask", "c16"], w=[PB[2 + hf]])
                    P.op("act", lambda e, hf=hf: e.activation(out=rso[:, hf * 512:(hf + 1) * 512], in_=pb[2 + hf][:], func=AF.Ln, scale=1.0 / 128, bias=EPS), r=[PB[2 + hf]], w=RSO)
                P.op("act", lambda e: e.activation(out=rso, in_=rso, func=AF.Exp, scale=-0.5), r=RSO, w=RSO)
                for hf in range(2):
                    P.op("dve", lambda e, hf=hf: e.tensor_tensor(out=sqo[:, hf * 512:(hf + 1) * 512], in0=oS[:, hf * 512:(hf + 1) * 512], in1=rso[:, hf * 512:(hf + 1) * 512], op=ALU.mult), r=OSK + RSO, w=SQO)
                    P.op("dve", lambda e, hf=hf: e.scalar_tensor_tensor(out=mixT[:, hf * 512:(hf + 1) * 512], in0=sqo[:, hf * 512:(hf + 1) * 512], scalar=cols[:, 16 + hf:17 + hf], in1=gsl[:, hf * 512:(hf + 1) * 512], op0=ALU.mult, op1=ALU.mult),
                         r=SQO + ["cols", GSLK], w=["mixT"])
                for hf in range(2):
                    for k in range(8):
                        P.op("pe", lambda e, hf=hf, k=k: e.matmul(pb[2 + hf][:], lhsT=mixT[:, hs(k)], rhs=WA[:, 32832 + k * 1024 + hf * 512:32832 + k * 1024 + (hf + 1) * 512], start=(k == 0), stop=(k == 7)),
                             r=["mixT", WOUT[k]], w=[PB[2 + hf]])
                for hf in range(2):
                    P.op("act", lambda e, hf=hf: e.activation(out=ptmp[:, hf * 512:(hf + 1) * 512], in_=pb[2 + hf][:], func=AF.Square, accum_out=smallo[:, hf:hf + 1]), r=[PB[2 + hf]], w=PTK + ["smallo"])
                P.op("dve", lambda e: e.tensor_tensor(out=smallo[:, 2:3], in0=smallo[:, 0:1], in1=smallo[:, 1:2], op=ALU.add), r=["smallo"], w=["smallo"])
                P.op("act", lambda e: e.activation(out=smallo[:, 3:4], in_=smallo[:, 2:3], func=AF.Ln, scale=1.0 / D, bias=EPS), r=["smallo"], w=["smallo"])
                P.op("act", lambda e: e.activation(out=smallo[:, 4:5], in_=smallo[:, 3:4], func=AF.Exp, scale=-0.5), r=["smallo"], w=["smallo"])
                for hf in range(2):
                    P.op("dve", lambda e, hf=hf: e.scalar_tensor_tensor(out=ptmp[:, hf * 512:(hf + 1) * 512], in0=pb[2 + hf][:], scalar=smallo[:, 4:5], in1=rows[:, hf * 512:(hf + 1) * 512], op0=ALU.mult, op1=ALU.mult),
                         r=[PB[2 + hf], "smallo", "rows"], w=PTK)
                P.op("pool", lambda e: e.tensor_tensor(out=x1t[:], in0=ptmp, in1=xcur[:], op=ALU.add), r=PTK + [XK], w=["x1t"])
                P.op("sp", lambda e, r0=r0: e.dma_start(out=x1_d[r0:r0 + 128, :], in_=x1t[:]), r=["x1t"], w=["x1d%d" % T], dma="x1o")

            nxt = tiles1[tiles1.index(T) + 1] if tiles1.index(T) + 1 < len(tiles1) else None
            if lp and nxt is not None and nxt == T + 1 and nxt <= 14:
                pending[0] = P.record(do_out)
            else:
                do_out()

        for T in tiles1:
            sweep1_tile(T)

        if 'spo' not in skip:
          P.op("sp", lambda e: e.dma_start(out=sretp_d.rearrange("h d e -> d h e"), in_=Sp["r"][:].rearrange("p (h e) -> p h e", h=4)), r=["SpR"], dma="spo")
        if 'spo' not in skip:
          P.op("sp", lambda e: e.dma_start(out=sgdnp_d.rearrange("h d e -> d h e"), in_=Sp["g"][:].rearrange("p (h e) -> p h e", h=4)), r=["SpG"], dma="spo")

        if not wup_done[0]:
            emit_wup()
        xg = [SS, xgB]
        XG = ["xg0", "xg1"]
        cbb = convbuf.bitcast(BF16)
        hT = [cbb[:, 0:2048], cbb[:, 2048:4096]]
        HT = ["hT0", "hT1"]
        h32 = gsil
        post32 = xt
        r32 = [v_r[:, 0:256], v_r[:, 256:512]]
        R32 = ["r32_0", "r32_1"]
        actr = [mixT[:, i * 256:(i + 1) * 256] for i in range(4)]
        ACTR = ["act%d" % i for i in range(4)]
        P.barrier_keys(XG + HT + ["h32", "post32", "sm2", "rows"] + R32 + ACTR)
        P.op("sp", lambda e: e.dma_start(out=rows[:], in_=rows_d[:, 1024:2048]), w=["rows"], dma="rows")
        groups = []
        t2 = sorted(tiles2)
        i = 0
        while i < len(t2):
            if i + 1 < len(t2) and t2[i + 1] == t2[i] + 1 and t2[i] < 15 and t2[i + 1] <= 15:
                groups.append([t2[i], t2[i + 1]]); i += 2
            else:
                groups.append([t2[i]]); i += 1

        def wdn(fc, hf):
            if fc < 8:
                return WA[:, 32768 + fc * 1024 + hf * 512:32768 + fc * 1024 + (hf + 1) * 512]
            return WBb[:, (fc - 8) * 1024 + hf * 512:(fc - 8) * 1024 + (hf + 1) * 512]

        def prep(g):
            p = g % 2
            tl = groups[g]
            nt = len(tl)
            NTOK = nt * 128
            r0 = tl[0] * 128
            P.op("sp", lambda e: e.dma_start(out=xg[p][:, 0:nt * 1024].rearrange("p (t d) -> p t d", t=nt),
                                             in_=x1_d[r0:r0 + NTOK, :].rearrange("(t p) d -> p t d", p=128)),
                 r=["x1d%d" % T for T in tl], w=[XG[p]], dma="xg%d" % p)
            for ti in range(nt):
                xs = xg[p][:, ti * 1024:(ti + 1) * 1024]
                c0 = 8 * ti
                P.op("act", lambda e, xs=xs, c0=c0: e.activation(out=h32[:], in_=xs, func=AF.Square, accum_out=small2[:, c0:c0 + 1]), r=[XG[p]], w=["h32", "sm2"])
                P.op("act", lambda e, c0=c0: e.activation(out=small2[:, c0 + 1:c0 + 2], in_=small2[:, c0:c0 + 1], func=AF.Ln, scale=1.0 / D, bias=EPS), r=["sm2"], w=["sm2"])
                P.op("act", lambda e, c0=c0: e.activation(out=small2[:, c0 + 2:c0 + 3], in_=small2[:, c0 + 1:c0 + 2], func=AF.Exp, scale=-0.5), r=["sm2"], w=["sm2"])
                P.op("dve", lambda e, xs=xs, c0=c0: e.tensor_scalar(out=h32[:], in0=xs, scalar1=small2[:, c0 + 2:c0 + 3], scalar2=None, op0=ALU.mult), r=[XG[p], "sm2"], w=["h32"])
                hv = hT[p][:, 0:8 * NTOK].rearrange("p (k t) -> p k t", k=8)
                for hf in range(2):
                    for k in range(hf * 4, hf * 4 + 4):
                        P.op("pe", lambda e, k=k: e.transpose(pb[7][:, hs(k % 4)], h32[:, hs(k)], ident), r=["h32", "cmask"], w=[PB[7]])
                    P.op("dve", lambda e, hf=hf, hv=hv, ti=ti: e.tensor_tensor(
                        out=hv[:, hf * 4:hf * 4 + 4, ti * 128:(ti + 1) * 128],
                        in0=pb[7][:].rearrange("p (k t) -> p k t", k=4),
                        in1=cols[:, 8 + hf * 4:8 + hf * 4 + 4].unsqueeze(2).to_broadcast([128, 4, 128]), op=ALU.mult),
                        r=[PB[7], "cols"], w=[HT[p]])

        def down(g, fc):
            tl = groups[g]
            for ti in range(len(tl)):
                for hf in range(2):
                    P.op("pe", lambda e, ti=ti, hf=hf: e.matmul(pb[ti * 2 + hf][:], lhsT=actr[fc % 4][:, ti * 128:(ti + 1) * 128], rhs=wdn(fc, hf), start=(fc == 0), stop=(fc == 31)),
                         r=[ACTR[fc % 4], "wdn%d" % fc], w=[PB[ti * 2 + hf]])

        def main(g):
            p = g % 2
            tl = groups[g]
            NTOK = len(tl) * 128
            for fc in range(32):
                bank = 4 + fc % 3
                for k in range(8):
                    P.op("pe", lambda e, fc=fc, k=k, bank=bank: e.matmul(pb[bank][:, 0:NTOK], lhsT=WA[:, k * 4096 + fc * 128:k * 4096 + (fc + 1) * 128], rhs=hT[p][:, k * NTOK:(k + 1) * NTOK], start=(k == 0), stop=(k == 7)),
                         r=[WUP[k], HT[p]], w=[PB[bank]])
                P.op("act", lambda e, fc=fc, bank=bank: e.activation(out=r32[fc % 2][:, 0:NTOK], in_=pb[bank][:, 0:NTOK], func=AF.Relu), r=[PB[bank]], w=[R32[fc % 2]])
                P.op("pool", lambda e, fc=fc: e.tensor_tensor(out=actr[fc % 4][:, 0:NTOK], in0=r32[fc % 2][:, 0:NTOK], in1=r32[fc % 2][:, 0:NTOK], op=ALU.mult), r=[R32[fc % 2]], w=[ACTR[fc % 4]])
                if fc >= 1:
                    down(g, fc - 1)
                if fc == 12 and g + 1 < len(groups):
                    prep(g + 1)
            down(g, 31)

        def post(g):
            p = g % 2
            tl = groups[g]
            for ti, T in enumerate(tl):
                xs = xg[p][:, ti * 1024:(ti + 1) * 1024]
                c0 = 16 + 8 * ti
                for hf in range(2):
                    P.op("act", lambda e, hf=hf, ti=ti, c0=c0: e.activation(out=post32[:, hf * 512:(hf + 1) * 512], in_=pb[ti * 2 + hf][:], func=AF.Square, accum_out=small2[:, c0 + hf:c0 + hf + 1]), r=[PB[ti * 2 + hf]], w=["post32", "sm2"])
                P.op("dve", lambda e, c0=c0: e.tensor_tensor(out=small2[:, c0 + 2:c0 + 3], in0=small2[:, c0:c0 + 1], in1=small2[:, c0 + 1:c0 + 2], op=ALU.add), r=["sm2"], w=["sm2"])
                P.op("act", lambda e, c0=c0: e.activation(out=small2[:, c0 + 3:c0 + 4], in_=small2[:, c0 + 2:c0 + 3], func=AF.Ln, scale=1.0 / D, bias=EPS), r=["sm2"], w=["sm2"])
                P.op("act", lambda e, c0=c0: e.activation(out=small2[:, c0 + 4:c0 + 5], in_=small2[:, c0 + 3:c0 + 4], func=AF.Exp, scale=-0.5), r=["sm2"], w=["sm2"])
                for hf in range(2):
                    P.op("dve", lambda e, hf=hf, ti=ti, c0=c0: e.scalar_tensor_tensor(out=post32[:, hf * 512:(hf + 1) * 512], in0=pb[ti * 2 + hf][:], scalar=small2[:, c0 + 4:c0 + 5], in1=rows[:, hf * 512:(hf + 1) * 512], op0=ALU.mult, op1=ALU.mult),
                         r=[PB[ti * 2 + hf], "sm2", "rows"], w=["post32"])
                P.op("pool", lambda e, xs=xs: e.tensor_tensor(out=xs, in0=post32[:], in1=xs, op=ALU.add), r=["post32", XG[p]], w=[XG[p]])
                P.op("sp", lambda e, xs=xs, T=T: e.dma_start(out=y_d[T * 128:(T + 1) * 128, :], in_=xs), r=[XG[p]], dma="yo")

        if groups:
            prep(0)
        ALLW = WIN + WOUT
        ALLS = SK(0, 20) + XN32
        for fc in range(0 if 'wload2' not in skip else 99, 32):
            if fc < 8:
                dst = WA[:, 32768 + fc * 1024:32768 + (fc + 1) * 1024]
                wk = ALLW
            else:
                dst = WBb[:, (fc - 8) * 1024:(fc - 7) * 1024]
                wk = ALLS
            P.op("sp", lambda e, fc=fc, dst=dst: e.dma_start(out=dst, in_=wdn16_d[fc * 128:(fc + 1) * 128, :]),
                 r=["wdn16_%d" % (fc // 4)], w=wk + ["wdn%d" % fc], dma="wdn")
        WUP = ["wup%d" % k for k in range(8)]
        if "wload2" not in skip:
            for gq in range(4):
                P.regroup(["wdn%d" % fc for fc in range(gq * 8, gq * 8 + 8)], "wdn%d" % gq)
        for g in range(len(groups)):
            main(g)
            post(g)

        allops, P.collect = P.collect, None
        seg = []
        segi = 0
        for o in allops + [("barrier", None, (), (), None)]:
            if o[0] == "barrier":
                global _SCHED_EPS
                _SCHED_EPS = _SEG_EPS[min(segi, len(_SEG_EPS) - 1)]
                segi += 1
                for (eng, fn, r, w, dma) in (_schedule(seg) if sched else seg):
                    P.op(eng, fn, r, w, dma)
                seg = []
                if o[2]:
                    P.barrier_keys(o[2])
            else:
                seg.append(o)
        P.emit(nc, st)
    return nc


def _consts():
    f = np.float32
    out = []
    idx = np.arange(128)
    for typ in "PS":
        L = 128 if typ == "P" else 8
        seg = idx // L
        same = seg[:, None] == seg[None, :]
        U = ((idx[:, None] <= idx[None, :]) & same).astype(f)
        MG = ((idx[:, None] > idx[None, :]) & same).astype(f)
        CMI = ((idx[None, :] >= idx[:, None]) & same).astype(f)
        CMS = ((idx[:, None] > idx[None, :]) & same).astype(f)
        out += [U, MG, CMI, CMS]
    gam = 1.0 - 2.0 ** (-5.0 - np.arange(4, dtype=np.float64))
    ktc = []
    for typ in "PS":
        L = 128 if typ == "P" else 8
        seg = idx // L
        pos = idx % L
        same = seg[:, None] == seg[None, :]
        dt = np.zeros((128, 512), f)
        qs = np.zeros((128, 512), f)
        for h in range(4):
            dmat = np.where((idx[None, :] >= idx[:, None]) & same, gam[h] ** (idx[None, :] - idx[:, None]).clip(0), 0.0)
            dt[:, h * 128:(h + 1) * 128] = dmat
            qs[:, h * 128:(h + 1) * 128] = np.broadcast_to((gam[h] ** (pos + 1))[None, :], (128, 128))
        out += [dt, qs]
        ktc.append(np.stack([gam[h] ** (L - 1 - pos) for h in range(4)], 1).astype(f))
    out.append(np.eye(128, dtype=f))
    out.append(np.ones((128, 128), f))
    out += ktc
    segm = (idx[:, None] // 8 == np.arange(16)[None, :]).astype(f)
    out.append(segm)
    cm = np.concatenate([np.asarray(a, f) for a in out], axis=1)
    inv_freq = (10000.0 ** (-np.arange(0, 128, 2, dtype=np.float32) / np.float32(128))).astype(f)
    pos = np.concatenate([np.arange(2048, dtype=f), (16384 + (np.arange(128) % 8)).astype(f)])
    ang = (pos[:, None] * inv_freq[None, :]).astype(f).astype(np.float64)
    c, s = np.cos(ang), np.sin(ang)
    sc = 128.0 ** -0.5
    rope = np.concatenate([c, s, c * sc, s * sc], 1).astype(f)
    return np.ascontiguousarray(cm), np.ascontiguousarray(rope)


_NC = None


def kernel(x_prompt, x_sample, state_ret, state_gdn, state_conv, pre_mix_w, w_in, conv_w, A_log,
           dt_bias, ret_norm_w, gdn_norm_w, w_out, post_mix_w, pre_mlp_w, w_up, w_down, post_mlp_w):
    global _NC
    f = np.float32
    a = lambda v: np.ascontiguousarray(np.asarray(v, dtype=f))
    x_prompt, x_sample = a(x_prompt), a(x_sample)
    state_ret, state_gdn, state_conv = a(state_ret), a(state_gdn), a(state_conv)
    cm, rope = _consts()
    colsv = np.zeros((128, 32), f)
    colsv[:, 0:8] = a(pre_mix_w)[0].reshape(8, 128).T
    colsv[:, 8:16] = a(pre_mlp_w)[0].reshape(8, 128).T
    colsv[:, 16] = a(ret_norm_w)[0]
    colsv[:, 17] = a(gdn_norm_w)[0]
    rowsv = np.ascontiguousarray(np.broadcast_to(np.concatenate([a(post_mix_w)[0], a(post_mlp_w)[0]])[None, :], (128, 2048)))
    cwv = np.ascontiguousarray(a(conv_w)[0].reshape(4, 12, 128).transpose(2, 1, 0).reshape(128, 48))
    adv = np.ascontiguousarray(np.broadcast_to(np.concatenate([a(A_log)[0], a(dt_bias)[0]])[None, :], (128, 8)))
    shared = {"w_in": a(w_in)[0], "w_out": a(w_out)[0], "w_up": a(w_up)[0], "w_down": a(w_down)[0],
              "cols": colsv, "rows": rowsv, "cw": cwv, "ad": adv, "rope": rope, "cmask": cm}
    in_maps = []
    for b in range(8):
        m = dict(shared)
        m["x"] = np.ascontiguousarray(np.concatenate([x_prompt[b], x_sample[16 * b:16 * b + 16].reshape(128, D)], 0))
        m["sret"] = np.ascontiguousarray(state_ret[0, 16 * b:16 * b + 16])
        m["sgdn"] = np.ascontiguousarray(state_gdn[0, 16 * b:16 * b + 16])
        m["sconv"] = np.ascontiguousarray(state_conv[0, 16 * b:16 * b + 16].reshape(48, 1536))
        in_maps.append(m)
    if _NC is None:
        _NC = build_nc()
    res = run_bass_kernel_spmd(_NC, in_maps, core_ids=list(range(8)))
    R = res.results
    yp = np.stack([R[b]["y"][:2048] for b in range(8)], 0)
    ys = np.concatenate([R[b]["y"][2048:].reshape(16, 8, D) for b in range(8)], 0)
    rp = np.stack([R[b]["sret_p"] for b in range(8)], 0)[None]
    gp = np.stack([R[b]["sgdn_p"] for b in range(8)], 0)[None]
    cp = np.stack([R[b]["conv_p"] for b in range(8)], 0)[None]
    rs = np.concatenate([R[b]["sret_s"] for b in range(8)], 0)[None]
    gs = np.concatenate([R[b]["sgdn_s"] for b in range(8)], 0)[None]
    cs = np.concatenate([R[b]["conv_s"].reshape(16, 3, 1536) for b in range(8)], 0)[None]
    return tuple(np.ascontiguousarray(v.astype(f)) for v in (yp, ys, rp, gp, cp, rs, gs, cs))
```

```python
from contextlib import ExitStack
import numpy as np
import concourse.bass as bass
import concourse.mybir as mybir
from concourse.bass_utils import run_bass_kernel_spmd

F32 = mybir.dt.float32
BF16 = mybir.dt.bfloat16
ALU = mybir.AluOpType
AF = mybir.ActivationFunctionType

D = 1024
NT = 17
INW = 4104
EPS = 1e-6
ENGS = ["pe", "act", "dve", "pool", "sp"]


class Prog:
    def __init__(self):
        self.ops = []
        self.cnt = {e: 0 for e in ENGS}
        self.obs = {e: {} for e in ENGS}
        self.bufs = {}
        self.dsem = {}
        self.pnext = {}
        self.rec = None
        self.collect = None
        self.alias = {"S%d" % i: ("S%da" % i, "S%db" % i) for i in range(15, 20)}

    def _b(self, k):
        if k not in self.bufs:
            self.bufs[k] = {"w": None, "r": []}
        return self.bufs[k]

    def _need(self, eng, tok, waits):
        if tok is None:
            return
        sk, v = tok
        if eng == "pe" and sk == "E:pe":
            return
        if self.obs[eng].get(sk, 0) >= v:
            return
        self.obs[eng][sk] = v
        waits.append((sk, v))

    def _exp(self, keys):
        out = []
        for k in keys:
            out.extend(self.alias.get(k, (k,)))
        return out

    def op(self, eng, fn, r=(), w=(), dma=None):
        if self.rec is not None:
            self.rec.append((eng, fn, tuple(r), tuple(w), dma))
            return
        r = self._exp(r)
        w = self._exp(w)
        if self.collect is not None:
            self.collect.append((eng, fn, tuple(r), tuple(w), dma))
            return
        waits = []
        for k in r:
            self._need(eng, self._b(k)["w"], waits)
            if k.startswith("pb"):
                for t in self._b(k)["r"]:
                    if t[0] != "E:" + eng:
                        self._need(eng, t, waits)
        for k in w:
            b = self._b(k)
            self._need(eng, b["w"], waits)
            for t in b["r"]:
                self._need(eng, t, waits)
        if dma is None:
            self.cnt[eng] += 1
            tok = ("E:" + eng, self.cnt[eng])
            inc = ("E:" + eng, 1)
        else:
            npool = 16 if eng == "sp" else 8
            i = self.pnext.get(eng, 0)
            self.pnext[eng] = (i + 1) % npool
            sk = "D:%s%02d" % (eng, i)
            prev = self.dsem.get(sk, 0)
            if prev > 0:
                self._need(eng, (sk, prev), waits)
            self.dsem[sk] = prev + 16
            tok = (sk, self.dsem[sk])
            inc = (sk, 16)
        for k in r:
            self._b(k)["r"].append(tok)
        for k in w:
            b = self._b(k)
            b["w"] = tok
            b["r"] = []
        self.ops.append((eng, fn, waits, inc))

    def regroup(self, keys, dma):
        return

    def record(self, f):
        self.rec = []
        f()
        ops, self.rec = self.rec, None
        return ops

    def play(self, ops):
        for (eng, fn, r, w, dma) in ops:
            self.op(eng, fn, r, w, dma)

    def barrier_keys(self, keys):
        if self.collect is not None:
            self.collect.append(("barrier", None, tuple(keys), (), None))
            return
        toks = [("E:" + e, self.cnt[e]) for e in ENGS if self.cnt[e] > 0] + list(self.dsem.items())
        for k in self._exp(keys):
            self._b(k)["r"] = list(toks)

    def emit(self, nc, stack):
        sems = {}
        keys = ["E:" + e for e in ENGS] + sorted(self.dsem.keys())
        for i, k in enumerate(keys):
            sems[k] = stack.enter_context(nc.semaphore("s%d" % i))
        final = list(self.dsem.items())
        block = stack.enter_context(nc.Block())

        def run(engname):
            def body(e):
                for (eng, fn, waits, inc) in self.ops:
                    if eng != engname:
                        continue
                    for (sk, v) in waits:
                        e.wait_ge(sems[sk], v)
                    fn(e).then_inc(sems[inc[0]], inc[1])
                if engname == "sp":
                    for (sk, v) in final:
                        e.wait_ge(sems[sk], v)
            return body

        block.tensor(run("pe"))
        block.scalar(run("act"))
        block.vector(run("dve"))
        block.gpsimd(run("pool"))
        block.sync(run("sp"))


class _Mock:
    def __init__(self):
        self.name = None
        self.args = ()
        self.kw = {}

    def __getattr__(self, name):
        def f(*a, **kw):
            self.name, self.args, self.kw = name, a, kw
            return self
        return f


def _free(ap):
    n = 1
    for d in ap.shape[1:]:
        n *= int(d)
    return n


def _op_cost(eng, fn, dma):
    m = _Mock()
    try:
        fn(m)
    except Exception:
        return 0.3, 0.3
    out = m.kw.get("out", m.args[0] if m.args else None)
    n = _free(out) if out is not None else 512
    if dma is not None:
        nbytes = n * int(out.shape[0]) * (2 if out.dtype == BF16 else 4)
        lat = 2.0 + nbytes / 150e3
        if eng == "pool":
            return (5.0 if nbytes > 500000 else 0.3), lat
        return (2.5 if nbytes > 200000 else 0.4), lat
    if eng == "pe":
        if m.name == "transpose":
            src = m.args[1] if len(m.args) > 1 else m.kw.get("in_")
            c = 0.07 + (2.0 if src.dtype == F32 else 1.0) * n / 2400.0
        else:
            lhsT = m.kw.get("lhsT")
            c = 0.06 + (4.0 if lhsT.dtype == F32 else 1.0) * n / 2400.0
        return c, c + 0.25
    if eng == "act":
        c = 0.22 + n / 1000.0
    elif eng == "dve":
        c = 0.15 + n / 960.0
    else:
        c = 0.3 + n / 450.0
    return c, c + 0.1


_SCHED_DEBUG = False
_SCHED_WINDOW = 3000
_SCHED_EPS = 0.0
_SEG_EPS = (0.1, 0.05)


def _schedule(ops, window=None):
    window = window or _SCHED_WINDOW
    n = len(ops)
    deps = [set() for _ in range(n)]
    lastw = {}
    readers = {}
    for i, (eng, fn, r, w, dma) in enumerate(ops):
        for k in r:
            if k in lastw:
                deps[i].add(lastw[k])
        for k in w:
            if k in lastw:
                deps[i].add(lastw[k])
            for j in readers.get(k, ()):
                deps[i].add(j)
        for k in r:
            readers.setdefault(k, []).append(i)
        for k in w:
            lastw[k] = i
            readers[k] = []
        deps[i].discard(i)
    users = [[] for _ in range(n)]
    ndep = [len(d) for d in deps]
    for i, d in enumerate(deps):
        for j in d:
            users[j].append(i)
    cost = [_op_cost(e, f, d) for (e, f, r, w, d) in ops]
    finish = [0.0] * n
    ready = [0.0] * n
    avail = {e: [] for e in ENGS}
    done = [False] * n
    efree = {e: 0.0 for e in ENGS}
    lo = 0
    hi = 0

    def admit(upto):
        nonlocal hi
        while hi < min(upto, n):
            if ndep[hi] == 0:
                avail[ops[hi][0]].append(hi)
            hi += 1
    blev = [0.0] * n
    if _SCHED_EPS > 0:
        for i in range(n - 1, -1, -1):
            m = 0.0
            for u in users[i]:
                if blev[u] > m:
                    m = blev[u]
            blev[i] = m + cost[i][1]
    order = []
    bind = [None] * n
    elast = {e: None for e in ENGS}
    rsrc = [None] * n
    admit(window)
    while len(order) < n:
        best = None
        for e in ENGS:
            be = None
            if _SCHED_EPS > 0 and avail[e]:
                smin = min(max(ready[i], efree[e]) for i in avail[e])
                for i in avail[e]:
                    st_ = max(ready[i], efree[e])
                    if st_ <= smin + _SCHED_EPS:
                        key2 = (-blev[i], i)
                        if be is None or key2 < be[2]:
                            be = ((st_, i), i, key2)
            else:
                for i in avail[e]:
                    st_ = max(ready[i], efree[e])
                    key = (st_, i)
                    if be is None or key < be[0]:
                        be = (key, i)
            if be is not None and (best is None or be[0] < best[0]):
                best = be
        if best is None:
            admit(hi + window)
            continue
        (st_, i) = best[0]
        e = ops[i][0]
        avail[e].remove(i)
        busy, lat = cost[i]
        bind[i] = elast[e] if (efree[e] >= ready[i] and elast[e] is not None) else rsrc[i]
        elast[e] = i
        efree[e] = st_ + busy
        finish[i] = st_ + lat
        done[i] = True
        order.append(i)
        for u in users[i]:
            ndep[u] -= 1
            if finish[i] >= ready[u]:
                rsrc[u] = i
            ready[u] = max(ready[u], finish[i])
            if ndep[u] == 0 and u < hi:
                avail[ops[u][0]].append(u)
        while lo < n and done[lo]:
            lo += 1
        admit(lo + window)
    if _SCHED_DEBUG:
        import collections
        agg = collections.Counter()
        i = max(range(n), key=lambda j: finish[j]) if n else None
        while i is not None:
            j = bind[i]
            t_prev = finish[j] if j is not None else 0.0
            agg[(ops[i][0], ops[i][1].__code__.co_firstlineno)] += finish[i] - t_prev
            i = j
        print('CRIT', [(k, round(v, 1)) for k, v in agg.most_common(40)])
        print('SCHED segment n=%d makespan=%.1f us busy=%s' % (n, max(finish) if n else 0.0, {e: round(sum(cost[i][0] for i in range(n) if ops[i][0] == e), 1) for e in ENGS}))
    return [ops[i] for i in order]


def hs(h):
    return slice(h * 128, (h + 1) * 128)


def build_nc(tiles1=None, tiles2=None, dbg=False, skip=(), stage=99, sched=True):
    tiles1 = list(range(NT)) if tiles1 is None else tiles1
    tiles2 = list(range(NT)) if tiles2 is None else tiles2
    nc = bass.Bass("TRN2", target_bir_lowering=False)

    def din(name, shape, dt=F32):
        return nc.dram_tensor(name, list(shape), dt, kind="ExternalInput").ap()

    def dout(name, shape):
        return nc.dram_tensor(name, list(shape), F32, kind="ExternalOutput").ap()

    x_d = din("x", [NT * 128, D])
    sret_d = din("sret", [16, 4, 128, 128])
    sgdn_d = din("sgdn", [16, 4, 128, 128])
    sconv_d = din("sconv", [48, 1536])
    win_d = din("w_in", [D, INW])
    wout_d = din("w_out", [D, D])
    wup_d = din("w_up", [D, 4096])
    wdn_d = din("w_down", [4096, D])
    cols_d = din("cols", [128, 32])
    rows_d = din("rows", [128, 2048])
    cw_d = din("cw", [128, 48])
    ad_d = din("ad", [128, 8])
    rope_d = din("rope", [NT * 128, 256])
    cm_d = din("cmask", [128, 2 * 4 * 128 + 2 * 2 * 512 + 128 + 128 + 8 + 16])
    y_d = dout("y", [NT * 128, D])
    sretp_d = dout("sret_p", [4, 128, 128])
    sgdnp_d = dout("sgdn_p", [4, 128, 128])
    convp_d = dout("conv_p", [3, 1536])
    srets_d = dout("sret_s", [16, 4, 128, 128])
    sgdns_d = dout("sgdn_s", [16, 4, 128, 128])
    convs_d = dout("conv_s", [48, 1536])
    x1_d = (nc.dram_tensor("x1scr", [NT * 128, D], F32, kind="ExternalOutput").ap() if dbg
            else nc.dram_tensor("x1scr", [NT * 128, D], F32).ap())

    wup16_d = nc.dram_tensor("wup16", [D, 4096], BF16).ap()
    wdn16_d = nc.dram_tensor("wdn16", [4096, D], BF16).ap()

    P = Prog()
    P.collect = []
    st = ExitStack()
    with st:
        def sb(name, shape, dt=F32):
            return st.enter_context(nc.sbuf_tensor("sb_" + name, list(shape), dt))

        WA = sb("WA", [128, 41024], BF16)
        WB = sb("WB", [128, 12288], F32)
        WBb = WB.bitcast(BF16)
        cmask = sb("cmask", [128, cm_d.shape[1]])
        cols = sb("cols", [128, 32])
        rows = sb("rows", [128, 1024])
        gsilB = sb("gsilB", [128, 1024])
        cwt = sb("cwt", [128, 48])
        adt = sb("adt", [128, 8])
        negA = sb("negA", [128, 4])
        Sp = {"r": sb("SpR", [128, 512]), "g": sb("SpG", [128, 512])}
        Sp16 = {"r": sb("Sp16R", [128, 512], BF16), "g": sb("Sp16G", [128, 512], BF16)}
        ident16 = sb("ident16", [128, 128], BF16)
        ones16 = sb("ones16", [128, 128], BF16)
        xt = sb("xt", [128, 1024])
        xtB = sb("xtB", [128, 1024])
        qrkr16 = sb("qrkr16", [128, 1024], BF16)
        gsil = sb("gsil", [128, 1024])
        v_r = sb("v_r", [128, 512])
        convbuf = sb("convbuf", [128, 12 * 176])
        mixT = sb("mixT", [128, 1024], BF16)
        ropet = sb("ropet", [128, 256])
        xnT = sb("xnT", [128, 1024], BF16)
        x1t = sb("x1t", [128, 1024])
        SS = sb("SS", [128, 16 * 128])
        smallc = sb("smallc", [128, 168])
        smallo = sb("smallo", [128, 8])
        smalla = sb("smalla", [128, 16])
        smallg = sb("smallg", [128, 32])
        small2 = sb("small2", [128, 32])
        xgB = sb("xgB", [128, 2048])
        pb = [st.enter_context(nc.psum_tensor("pb%d" % i, [128, 512], F32)) for i in range(8)]
        PB = ["pb%d" % i for i in range(8)]

        o = 0
        U = {}; MG = {}; CMI = {}; CMS = {}; DTR = {}; QSC = {}
        for ti, typ in enumerate("PS"):
            U[typ] = cmask[:, o:o + 128]; o += 128
            MG[typ] = cmask[:, o:o + 128]; o += 128
            CMI[typ] = cmask[:, o:o + 128]; o += 128
            CMS[typ] = cmask[:, o:o + 128]; o += 128
        for typ in "PS":
            DTR[typ] = cmask[:, o:o + 512]; o += 512
            QSC[typ] = cmask[:, o:o + 512]; o += 512
        ident = cmask[:, o:o + 128]; o += 128
        ones = cmask[:, o:o + 128]; o += 128
        KTC = {"P": cmask[:, o:o + 4], "S": cmask[:, o + 4:o + 8]}; o += 8
        segm = cmask[:, o:o + 16]; o += 16

        def SL(i, n=1):
            return WB[:, i * 512:(i + n) * 512]

        def SK(i, n=1):
            return ["S%d" % j for j in range(i, i + n)]

        xn32 = WB[:, 10240:11264]
        XN32 = ["xn32"]

        GAM = [1.0 - 2.0 ** (-5.0 - h) for h in range(4)]

        P.op("sp", lambda e: e.dma_start(out=cmask[:], in_=cm_d), w=["cmask"], dma="cmask")
        P.op("sp", lambda e: e.dma_start(out=cols[:], in_=cols_d), w=["cols"], dma="cols")
        P.op("sp", lambda e: e.dma_start(out=rows[:], in_=rows_d[:, 0:1024]), w=["rows"], dma="rows")
        P.op("sp", lambda e: e.dma_start(out=cwt[:], in_=cw_d), w=["cwt"], dma="cwt")
        P.op("sp", lambda e: e.dma_start(out=adt[:], in_=ad_d), w=["adt"], dma="adt")
        for k in range(0 if 'wload1' not in skip else 99, 8):
            P.op("pool", lambda e, k=k: e.dma_start(out=WA[:, k * INW:(k + 1) * INW], in_=win_d[k * 128:(k + 1) * 128, :]),
                 w=["win%d" % k], dma="win")
        for k in range(0 if 'wload1' not in skip else 99, 8):
            P.op("pool", lambda e, k=k: e.dma_start(out=WA[:, 32832 + k * 1024:32832 + (k + 1) * 1024], in_=wout_d[k * 128:(k + 1) * 128, :]),
                 w=["wout%d" % k], dma="wout")
        WIN = ["win%d" % k for k in range(8)]
        WOUT = ["wout%d" % k for k in range(8)]
        if "wload1" not in skip:
            P.regroup(WIN, "win")
            P.regroup(WOUT, "wout")
        P.op("act", lambda e: e.activation(out=negA[:], in_=adt[:, 0:4], func=AF.Exp), r=["adt"], w=["negA"])
        P.op("dve", lambda e: e.tensor_scalar(out=negA[:], in0=negA[:], scalar1=-1.0, scalar2=None, op0=ALU.mult), r=["negA"], w=["negA"])
        P.op("dve", lambda e: e.memset(Sp["r"][:], 0.0), w=["SpR"])
        P.op("dve", lambda e: e.memset(Sp["g"][:], 0.0), w=["SpG"])
        P.op("dve", lambda e: e.memset(Sp16["r"][:], 0.0), w=["Sp16R"])
        P.op("dve", lambda e: e.memset(Sp16["g"][:], 0.0), w=["Sp16G"])
        P.op("act", lambda e: e.activation(out=ident16[:], in_=ident, func=AF.Copy), r=["cmask"], w=["c16"])
        P.op("act", lambda e: e.activation(out=ones16[:], in_=ones, func=AF.Copy), r=["cmask"], w=["c16"])
        P.op("pool", lambda e: e.memset(convbuf[:], 0.0), w=["convbuf"])

        def win(k, c0, c1):
            return WA[:, k * INW + c0:k * INW + c1]

        def rms_cols(src_ap, srckeys, n_inv, junk_ap, junkkeys, outcol, tmpc):
            P.op("act", lambda e: e.activation(out=junk_ap, in_=src_ap, func=AF.Square, accum_out=smalla[:, tmpc:tmpc + 1]),
                 r=srckeys, w=junkkeys + ["smalla"])
            P.op("act", lambda e: e.activation(out=smalla[:, tmpc + 1:tmpc + 2], in_=smalla[:, tmpc:tmpc + 1], func=AF.Ln, scale=n_inv, bias=EPS),
                 r=["smalla"], w=["smalla"])
            P.op("act", lambda e: e.activation(out=smalla[:, outcol:outcol + 1], in_=smalla[:, tmpc + 1:tmpc + 2], func=AF.Exp, scale=-0.5),
                 r=["smalla"], w=["smalla"])

        def norm_transpose(src, srckeys, wc0, dstT, dstkeys):
            rms_cols(src[:], srckeys, 1.0 / D, xn32, XN32, 8, 0)
            P.op("dve", lambda e: e.tensor_scalar(out=xn32, in0=src[:], scalar1=smalla[:, 8:9], scalar2=None, op0=ALU.mult),
                 r=srckeys + ["smalla"], w=XN32)
            for k in range(8):
                P.op("pe", lambda e, k=k: e.transpose(pb[k // 4][:, hs(k % 4)], xn32[:, hs(k)], ident),
                     r=XN32 + ["cmask"], w=[PB[k // 4]])
            for hf in range(2):
                P.op("dve", lambda e, hf=hf: e.tensor_tensor(
                    out=dstT[:, hf * 512:(hf + 1) * 512].rearrange("p (k t) -> p k t", k=4),
                    in0=pb[hf][:].rearrange("p (k t) -> p k t", k=4),
                    in1=cols[:, wc0 + hf * 4:wc0 + hf * 4 + 4].unsqueeze(2).to_broadcast([128, 4, 128]), op=ALU.mult),
                    r=[PB[hf], "cols"], w=dstkeys)

        SSB = [SS, xgB]

        def load_states(kind, h, par):
            src = sret_d if kind == "r" else sgdn_d
            P.op("sp", lambda e: e.dma_start(out=SSB[par][:].rearrange("p (s e) -> p s e", s=16), in_=src[:, h].rearrange("s d e -> d s e")),
                 w=["SS%d" % par], dma="SS%d" % par)

        def state_mm(typ, kind, h, bank, rhs_tile, rhskeys, par=0):
            SS = SSB[par]
            if typ == "P":
                P.op("pe", lambda e: e.matmul(pb[bank][:, hs(h)], lhsT=Sp16[kind][:, hs(h)], rhs=rhs_tile[:, hs(h)], start=False, stop=True),
                     r=["Sp16" + kind.upper()] + rhskeys, w=[PB[bank]])
            else:
                for s in range(16):
                    c0 = h * 128 + s * 8
                    P.op("pe", lambda e, s=s, c0=c0: e.matmul(pb[bank][:, c0:c0 + 8], lhsT=SS[:, s * 128:(s + 1) * 128], rhs=rhs_tile[:, c0:c0 + 8], start=False, stop=(s == 15)),
                         r=["SS%d" % par] + rhskeys, w=[PB[bank]])

        def state_update(typ, kind, h, kt_tile, ktkeys, v_tile, vkeys, dec_f, dec_ap, out_d, par=0, ks=9):
            SS = SSB[par]
            if typ == "P":
                P.op("pe", lambda e: e.matmul(pb[4][:, hs(h)], lhsT=kt_tile[:, hs(h)], rhs=v_tile[:, hs(h)], start=True, stop=True),
                     r=ktkeys + vkeys, w=[PB[4]])
                key = "Sp" + kind.upper()
                sc = dec_f if dec_ap is None else dec_ap[:, h:h + 1]
                rk = [PB[4], key] + ([] if dec_ap is None else ["smallc"])
                P.op("dve", lambda e: e.scalar_tensor_tensor(out=Sp[kind][:, hs(h)], in0=Sp[kind][:, hs(h)], scalar=sc, in1=pb[4][:, hs(h)], op0=ALU.mult, op1=ALU.add),
                     r=rk, w=[key])
                P.op("pool", lambda e: e.tensor_copy(out=Sp16[kind][:, hs(h)], in_=Sp[kind][:, hs(h)]), r=[key], w=["Sp16" + kind.upper()])
            else:
                kexp = WB[:, ks * 512:(ks + 4) * 512]
                KEXP = SK(ks, 4)
                SoS = WB[:, 15 * 512:19 * 512]
                SOS = SK(15, 4)
                P.op("dve", lambda e: e.tensor_tensor(out=kexp.rearrange("p (s d) -> p s d", s=16),
                                                     in0=kt_tile[:, hs(h)].unsqueeze(1).to_broadcast([128, 16, 128]),
                                                     in1=segm.unsqueeze(2).to_broadcast([128, 16, 128]), op=ALU.mult),
                     r=ktkeys + ["cmask"], w=KEXP)
                for g4 in range(4):
                    bank = 4 + (g4 % 2)
                    for q in range(4):
                        s = g4 * 4 + q
                        P.op("pe", lambda e, s=s, q=q, bank=bank: e.matmul(pb[bank][:, hs(q)], lhsT=kexp[:, s * 128:(s + 1) * 128], rhs=v_tile[:, hs(h)], start=True, stop=True),
                             r=[KEXP[g4]] + vkeys, w=[PB[bank]])
                    if dec_ap is None:
                        P.op("dve", lambda e, g4=g4, bank=bank: e.scalar_tensor_tensor(out=SoS[:, g4 * 512:(g4 + 1) * 512], in0=SS[:, g4 * 512:(g4 + 1) * 512], scalar=dec_f, in1=pb[bank][:], op0=ALU.mult, op1=ALU.add),
                             r=[PB[bank], "SS%d" % par], w=[SOS[g4]])
                    for q in (range(4) if dec_ap is not None else ()):
                        s = g4 * 4 + q
                        sc = dec_f if dec_ap is None else dec_ap[:, s * 4 + h:s * 4 + h + 1]
                        rk = [PB[bank], "SS%d" % par] + ([] if dec_ap is None else ["smallc"])
                        P.op("dve", lambda e, s=s, q=q, sc=sc, bank=bank: e.scalar_tensor_tensor(out=SoS[:, s * 128:(s + 1) * 128], in0=SS[:, s * 128:(s + 1) * 128], scalar=sc, in1=pb[bank][:, hs(q)], op0=ALU.mult, op1=ALU.add),
                             r=rk, w=[SOS[g4]])
                    P.op("sp", lambda e, g4=g4: e.dma_start(out=out_d[g4 * 4:(g4 + 1) * 4, h].rearrange("s d e -> d s e"), in_=SoS[:, g4 * 512:(g4 + 1) * 512].rearrange("p (s e) -> p s e", s=4)),
                         r=[SOS[g4]], dma="sout")

        wup_done = [False]
        pending = [[]]

        cast_next = [0]

        def emit_casts(n):
            if 'wload2' in skip:
                return
            while n > 0 and cast_next[0] < 16:
                i = cast_next[0]
                cast_next[0] += 1
                n -= 1
                if i < 8:
                    P.op("pool", lambda e, i=i: e.dma_start(out=wup16_d[i * 128:(i + 1) * 128, :], in_=wup_d[i * 128:(i + 1) * 128, :]),
                         w=["wup16_%d" % i], dma="cast")
                else:
                    j = i - 8
                    P.op("pool", lambda e, j=j: e.dma_start(out=wdn16_d[j * 512:(j + 1) * 512, :], in_=wdn_d[j * 512:(j + 1) * 512, :]),
                         w=["wdn16_%d" % j], dma="cast")

        def emit_wup():
            wup_done[0] = True
            if 'wload2' in skip:
                return
            emit_casts(99)
            for k in range(8):
                P.op("sp", lambda e, k=k: e.dma_start(out=WA[:, k * 4096:(k + 1) * 4096], in_=wup16_d[k * 128:(k + 1) * 128, :]),
                     r=["wup16_%d" % k], w=WIN + ["wup%d" % k], dma="wup")

        def sweep1_tile(T):
            typ = "P" if T < 16 else "S"
            lp = (typ == "P")
            gsl = [gsil, gsilB][T % 2]
            GSLK = "gsil%d" % (T % 2)

            def V(ap):
                if not lp:
                    return ap
                v = ap.bitcast(BF16)
                return v[:, 0:v.shape[1] // 2]

            def TP(bank):
                return pb[bank][:].bitcast(BF16)[:, 0:512] if lp else pb[bank][:]
            identV = ident16[:] if lp else ident
            onesV = ones16[:] if lp else ones
            vr = V(v_r[:])
            CH32 = True

            def VC(ap):
                return ap if CH32 else V(ap)

            def TPC(bank):
                return pb[bank][:] if CH32 else TP(bank)
            identC = ident if CH32 else identV
            nseg = 1 if typ == "P" else 16
            L = 128 // nseg
            r0 = T * 128
            XT = [xt, xtB]
            xcur = XT[T % 2]
            XK = "xt%d" % (T % 2)

            def load_x(Tn):
                buf = XT[Tn % 2]
                P.op("sp", lambda e: e.dma_start(out=buf[:], in_=x_d[Tn * 128:(Tn + 1) * 128, :]), w=["xt%d" % (Tn % 2)], dma="xt")
            if T == tiles1[0]:
                load_x(T)
            P.op("sp", lambda e, r0=r0: e.dma_start(out=ropet[:], in_=rope_d[r0:r0 + 128, :]), w=["ropet"], dma="ropet")
            norm_transpose(xcur, [XK], 0, xnT, ["xnT"])

            if typ == "P":
                cb4 = convbuf[:, 0:12 * 131].rearrange("p (c s t) -> p c s t", c=12, s=1)
            else:
                cb4 = convbuf[:, 0:12 * 176].rearrange("p (c s t) -> p c s t", c=12, s=16)
                sct = WB[0:48, 15 * 512:18 * 512]
                P.op("sp", lambda e: e.dma_start(out=sct, in_=sconv_d), w=SK(15, 3), dma="sct")
                for c in range(12):
                    bank = 5 + c // 8
                    cc = c % 8
                    P.op("pe", lambda e, c=c, bank=bank, cc=cc: e.transpose(pb[bank][:, cc * 48:(cc + 1) * 48], sct[:, hs(c)], ident[0:48, 0:48]),
                         r=SK(15, 3) + ["cmask"], w=[PB[bank]])
                P.op("act", lambda e: e.activation(out=cb4[:, 0:8, :, 0:3], in_=pb[5][:, 0:384].rearrange("p (c s t) -> p c s t", c=8, s=16), func=AF.Copy),
                     r=[PB[5]], w=["convbuf"])
                P.op("act", lambda e: e.activation(out=cb4[:, 8:12, :, 0:3], in_=pb[6][:, 0:192].rearrange("p (c s t) -> p c s t", c=4, s=16), func=AF.Copy),
                     r=[PB[6]], w=["convbuf"])

            pend, pending[0] = pending[0], []
            TMB = (4, 0, 1) if lp else (2, 3, 4)
            NSLOT = 23
            pslot = [0]

            def play_pend():
                if pend:
                    i = pslot[0]
                    P.play(pend[(len(pend) * i) // NSLOT:(len(pend) * (i + 1)) // NSLOT])
                pslot[0] += 1
            for c in range(12):
                bank = 5 + c // 4
                for k in range(8):
                    P.op("pe", lambda e, c=c, k=k, bank=bank: e.matmul(pb[bank][:, hs(c % 4)], lhsT=win(k, 2048 + c * 128, 2048 + (c + 1) * 128), rhs=xnT[:, hs(k)], start=(k == 0), stop=(k == 7)),
                         r=[WIN[k], "xnT"], w=[PB[bank]])
                play_pend()
            for g in range(3):
                P.op("act", lambda e, g=g: e.activation(out=cb4[:, g * 4:(g + 1) * 4, :, 3:3 + L], in_=pb[5 + g][:].rearrange("p (c s t) -> p c s t", c=4, s=nseg), func=AF.Copy),
                     r=[PB[5 + g]], w=["convbuf"])
            c01 = WB[:, 0:1536]; c23 = WB[:, 1536:3072]; cq = WB[:, 3072:4608]
            C01 = SK(0, 3); C23 = SK(3, 3); CQ = SK(6, 3)
            cqv = V(SL(15)) if lp else cq[:, 1024:1536]
            def cview(buf):
                return buf.rearrange("p (c s t) -> p c s t", c=12, s=nseg)
            def cwb(t):
                return cwt[:].rearrange("p (c t) -> p c t", t=4)[:, :, t:t + 1].unsqueeze(3).to_broadcast([128, 12, nseg, L])

            def do_conv(parts=(0, 1)):
                if 0 in parts:
                    c9 = WB[:, 9 * 512:12 * 512]
                    C9 = SK(9, 3)
                    P.op("pool", lambda e: e.tensor_tensor(out=cview(c01), in0=cb4[:, :, :, 0:L], in1=cwb(0), op=ALU.mult), r=["convbuf", "cwt"], w=C01)
                    P.op("pool", lambda e: e.tensor_tensor(out=cview(c9), in0=cb4[:, :, :, 1:1 + L], in1=cwb(1), op=ALU.mult), r=["convbuf", "cwt"], w=C9)
                    P.op("dve", lambda e: e.tensor_tensor(out=cview(c23), in0=cb4[:, :, :, 2:2 + L], in1=cwb(2), op=ALU.mult), r=["convbuf", "cwt"], w=C23)
                    P.op("dve", lambda e: e.tensor_tensor(out=cview(cq), in0=cb4[:, :, :, 3:3 + L], in1=cwb(3), op=ALU.mult), r=["convbuf", "cwt"], w=CQ)
                    P.op("dve", lambda e: e.tensor_tensor(out=c23, in0=c23, in1=cq, op=ALU.add), r=C23 + CQ, w=C23)
                    P.op("dve", lambda e: e.tensor_tensor(out=c23, in0=c23, in1=c9, op=ALU.add), r=C23 + C9, w=C23)
                    P.op("dve", lambda e: e.tensor_tensor(out=c01, in0=c01, in1=c23, op=ALU.add), r=C01 + C23, w=C01)
                if 1 in parts:
                    P.op("act", lambda e: e.activation(out=cq[:, 0:1024], in_=c01[:, 0:1024], func=AF.Silu), r=C01, w=CQ)
                    P.op("act", lambda e: e.activation(out=cqv, in_=c01[:, 1024:1536], func=AF.Silu), r=C01, w=CQ + (["S15"] if lp else []))
                if typ == "P" and 0 in parts:
                    P.op("pool", lambda e: e.tensor_copy(out=cb4[:, :, :, 0:3], in_=cb4[:, :, :, 128:131]), r=["convbuf"], w=["convbuf"])
            if lp:
                do_conv((0,))
            for c in range(8):
                col0 = 1536 + c * 128 if c < 4 else 3584 + (c - 4) * 128
                for k in range(8):
                    P.op("pe", lambda e, c=c, k=k, col0=col0: e.matmul(pb[c // 4][:, hs(c % 4)], lhsT=win(k, col0, col0 + 128), rhs=xnT[:, hs(k)], start=(k == 0), stop=(k == 7)),
                         r=[WIN[k], "xnT"], w=[PB[c // 4]])
                play_pend()
            for hf in range(2):
                P.op("act", lambda e, hf=hf: e.activation(out=gsl[:, hf * 512:(hf + 1) * 512], in_=pb[hf][:], func=AF.Silu), r=[PB[hf]], w=[GSLK])
            for c in range(3):
                for k in range(8):
                    P.op("pe", lambda e, c=c, k=k: e.matmul(pb[TMB[c]][:], lhsT=xnT[:, hs(k)], rhs=win(k, c * 512, (c + 1) * 512), start=(k == 0), stop=(k == 7)),
                         r=[WIN[k], "xnT"], w=[PB[TMB[c]]])
                play_pend()
            P.op("act", lambda e: e.activation(out=vr, in_=pb[TMB[2]][:], func=AF.Copy), r=[PB[TMB[2]]], w=["v_r"])
            for k in range(8):
                P.op("pe", lambda e, k=k: e.matmul(pb[5][:, 0:8], lhsT=xnT[:, hs(k)], rhs=win(k, 4096, 4104), start=(k == 0), stop=(k == 7)),
                     r=[WIN[k], "xnT"], w=[PB[5]])
            g0 = 16 * (T % 2)
            SMG = "smg%d" % (T % 2)
            gcol = smallg[:, g0:g0 + 4]
            bcol = smallg[:, g0 + 4:g0 + 8]
            gtmp = smallg[:, g0 + 8:g0 + 12]
            P.op("dve", lambda e: e.tensor_tensor(out=gtmp, in0=pb[5][:, 0:4], in1=adt[:, 4:8], op=ALU.add), r=[PB[5], "adt"], w=[SMG])
            P.op("act", lambda e: e.activation(out=gtmp, in_=gtmp, func=AF.Exp), r=[SMG], w=[SMG])
            P.op("act", lambda e: e.activation(out=gtmp, in_=gtmp, func=AF.Ln, bias=1.0), r=[SMG], w=[SMG])
            P.op("dve", lambda e: e.tensor_tensor(out=gcol, in0=gtmp, in1=negA[:], op=ALU.mult), r=[SMG, "negA"], w=[SMG])
            P.op("act", lambda e: e.activation(out=bcol, in_=pb[5][:, 4:8], func=AF.Sigmoid), r=[PB[5]], w=[SMG])

            if T >= 15:
                rawtm = WB[:, 9 * 512:12 * 512]
                for c in range(3):
                    for k in range(8):
                        P.op("pe", lambda e, c=c, k=k: e.matmul(pb[5 + c][:], lhsT=xnT[:, hs(k)], rhs=win(k, 2048 + c * 512, 2048 + (c + 1) * 512), start=(k == 0), stop=(k == 7)),
                             r=[WIN[k], "xnT"], w=[PB[5 + c]])
                    P.op("act", lambda e, c=c: e.activation(out=rawtm[:, c * 512:(c + 1) * 512], in_=pb[5 + c][:], func=AF.Copy), r=[PB[5 + c]], w=SK(9, 3))
                if T == 15:
                    P.op("sp", lambda e: e.dma_start(out=convp_d, in_=rawtm[125:128, :]), r=SK(9, 3), dma="convo")
                else:
                    for s in range(16):
                        P.op("sp", lambda e, s=s: e.dma_start(out=convs_d[s * 3:s * 3 + 3, :], in_=rawtm[s * 8 + 5:s * 8 + 8, :]), r=SK(9, 3), dma="convo")

            if T == NT - 1:
                emit_wup()
            if lp:
                qr, kr = qrkr16[:, 0:512], qrkr16[:, 512:1024]
                QRK, KRK = "qrkr", "qrkr"
                tmp12, tmp34 = xn32[:, 0:512], xn32[:, 512:1024]
                T12K, T34K = "xn32", "xn32"
                H = lambda i, j: SL(i).bitcast(BF16)[:, j * 512:(j + 1) * 512]
                ktl, qT, kT, qTs, scm = H(3, 0), H(3, 1), H(4, 0), H(4, 1), H(5, 0)
                KTLK, QTK, KTK, QTSK, SCMK = "S3", "S3", "S4", "S4", "S5"
                tpq = pb[4][:].bitcast(BF16)[:, 0:512]
                tpk = pb[4][:].bitcast(BF16)[:, 512:1024]
                TPQB, TPKB, SCB = 4, 4, 4
            else:
                qr, kr, ktl, qT, kT, qTs, scm = [SL(i) for i in (0, 1, 4, 5, 6, 7, 8)]
                QRK, KRK, KTLK, QTK, KTK, QTSK, SCMK = "S0", "S1", "S4", "S5", "S6", "S7", "S8"
                tmp12, tmp34 = SL(2), SL(3)
                T12K, T34K = "S2", "S3"
                tpq, tpk = pb[0][:], pb[1][:]
                TPQB, TPKB, SCB = 0, 1, 2

            def do_rope():
                for (src, dst, dk, co) in ((TMB[0], qr, QRK, 0), (TMB[1], kr, KRK, 128)):
                    s4 = pb[src][:].rearrange("p (h two f) -> p h two f", h=4, two=2)
                    d4 = dst.rearrange("p (h two f) -> p h two f", h=4, two=2)
                    cosb = ropet[:, co:co + 64].unsqueeze(1).to_broadcast([128, 4, 64])
                    sinb = ropet[:, co + 64:co + 128].unsqueeze(1).to_broadcast([128, 4, 64])
                    t1 = tmp12[:, 0:256].rearrange("p (h f) -> p h f", h=4)
                    t2 = tmp12[:, 256:512].rearrange("p (h f) -> p h f", h=4)
                    t3 = tmp34[:, 0:256].rearrange("p (h f) -> p h f", h=4)
                    t4 = tmp34[:, 256:512].rearrange("p (h f) -> p h f", h=4)
                    P.op("dve", lambda e, s4=s4, t1=t1, cosb=cosb: e.tensor_tensor(out=t1, in0=s4[:, :, 0, :], in1=cosb, op=ALU.mult), r=[PB[src], "ropet"], w=[T12K])
                    P.op("dve", lambda e, s4=s4, t2=t2, sinb=sinb: e.tensor_tensor(out=t2, in0=s4[:, :, 1, :], in1=sinb, op=ALU.mult), r=[PB[src], "ropet"], w=[T12K])
                    P.op("dve", lambda e, s4=s4, t3=t3, sinb=sinb: e.tensor_tensor(out=t3, in0=s4[:, :, 0, :], in1=sinb, op=ALU.mult), r=[PB[src], "ropet"], w=[T34K])
                    P.op("dve", lambda e, s4=s4, t4=t4, cosb=cosb: e.tensor_tensor(out=t4, in0=s4[:, :, 1, :], in1=cosb, op=ALU.mult), r=[PB[src], "ropet"], w=[T34K])
                    P.op("pool", lambda e, d4=d4, t1=t1, t2=t2: e.tensor_tensor(out=d4[:, :, 0, :], in0=t1, in1=t2, op=ALU.subtract), r=[T12K], w=[dk])
                    P.op("pool", lambda e, d4=d4, t3=t3, t4=t4: e.tensor_tensor(out=d4[:, :, 1, :], in0=t3, in1=t4, op=ALU.add), r=[T34K], w=[dk])

            def do_ret():
                P.op("pool", lambda e: e.tensor_tensor(out=ktl.rearrange("p (h d) -> p h d", h=4), in0=kr.rearrange("p (h d) -> p h d", h=4),
                                                       in1=KTC[typ].unsqueeze(2).to_broadcast([128, 4, 128]), op=ALU.mult), r=[KRK, "cmask"], w=[KTLK])
                for h in range(4):
                    P.op("pe", lambda e, h=h: e.transpose(tpq[:, hs(h)], qr[:, hs(h)], identV), r=[QRK, "cmask", "c16"], w=[PB[TPQB]])
                    P.op("pe", lambda e, h=h: e.transpose(tpk[:, hs(h)], kr[:, hs(h)], identV), r=[KRK, "cmask", "c16"], w=[PB[TPKB]])
                P.op("act", lambda e: e.activation(out=qT, in_=tpq, func=AF.Copy), r=[PB[TPQB]], w=[QTK])
                P.op("dve", lambda e: e.tensor_tensor(out=qTs, in0=tpq, in1=QSC[typ], op=ALU.mult), r=[PB[TPQB], "cmask"], w=[QTSK])
                P.op("act", lambda e: e.activation(out=kT, in_=tpk, func=AF.Copy), r=[PB[TPKB]], w=[KTK])
                for h in range(4):
                    P.op("pe", lambda e, h=h: e.matmul(pb[SCB][:, hs(h)], lhsT=kT[:, hs(h)], rhs=qT[:, hs(h)], start=True, stop=True), r=[QTK, KTK], w=[PB[SCB]])
                P.op("dve", lambda e: e.tensor_tensor(out=scm, in0=pb[SCB][:], in1=DTR[typ], op=ALU.mult), r=[PB[SCB], "cmask"], w=[SCMK])
                if typ == "P":
                    for h in range(4):
                        P.op("pe", lambda e, h=h: e.matmul(pb[6][:, hs(h)], lhsT=vr[:, hs(h)], rhs=scm[:, hs(h)], start=True, stop=False), r=["v_r", SCMK], w=[PB[6]])
                        state_mm(typ, "r", h, 6, qTs, [QTSK])
                    for h in range(4):
                        state_update(typ, "r", h, ktl, [KTLK], vr, ["v_r"], GAM[h] ** L, None, srets_d)
                else:
                    load_states("r", 0, 0)
                    for h in range(4):
                        if h + 1 < 4:
                            load_states("r", h + 1, (h + 1) % 2)
                        P.op("pe", lambda e, h=h: e.matmul(pb[6][:, hs(h)], lhsT=vr[:, hs(h)], rhs=scm[:, hs(h)], start=True, stop=False), r=["v_r", SCMK], w=[PB[6]])
                        state_mm(typ, "r", h, 6, qTs, [QTSK], par=h % 2)
                        state_update(typ, "r", h, ktl, [KTLK], vr, ["v_r"], GAM[h] ** L, None, srets_d, par=h % 2, ks=9)

            if lp:
                do_conv((1,))
            do_rope()
            if lp:
                ret_ops = P.record(do_ret)
            else:
                do_ret()
                ret_ops = []

            if not lp:
                do_conv()
            sq = V(c01[:, 0:1024]); rinv = c23[:, 0:1024]
            P.op("act", lambda e: e.activation(out=sq, in_=cq[:, 0:1024], func=AF.Square), r=CQ, w=C01)
            for hf in range(2):
                P.op("pe", lambda e, hf=hf: e.matmul(pb[hf][:], lhsT=onesV, rhs=sq[:, hf * 512:(hf + 1) * 512], start=True, stop=True), r=C01 + ["cmask", "c16"], w=[PB[hf]])
                P.op("act", lambda e, hf=hf: e.activation(out=rinv[:, hf * 512:(hf + 1) * 512], in_=pb[hf][:], func=AF.Ln, bias=EPS), r=[PB[hf]], w=C23)
            P.op("act", lambda e: e.activation(out=rinv[:, 0:512], in_=rinv[:, 0:512], func=AF.Exp, scale=-0.5, bias=float(np.log(128.0 ** -0.5))), r=C23, w=C23)
            P.op("act", lambda e: e.activation(out=rinv[:, 512:1024], in_=rinv[:, 512:1024], func=AF.Exp, scale=-0.5), r=C23, w=C23)
            qkn = V(WB[:, 9 * 512:11 * 512]); QKN = SK(9, 2)
            qTn = qkn[:, 0:512]; kTn = qkn[:, 512:1024]
            P.op("dve", lambda e: e.tensor_tensor(out=qkn, in0=cq[:, 0:1024], in1=rinv, op=ALU.mult), r=CQ + C23, w=QKN)
            gU, dmatT, dmats, ecumB = SL(11), SL(12), SL(13), SL(14)
            P.op("pool", lambda e: e.tensor_tensor(out=gU.rearrange("p (h i) -> p h i", h=4), in0=U[typ].unsqueeze(1).to_broadcast([128, 4, 128]),
                                                   in1=gcol.unsqueeze(2).to_broadcast([128, 4, 128]), op=ALU.mult), r=["cmask", "smallc", SMG], w=["S11"])
            P.op("pe", lambda e: e.matmul(pb[4][:, 0:4], lhsT=U[typ], rhs=gcol, start=True, stop=True), r=["cmask", "smallc", SMG], w=[PB[4]])
            P.op("pe", lambda e: e.matmul(pb[4][:, 4:8], lhsT=MG[typ], rhs=gcol, start=True, stop=True), r=["cmask", "smallc", SMG], w=[PB[4]])
            if typ == "P":
                P.op("pe", lambda e: e.matmul(pb[4][:, 8:12], lhsT=ones, rhs=gcol, start=True, stop=True), r=["cmask", "smallc", SMG], w=[PB[4]])
            else:
                gseg = smallc[:, 100:164]
                P.op("dve", lambda e: e.tensor_tensor(out=gseg.rearrange("p (s h) -> p s h", s=16), in0=gcol.unsqueeze(1).to_broadcast([128, 16, 4]),
                                                     in1=segm.unsqueeze(2).to_broadcast([128, 16, 4]), op=ALU.mult), r=["smallc", "cmask", SMG], w=["smallc"])
                P.op("pe", lambda e: e.matmul(pb[4][:, 8:72], lhsT=ones, rhs=gseg, start=True, stop=True), r=["cmask", "smallc"], w=[PB[4]])
            ncol = 8 + 4 * nseg
            ecols = smallc[:, 28:28 + ncol]
            P.op("act", lambda e: e.activation(out=ecols, in_=pb[4][:, 0:ncol], func=AF.Exp), r=[PB[4]], w=["smallc"])
            edec = smallc[:, 36:36 + 4 * nseg]
            P.op("dve", lambda e: e.tensor_tensor(out=smallc[:, 9:13], in0=bcol, in1=smallc[:, 28:32], op=ALU.mult), r=["smallc", SMG], w=["smallc"])
            P.op("dve", lambda e: e.tensor_scalar(out=smallc[:, 2:6], in0=bcol, scalar1=-1.0, scalar2=None, op0=ALU.mult), r=["smallc", SMG], w=["smallc"])
            becol = smallc[:, 9:13]
            negb = smallc[:, 2:6]
            etail = smallc[:, 32:36]
            for h in range(4):
                P.op("pe", lambda e, h=h: e.matmul(pb[5][:, hs(h)], lhsT=MG[typ], rhs=gU[:, hs(h)], start=True, stop=True), r=["cmask", "S11"], w=[PB[5]])
                P.op("pe", lambda e, h=h: e.matmul(pb[0][:, hs(h)], lhsT=gU[:, hs(h)], rhs=MG[typ], start=True, stop=True), r=["cmask", "S11"], w=[PB[0]])
                P.op("pe", lambda e, h=h: e.matmul(pb[1][:, hs(h)], lhsT=ones, rhs=gU[:, hs(h)], start=True, stop=True), r=["cmask", "S11"], w=[PB[1]])
            P.op("act", lambda e: e.activation(out=dmatT, in_=pb[5][:], func=AF.Exp), r=[PB[5]], w=["S12"])
            P.op("act", lambda e: e.activation(out=dmats, in_=pb[0][:], func=AF.Exp), r=[PB[0]], w=["S13"])
            P.op("act", lambda e: e.activation(out=ecumB, in_=pb[1][:], func=AF.Exp), r=[PB[1]], w=["S14"])
            P.op("pool", lambda e: e.tensor_tensor(out=dmatT.rearrange("p (h i) -> p h i", h=4), in0=dmatT.rearrange("p (h i) -> p h i", h=4),
                                                   in1=CMI[typ].unsqueeze(1).to_broadcast([128, 4, 128]), op=ALU.mult), r=["S12", "cmask"], w=["S12"])
            P.op("pool", lambda e: e.tensor_tensor(out=dmats.rearrange("p (h i) -> p h i", h=4), in0=dmats.rearrange("p (h i) -> p h i", h=4),
                                                   in1=CMS[typ].unsqueeze(1).to_broadcast([128, 4, 128]), op=ALU.mult), r=["S13", "cmask"], w=["S13"])
            for h in range(4):
                P.op("pe", lambda e, h=h: e.transpose(TP(2)[:, hs(h)], kTn[:, hs(h)], identV), r=QKN + ["cmask", "c16"], w=[PB[2]])
                P.op("pe", lambda e, h=h: e.transpose(TP(3)[:, hs(h)], cqv[:, hs(h)], identV), r=CQ + ["cmask", "c16", "S15"], w=[PB[3]])
            kb, ktg, vb = VC(SL(0)), V(SL(1)), VC(SL(2))
            P.op("dve", lambda e: e.tensor_tensor(out=kb.rearrange("p (h d) -> p h d", h=4), in0=TP(2).rearrange("p (h d) -> p h d", h=4),
                                                  in1=becol.unsqueeze(2).to_broadcast([128, 4, 128]), op=ALU.mult), r=[PB[2], "smallc"] + C01, w=["S0"])
            P.op("dve", lambda e: e.tensor_tensor(out=ktg.rearrange("p (h d) -> p h d", h=4), in0=TP(2).rearrange("p (h d) -> p h d", h=4),
                                                  in1=etail.unsqueeze(2).to_broadcast([128, 4, 128]), op=ALU.mult), r=[PB[2], "smallc"], w=["S1"])
            P.op("dve", lambda e: e.tensor_tensor(out=vb.rearrange("p (h d) -> p h d", h=4), in0=TP(3).rearrange("p (h d) -> p h d", h=4),
                                                  in1=bcol.unsqueeze(2).to_broadcast([128, 4, 128]), op=ALU.mult), r=[PB[3], "smallc", SMG], w=["S2"])
            for h in range(4):
                P.op("pe", lambda e, h=h: e.matmul(pb[4][:, hs(h)], lhsT=kTn[:, hs(h)], rhs=kTn[:, hs(h)], start=True, stop=True), r=QKN, w=[PB[4]])
            NM = [(VC(SL(15)), VC(SL(16)), "S15", "S16"), (VC(SL(17)), VC(SL(18)), "S17", "S18")]
            Pc = VC(SL(19))
            Nc, Mc, NK, MK = NM[0]
            for h in range(4):
                P.op("dve", lambda e, h=h, Nc=Nc: e.scalar_tensor_tensor(out=Nc[:, hs(h)], in0=pb[4][:, hs(h)], scalar=negb[:, h:h + 1], in1=dmats[:, hs(h)], op0=ALU.mult, op1=ALU.mult),
                     r=[PB[4], "smallc", "S13"], w=[NK])
            for h in range(4):
                P.op("pe", lambda e, h=h, Nc=Nc: e.transpose(TPC(5)[:, hs(h)], Nc[:, hs(h)], identC), r=[NK, "cmask", "c16"], w=[PB[5]])
            P.op("act", lambda e, Mc=Mc: e.activation(out=Mc, in_=TPC(5), func=AF.Copy), r=[PB[5]], w=[MK])
            P.op("dve", lambda e: e.tensor_tensor(out=Pc.rearrange("p (h i) -> p h i", h=4), in0=TPC(5).rearrange("p (h i) -> p h i", h=4),
                                                  in1=ident.unsqueeze(1).to_broadcast([128, 4, 128]), op=ALU.add), r=[PB[5], "cmask"], w=["S19"])
            nlev = 7 if typ == "P" else 3
            if lp and T >= 1:
                emit_casts(2)
            HALF = [(0, (0, 1), (0, 1, 2)), (1, (2, 3), (3, 5, 7))]
            for lv in range(nlev):
                Nc, Mc, NK, MK = NM[lv % 2]
                Nn, Mn, NKn, MKn = NM[(lv + 1) % 2]
                last = (lv == nlev - 1)
                for (hi, heads, (bM, bN, bP)) in HALF:
                    sf = "ab"[hi]
                    for h in heads:
                        if not last:
                            P.op("pe", lambda e, h=h, Nc=Nc, Mc=Mc, bM=bM: e.matmul(pb[bM][:, hs(h)], lhsT=Nc[:, hs(h)], rhs=Mc[:, hs(h)], start=True, stop=True), r=[NK + sf, MK + sf], w=[PB[bM]])
                            P.op("pe", lambda e, h=h, Nc=Nc, Mc=Mc, bN=bN: e.matmul(pb[bN][:, hs(h)], lhsT=Mc[:, hs(h)], rhs=Nc[:, hs(h)], start=True, stop=True), r=[NK + sf, MK + sf], w=[PB[bN]])
                        if lv > 0:
                            P.op("pe", lambda e, h=h, Nc=Nc, bP=bP: e.matmul(pb[bP][:, hs(h)], lhsT=Nc[:, hs(h)], rhs=Pc[:, hs(h)], start=True, stop=True), r=[NK + sf, "S19" + sf], w=[PB[bP]])
                for (hi, heads, (bM, bN, bP)) in HALF:
                    sf = "ab"[hi]
                    cs = slice(heads[0] * 128, (heads[-1] + 1) * 128)
                    if not last:
                        P.op("act", lambda e, Mn=Mn, bM=bM, cs=cs: e.activation(out=Mn[:, cs], in_=pb[bM][:, cs], func=AF.Copy), r=[PB[bM]], w=[MKn + sf])
                        P.op("act", lambda e, Nn=Nn, bN=bN, cs=cs: e.activation(out=Nn[:, cs], in_=pb[bN][:, cs], func=AF.Copy), r=[PB[bN]], w=[NKn + sf])
                    if lv > 0:
                        P.op("dve", lambda e, bP=bP, cs=cs: e.tensor_tensor(out=Pc[:, cs], in0=Pc[:, cs], in1=pb[bP][:, cs], op=ALU.add), r=[PB[bP], "S19" + sf], w=["S19" + sf])
                if ret_ops:
                    n0 = (len(ret_ops) * lv) // nlev
                    n1 = (len(ret_ops) * (lv + 1)) // nlev
                    P.play(ret_ops[n0:n1])
            wTn, vnT, vnew = V(SL(3)), V(SL(4)), V(SL(5))
            for h in range(4):
                P.op("pe", lambda e, h=h: e.matmul(pb[0][:, hs(h)], lhsT=kb[:, hs(h)], rhs=Pc[:, hs(h)], start=True, stop=True), r=["S0", "S19"], w=[PB[0]])
            P.op("act", lambda e: e.activation(out=wTn, in_=pb[0][:], func=AF.Copy, scale=-1.0), r=[PB[0]] + C23, w=["S3"])
            scg, qTg = V(SL(11)), V(SL(13))
            if typ == "S":
                load_states("g", 0, 0)
            for h in range(4):
                P.op("pe", lambda e, h=h: e.matmul(pb[3][:, hs(h)], lhsT=kTn[:, hs(h)], rhs=qTn[:, hs(h)], start=True, stop=True), r=QKN, w=[PB[3]])
            P.op("dve", lambda e: e.tensor_tensor(out=scg, in0=pb[3][:], in1=dmatT, op=ALU.mult), r=[PB[3], "S12"], w=["S11"])
            P.op("pool", lambda e: e.tensor_tensor(out=qTg, in0=qTn, in1=ecumB, op=ALU.mult), r=QKN + ["S14"], w=["S13"])
            if typ == "P":
                for h in range(4):
                    P.op("pe", lambda e, h=h: e.matmul(pb[1][:, hs(h)], lhsT=vb[:, hs(h)], rhs=Pc[:, hs(h)], start=True, stop=False), r=["S2", "S19"], w=[PB[1]])
                    state_mm(typ, "g", h, 1, wTn, ["S3"])
                P.op("act", lambda e: e.activation(out=vnT, in_=pb[1][:], func=AF.Copy), r=[PB[1]], w=["S4"])
                for h in range(4):
                    P.op("pe", lambda e, h=h: e.transpose(TP(2)[:, hs(h)], vnT[:, hs(h)], identV), r=["S4", "cmask", "c16"], w=[PB[2]])
                P.op("act", lambda e: e.activation(out=vnew, in_=TP(2), func=AF.Copy), r=[PB[2]], w=["S5"])
                for h in range(4):
                    P.op("pe", lambda e, h=h: e.matmul(pb[7][:, hs(h)], lhsT=vnew[:, hs(h)], rhs=scg[:, hs(h)], start=True, stop=False), r=["S5", "S11"], w=[PB[7]])
                    state_mm(typ, "g", h, 7, qTg, ["S13"])
                for h in range(4):
                    state_update(typ, "g", h, ktg, ["S1"], vnew, ["S5"], None, edec, sgdns_d)
            else:
                for h in range(4):
                    par = h % 2
                    if h + 1 < 4:
                        load_states("g", h + 1, (h + 1) % 2)
                    P.op("pe", lambda e, h=h: e.matmul(pb[1][:, hs(h)], lhsT=vb[:, hs(h)], rhs=Pc[:, hs(h)], start=True, stop=False), r=["S2", "S19"], w=[PB[1]])
                    state_mm(typ, "g", h, 1, wTn, ["S3"], par=par)
                    P.op("act", lambda e, h=h: e.activation(out=vnT[:, hs(h)], in_=pb[1][:, hs(h)], func=AF.Copy), r=[PB[1]], w=["S4"])
                    P.op("pe", lambda e, h=h: e.transpose(TP(2)[:, hs(h)], vnT[:, hs(h)], identV), r=["S4", "cmask", "c16"], w=[PB[2]])
                    P.op("act", lambda e, h=h: e.activation(out=vnew[:, hs(h)], in_=TP(2)[:, hs(h)], func=AF.Copy), r=[PB[2]], w=["S5"])
                    P.op("pe", lambda e, h=h: e.matmul(pb[7][:, hs(h)], lhsT=vnew[:, hs(h)], rhs=scg[:, hs(h)], start=True, stop=False), r=["S5", "S11"], w=[PB[7]])
                    state_mm(typ, "g", h, 7, qTg, ["S13"], par=par)
                    state_update(typ, "g", h, ktg, ["S1"], vnew, ["S5"], None, edec, sgdns_d, par=par, ks=6)

            if T in tiles1 and tiles1.index(T) + 1 < len(tiles1):
                load_x(tiles1[tiles1.index(T) + 1])
            oS = WB[:, 13 * 512:15 * 512]; OSK = SK(13, 2)
            sqo = WB[:, 9 * 512:11 * 512]; SQO = SK(9, 2)
            rso = WB[:, 11 * 512:13 * 512]; RSO = SK(11, 2)
            gsC = WB[:, 18 * 512:20 * 512]; GSK = SK(18, 2)
            ptmp = WB[:, 16 * 512:18 * 512]; PTK = SK(16, 2)
            for hf in range(2):
                P.op("act", lambda e, hf=hf: e.activation(out=oS[:, hf * 512:(hf + 1) * 512], in_=pb[6 + hf][:], func=AF.Copy), r=[PB[6 + hf]], w=OSK)

            def do_out():
                for hf in range(2):
                    P.op("act", lambda e, hf=hf: e.activation(out=V(sqo)[:, hf * 512:(hf + 1) * 512], in_=oS[:, hf * 512:(hf + 1) * 512], func=AF.Square), r=OSK, w=SQO)
                    P.op("pe", lambda e, hf=hf: e.matmul(pb[2 + hf][:], lhsT=onesV, rhs=V(sqo)[:, hf * 512:(hf + 1) * 512], start=True, stop=True), r=SQO + ["cmask", "c16"], w=[PB[2 + hf]])
                    P.op("act", lambda e, hf=hf: e.activation(out=rso[:, hf * 512:(hf + 1) * 512], in_=pb[2 + hf][:], func=AF.Ln, scale=1.0 / 128, bias=EPS), r=[PB[2 + hf]], w=RSO)
                P.op("act", lambda e: e.activation(out=rso, in_=rso, func=AF.Exp, scale=-0.5), r=RSO, w=RSO)
                for hf in range(2):
                    P.op("dve", lambda e, hf=hf: e.tensor_tensor(out=sqo[:, hf * 512:(hf + 1) * 512], in0=oS[:, hf * 512:(hf + 1) * 512], in1=rso[:, hf * 512:(hf + 1) * 512], op=ALU.mult), r=OSK + RSO, w=SQO)
                    P.op("dve", lambda e, hf=hf: e.scalar_tensor_tensor(out=mixT[:, hf * 512:(hf + 1) * 512], in0=sqo[:, hf * 512:(hf + 1) * 512], scalar=cols[:, 16 + hf:17 + hf], in1=gsl[:, hf * 512:(hf + 1) * 512], op0=ALU.mult, op1=ALU.mult),
                         r=SQO + ["cols", GSLK], w=["mixT"])
                for hf in range(2):
                    for k in range(8):
                        P.op("pe", lambda e, hf=hf, k=k: e.matmul(pb[2 + hf][:], lhsT=mixT[:, hs(k)], rhs=WA[:, 32832 + k * 1024 + hf * 512:32832 + k * 1024 + (hf + 1) * 512], start=(k == 0), stop=(k == 7)),
                             r=["mixT", WOUT[k]], w=[PB[2 + hf]])
                for hf in range(2):
                    P.op("act", lambda e, hf=hf: e.activation(out=ptmp[:, hf * 512:(hf + 1) * 512], in_=pb[2 + hf][:], func=AF.Square, accum_out=smallo[:, hf:hf + 1]), r=[PB[2 + hf]], w=PTK + ["smallo"])
                P.op("dve", lambda e: e.tensor_tensor(out=smallo[:, 2:3], in0=smallo[:, 0:1], in1=smallo[:, 1:2], op=ALU.add), r=["smallo"], w=["smallo"])
                P.op("act", lambda e: e.activation(out=smallo[:, 3:4], in_=smallo[:, 2:3], func=AF.Ln, scale=1.0 / D, bias=EPS), r=["smallo"], w=["smallo"])
                P.op("act", lambda e: e.activation(out=smallo[:, 4:5], in_=smallo[:, 3:4], func=AF.Exp, scale=-0.5), r=["smallo"], w=["smallo"])
                for hf in range(2):
                    P.op("dve", lambda e, hf=hf: e.scalar_tensor_tensor(out=ptmp[:, hf * 512:(hf + 1) * 512], in0=pb[2 + hf][:], scalar=smallo[:, 4:5], in1=rows[:, hf * 512:(hf + 1) * 512], op0=ALU.mult, op1=ALU.mult),
                         r=[PB[2 + hf], "smallo", "rows"], w=PTK)
                P.op("pool", lambda e: e.tensor_tensor(out=x1t[:], in0=ptmp, in1=xcur[:], op=ALU.add), r=PTK + [XK], w=["x1t"])
                P.op("sp", lambda e, r0=r0: e.dma_start(out=x1_d[r0:r0 + 128, :], in_=x1t[:]), r=["x1t"], w=["x1d%d" % T], dma="x1o")

            nxt = tiles1[tiles1.index(T) + 1] if tiles1.index(T) + 1 < len(tiles1) else None
            if lp and nxt is not None and nxt == T + 1 and nxt <= 14:
                pending[0] = P.record(do_out)
            else:
                do_out()

        for T in tiles1:
            sweep1_tile(T)

        if 'spo' not in skip:
          P.op("sp", lambda e: e.dma_start(out=sretp_d.rearrange("h d e -> d h e"), in_=Sp["r"][:].rearrange("p (h e) -> p h e", h=4)), r=["SpR"], dma="spo")
        if 'spo' not in skip:
          P.op("sp", lambda e: e.dma_start(out=sgdnp_d.rearrange("h d e -> d h e"), in_=Sp["g"][:].rearrange("p (h e) -> p h e", h=4)), r=["SpG"], dma="spo")

        if not wup_done[0]:
            emit_wup()
        xg = [SS, xgB]
        XG = ["xg0", "xg1"]
        cbb = convbuf.bitcast(BF16)
        hT = [cbb[:, 0:2048], cbb[:, 2048:4096]]
        HT = ["hT0", "hT1"]
        h32 = gsil
        post32 = xt
        r32 = [v_r[:, 0:256], v_r[:, 256:512]]
        R32 = ["r32_0", "r32_1"]
        actr = [mixT[:, i * 256:(i + 1) * 256] for i in range(4)]
        ACTR = ["act%d" % i for i in range(4)]
        P.barrier_keys(XG + HT + ["h32", "post32", "sm2", "rows"] + R32 + ACTR)
        P.op("sp", lambda e: e.dma_start(out=rows[:], in_=rows_d[:, 1024:2048]), w=["rows"], dma="rows")
        groups = []
        t2 = sorted(tiles2)
        i = 0
        while i < len(t2):
            if i + 1 < len(t2) and t2[i + 1] == t2[i] + 1 and t2[i] < 15 and t2[i + 1] <= 15:
                groups.append([t2[i], t2[i + 1]]); i += 2
            else:
                groups.append([t2[i]]); i += 1

        def wdn(fc, hf):
            if fc < 8:
                return WA[:, 32768 + fc * 1024 + hf * 512:32768 + fc * 1024 + (hf + 1) * 512]
            return WBb[:, (fc - 8) * 1024 + hf * 512:(fc - 8) * 1024 + (hf + 1) * 512]

        def prep(g):
            p = g % 2
            tl = groups[g]
            nt = len(tl)
            NTOK = nt * 128
            r0 = tl[0] * 128
            P.op("sp", lambda e: e.dma_start(out=xg[p][:, 0:nt * 1024].rearrange("p (t d) -> p t d", t=nt),
                                             in_=x1_d[r0:r0 + NTOK, :].rearrange("(t p) d -> p t d", p=128)),
                 r=["x1d%d" % T for T in tl], w=[XG[p]], dma="xg%d" % p)
            for ti in range(nt):
                xs = xg[p][:, ti * 1024:(ti + 1) * 1024]
                c0 = 8 * ti
                P.op("act", lambda e, xs=xs, c0=c0: e.activation(out=h32[:], in_=xs, func=AF.Square, accum_out=small2[:, c0:c0 + 1]), r=[XG[p]], w=["h32", "sm2"])
                P.op("act", lambda e, c0=c0: e.activation(out=small2[:, c0 + 1:c0 + 2], in_=small2[:, c0:c0 + 1], func=AF.Ln, scale=1.0 / D, bias=EPS), r=["sm2"], w=["sm2"])
                P.op("act", lambda e, c0=c0: e.activation(out=small2[:, c0 + 2:c0 + 3], in_=small2[:, c0 + 1:c0 + 2], func=AF.Exp, scale=-0.5), r=["sm2"], w=["sm2"])
                P.op("dve", lambda e, xs=xs, c0=c0: e.tensor_scalar(out=h32[:], in0=xs, scalar1=small2[:, c0 + 2:c0 + 3], scalar2=None, op0=ALU.mult), r=[XG[p], "sm2"], w=["h32"])
                hv = hT[p][:, 0:8 * NTOK].rearrange("p (k t) -> p k t", k=8)
                for hf in range(2):
                    for k in range(hf * 4, hf * 4 + 4):
                        P.op("pe", lambda e, k=k: e.transpose(pb[7][:, hs(k % 4)], h32[:, hs(k)], ident), r=["h32", "cmask"], w=[PB[7]])
                    P.op("dve", lambda e, hf=hf, hv=hv, ti=ti: e.tensor_tensor(
                        out=hv[:, hf * 4:hf * 4 + 4, ti * 128:(ti + 1) * 128],
                        in0=pb[7][:].rearrange("p (k t) -> p k t", k=4),
                        in1=cols[:, 8 + hf * 4:8 + hf * 4 + 4].unsqueeze(2).to_broadcast([128, 4, 128]), op=ALU.mult),
                        r=[PB[7], "cols"], w=[HT[p]])

        def down(g, fc):
            tl = groups[g]
            for ti in range(len(tl)):
                for hf in range(2):
                    P.op("pe", lambda e, ti=ti, hf=hf: e.matmul(pb[ti * 2 + hf][:], lhsT=actr[fc % 4][:, ti * 128:(ti + 1) * 128], rhs=wdn(fc, hf), start=(fc == 0), stop=(fc == 31)),
                         r=[ACTR[fc % 4], "wdn%d" % fc], w=[PB[ti * 2 + hf]])

        def main(g):
            p = g % 2
            tl = groups[g]
            NTOK = len(tl) * 128
            for fc in range(32):
                bank = 4 + fc % 3
                for k in range(8):
                    P.op("pe", lambda e, fc=fc, k=k, bank=bank: e.matmul(pb[bank][:, 0:NTOK], lhsT=WA[:, k * 4096 + fc * 128:k * 4096 + (fc + 1) * 128], rhs=hT[p][:, k * NTOK:(k + 1) * NTOK], start=(k == 0), stop=(k == 7)),
                         r=[WUP[k], HT[p]], w=[PB[bank]])
                P.op("act", lambda e, fc=fc, bank=bank: e.activation(out=r32[fc % 2][:, 0:NTOK], in_=pb[bank][:, 0:NTOK], func=AF.Relu), r=[PB[bank]], w=[R32[fc % 2]])
                P.op("pool", lambda e, fc=fc: e.tensor_tensor(out=actr[fc % 4][:, 0:NTOK], in0=r32[fc % 2][:, 0:NTOK], in1=r32[fc % 2][:, 0:NTOK], op=ALU.mult), r=[R32[fc % 2]], w=[ACTR[fc % 4]])
                if fc >= 1:
                    down(g, fc - 1)
                if fc == 12 and g + 1 < len(groups):
                    prep(g + 1)
            down(g, 31)

        def post(g):
            p = g % 2
            tl = groups[g]
            for ti, T in enumerate(tl):
                xs = xg[p][:, ti * 1024:(ti + 1) * 1024]
                c0 = 16 + 8 * ti
                for hf in range(2):
                    P.op("act", lambda e, hf=hf, ti=ti, c0=c0: e.activation(out=post32[:, hf * 512:(hf + 1) * 512], in_=pb[ti * 2 + hf][:], func=AF.Square, accum_out=small2[:, c0 + hf:c0 + hf + 1]), r=[PB[ti * 2 + hf]], w=["post32", "sm2"])
                P.op("dve", lambda e, c0=c0: e.tensor_tensor(out=small2[:, c0 + 2:c0 + 3], in0=small2[:, c0:c0 + 1], in1=small2[:, c0 + 1:c0 + 2], op=ALU.add), r=["sm2"], w=["sm2"])
                P.op("act", lambda e, c0=c0: e.activation(out=small2[:, c0 + 3:c0 + 4], in_=small2[:, c0 + 2:c0 + 3], func=AF.Ln, scale=1.0 / D, bias=EPS), r=["sm2"], w=["sm2"])
                P.op("act", lambda e, c0=c0: e.activation(out=small2[:, c0 + 4:c0 + 5], in_=small2[:, c0 + 3:c0 + 4], func=AF.Exp, scale=-0.5), r=["sm2"], w=["sm2"])
                for hf in range(2):
                    P.op("dve", lambda e, hf=hf, ti=ti, c0=c0: e.scalar_tensor_tensor(out=post32[:, hf * 512:(hf + 1) * 512], in0=pb[ti * 2 + hf][:], scalar=small2[:, c0 + 4:c0 + 5], in1=rows[:, hf * 512:(hf + 1) * 512], op0=ALU.mult, op1=ALU.mult),
                         r=[PB[ti * 2 + hf], "sm2", "rows"], w=["post32"])
                P.op("pool", lambda e, xs=xs: e.tensor_tensor(out=xs, in0=post32[:], in1=xs, op=ALU.add), r=["post32", XG[p]], w=[XG[p]])
                P.op("sp", lambda e, xs=xs, T=T: e.dma_start(out=y_d[T * 128:(T + 1) * 128, :], in_=xs), r=[XG[p]], dma="yo")

        if groups:
            prep(0)
        ALLW = WIN + WOUT
        ALLS = SK(0, 20) + XN32
        for fc in range(0 if 'wload2' not in skip else 99, 32):
            if fc < 8:
                dst = WA[:, 32768 + fc * 1024:32768 + (fc + 1) * 1024]
                wk = ALLW
            else:
                dst = WBb[:, (fc - 8) * 1024:(fc - 7) * 1024]
                wk = ALLS
            P.op("sp", lambda e, fc=fc, dst=dst: e.dma_start(out=dst, in_=wdn16_d[fc * 128:(fc + 1) * 128, :]),
                 r=["wdn16_%d" % (fc // 4)], w=wk + ["wdn%d" % fc], dma="wdn")
        WUP = ["wup%d" % k for k in range(8)]
        if "wload2" not in skip:
            for gq in range(4):
                P.regroup(["wdn%d" % fc for fc in range(gq * 8, gq * 8 + 8)], "wdn%d" % gq)
        for g in range(len(groups)):
            main(g)
            post(g)

        allops, P.collect = P.collect, None
        seg = []
        segi = 0
        for o in allops + [("barrier", None, (), (), None)]:
            if o[0] == "barrier":
                global _SCHED_EPS
                _SCHED_EPS = _SEG_EPS[min(segi, len(_SEG_EPS) - 1)]
                segi += 1
                for (eng, fn, r, w, dma) in (_schedule(seg) if sched else seg):
                    P.op(eng, fn, r, w, dma)
                seg = []
                if o[2]:
                    P.barrier_keys(o[2])
            else:
                seg.append(o)
        P.emit(nc, st)
    return nc


def _consts():
    f = np.float32
    out = []
    idx = np.arange(128)
    for typ in "PS":
        L = 128 if typ == "P" else 8
        seg = idx // L
        same = seg[:, None] == seg[None, :]
        U = ((idx[:, None] <= idx[None, :]) & same).astype(f)
        MG = ((idx[:, None] > idx[None, :]) & same).astype(f)
        CMI = ((idx[None, :] >= idx[:, None]) & same).astype(f)
        CMS = ((idx[:, None] > idx[None, :]) & same).astype(f)
        out += [U, MG, CMI, CMS]
    gam = 1.0 - 2.0 ** (-5.0 - np.arange(4, dtype=np.float64))
    ktc = []
    for typ in "PS":
        L = 128 if typ == "P" else 8
        seg = idx // L
        pos = idx % L
        same = seg[:, None] == seg[None, :]
        dt = np.zeros((128, 512), f)
        qs = np.zeros((128, 512), f)
        for h in range(4):
            dmat = np.where((idx[None, :] >= idx[:, None]) & same, gam[h] ** (idx[None, :] - idx[:, None]).clip(0), 0.0)
            dt[:, h * 128:(h + 1) * 128] = dmat
            qs[:, h * 128:(h + 1) * 128] = np.broadcast_to((gam[h] ** (pos + 1))[None, :], (128, 128))
        out += [dt, qs]
        ktc.append(np.stack([gam[h] ** (L - 1 - pos) for h in range(4)], 1).astype(f))
    out.append(np.eye(128, dtype=f))
    out.append(np.ones((128, 128), f))
    out += ktc
    segm = (idx[:, None] // 8 == np.arange(16)[None, :]).astype(f)
    out.append(segm)
    cm = np.concatenate([np.asarray(a, f) for a in out], axis=1)
    inv_freq = (10000.0 ** (-np.arange(0, 128, 2, dtype=np.float32) / np.float32(128))).astype(f)
    pos = np.concatenate([np.arange(2048, dtype=f), (16384 + (np.arange(128) % 8)).astype(f)])
    ang = (pos[:, None] * inv_freq[None, :]).astype(f).astype(np.float64)
    c, s = np.cos(ang), np.sin(ang)
    sc = 128.0 ** -0.5
    rope = np.concatenate([c, s, c * sc, s * sc], 1).astype(f)
    return np.ascontiguousarray(cm), np.ascontiguousarray(rope)


_NC = None


def kernel(x_prompt, x_sample, state_ret, state_gdn, state_conv, pre_mix_w, w_in, conv_w, A_log,
           dt_bias, ret_norm_w, gdn_norm_w, w_out, post_mix_w, pre_mlp_w, w_up, w_down, post_mlp_w):
    global _NC
    f = np.float32
    a = lambda v: np.ascontiguousarray(np.asarray(v, dtype=f))
    x_prompt, x_sample = a(x_prompt), a(x_sample)
    state_ret, state_gdn, state_conv = a(state_ret), a(state_gdn), a(state_conv)
    cm, rope = _consts()
    colsv = np.zeros((128, 32), f)
    colsv[:, 0:8] = a(pre_mix_w)[0].reshape(8, 128).T
    colsv[:, 8:16] = a(pre_mlp_w)[0].reshape(8, 128).T
    colsv[:, 16] = a(ret_norm_w)[0]
    colsv[:, 17] = a(gdn_norm_w)[0]
    rowsv = np.ascontiguousarray(np.broadcast_to(np.concatenate([a(post_mix_w)[0], a(post_mlp_w)[0]])[None, :], (128, 2048)))
    cwv = np.ascontiguousarray(a(conv_w)[0].reshape(4, 12, 128).transpose(2, 1, 0).reshape(128, 48))
    adv = np.ascontiguousarray(np.broadcast_to(np.concatenate([a(A_log)[0], a(dt_bias)[0]])[None, :], (128, 8)))
    shared = {"w_in": a(w_in)[0], "w_out": a(w_out)[0], "w_up": a(w_up)[0], "w_down": a(w_down)[0],
              "cols": colsv, "rows": rowsv, "cw": cwv, "ad": adv, "rope": rope, "cmask": cm}
    in_maps = []
    for b in range(8):
        m = dict(shared)
        m["x"] = np.ascontiguousarray(np.concatenate([x_prompt[b], x_sample[16 * b:16 * b + 16].reshape(128, D)], 0))
        m["sret"] = np.ascontiguousarray(state_ret[0, 16 * b:16 * b + 16])
        m["sgdn"] = np.ascontiguousarray(state_gdn[0, 16 * b:16 * b + 16])
        m["sconv"] = np.ascontiguousarray(state_conv[0, 16 * b:16 * b + 16].reshape(48, 1536))
        in_maps.append(m)
    if _NC is None:
        _NC = build_nc()
    res = run_bass_kernel_spmd(_NC, in_maps, core_ids=list(range(8)))
    R = res.results
    yp = np.stack([R[b]["y"][:2048] for b in range(8)], 0)
    ys = np.concatenate([R[b]["y"][2048:].reshape(16, 8, D) for b in range(8)], 0)
    rp = np.stack([R[b]["sret_p"] for b in range(8)], 0)[None]
    gp = np.stack([R[b]["sgdn_p"] for b in range(8)], 0)[None]
    cp = np.stack([R[b]["conv_p"] for b in range(8)], 0)[None]
    rs = np.concatenate([R[b]["sret_s"] for b in range(8)], 0)[None]
    gs = np.concatenate([R[b]["sgdn_s"] for b in range(8)], 0)[None]
    cs = np.concatenate([R[b]["conv_s"].reshape(16, 3, 1536) for b in range(8)], 0)[None]
    return tuple(np.ascontiguousarray(v.astype(f)) for v in (yp, ys, rp, gp, cp, rs, gs, cs))
```

```python
from contextlib import ExitStack
import numpy as np
import concourse.bass as bass
import concourse.mybir as mybir
from concourse.bass_utils import run_bass_kernel_spmd

F32 = mybir.dt.float32
BF16 = mybir.dt.bfloat16
ALU = mybir.AluOpType
AF = mybir.ActivationFunctionType

D = 1024
NT = 17
INW = 4104
EPS = 1e-6
ENGS = ["pe", "act", "dve", "pool", "sp"]


class Prog:
    def __init__(self):
        self.ops = []
        self.cnt = {e: 0 for e in ENGS}
        self.obs = {e: {} for e in ENGS}
        self.bufs = {}
        self.dsem = {}
        self.pnext = {}
        self.rec = None
        self.collect = None
        self.alias = {"S%d" % i: ("S%da" % i, "S%db" % i) for i in range(15, 20)}

    def _b(self, k):
        if k not in self.bufs:
            self.bufs[k] = {"w": None, "r": []}
        return self.bufs[k]

    def _need(self, eng, tok, waits):
        if tok is None:
            return
        sk, v = tok
        if eng == "pe" and sk == "E:pe":
            return
        if self.obs[eng].get(sk, 0) >= v:
            return
        self.obs[eng][sk] = v
        waits.append((sk, v))

    def _exp(self, keys):
        out = []
        for k in keys:
            out.extend(self.alias.get(k, (k,)))
        return out

    def op(self, eng, fn, r=(), w=(), dma=None):
        if self.rec is not None:
            self.rec.append((eng, fn, tuple(r), tuple(w), dma))
            return
        r = self._exp(r)
        w = self._exp(w)
        if self.collect is not None:
            self.collect.append((eng, fn, tuple(r), tuple(w), dma))
            return
        waits = []
        for k in r:
            self._need(eng, self._b(k)["w"], waits)
            if k.startswith("pb"):
                for t in self._b(k)["r"]:
                    if t[0] != "E:" + eng:
                        self._need(eng, t, waits)
        for k in w:
            b = self._b(k)
            self._need(eng, b["w"], waits)
            for t in b["r"]:
                self._need(eng, t, waits)
        if dma is None:
            self.cnt[eng] += 1
            tok = ("E:" + eng, self.cnt[eng])
            inc = ("E:" + eng, 1)
        else:
            npool = 16 if eng == "sp" else 8
            i = self.pnext.get(eng, 0)
            self.pnext[eng] = (i + 1) % npool
            sk = "D:%s%02d" % (eng, i)
            prev = self.dsem.get(sk, 0)
            if prev > 0:
                self._need(eng, (sk, prev), waits)
            self.dsem[sk] = prev + 16
            tok = (sk, self.dsem[sk])
            inc = (sk, 16)
        for k in r:
            self._b(k)["r"].append(tok)
        for k in w:
            b = self._b(k)
            b["w"] = tok
            b["r"] = []
        self.ops.append((eng, fn, waits, inc))

    def regroup(self, keys, dma):
        return

    def record(self, f):
        self.rec = []
        f()
        ops, self.rec = self.rec, None
        return ops

    def play(self, ops):
        for (eng, fn, r, w, dma) in ops:
            self.op(eng, fn, r, w, dma)

    def barrier_keys(self, keys):
        if self.collect is not None:
            self.collect.append(("barrier", None, tuple(keys), (), None))
            return
        toks = [("E:" + e, self.cnt[e]) for e in ENGS if self.cnt[e] > 0] + list(self.dsem.items())
        for k in self._exp(keys):
            self._b(k)["r"] = list(toks)

    def emit(self, nc, stack):
        sems = {}
        keys = ["E:" + e for e in ENGS] + sorted(self.dsem.keys())
        for i, k in enumerate(keys):
            sems[k] = stack.enter_context(nc.semaphore("s%d" % i))
        final = list(self.dsem.items())
        block = stack.enter_context(nc.Block())

        def run(engname):
            def body(e):
                for (eng, fn, waits, inc) in self.ops:
                    if eng != engname:
                        continue
                    for (sk, v) in waits:
                        e.wait_ge(sems[sk], v)
                    fn(e).then_inc(sems[inc[0]], inc[1])
                if engname == "sp":
                    for (sk, v) in final:
                        e.wait_ge(sems[sk], v)
            return body

        block.tensor(run("pe"))
        block.scalar(run("act"))
        block.vector(run("dve"))
        block.gpsimd(run("pool"))
        block.sync(run("sp"))


class _Mock:
    def __init__(self):
        self.name = None
        self.args = ()
        self.kw = {}

    def __getattr__(self, name):
        def f(*a, **kw):
            self.name, self.args, self.kw = name, a, kw
            return self
        return f


def _free(ap):
    n = 1
    for d in ap.shape[1:]:
        n *= int(d)
    return n


def _op_cost(eng, fn, dma):
    m = _Mock()
    try:
        fn(m)
    except Exception:
        return 0.3, 0.3
    out = m.kw.get("out", m.args[0] if m.args else None)
    n = _free(out) if out is not None else 512
    if dma is not None:
        nbytes = n * int(out.shape[0]) * (2 if out.dtype == BF16 else 4)
        lat = 2.0 + nbytes / 150e3
        if eng == "pool":
            return (5.0 if nbytes > 500000 else 0.3), lat
        return (1.5 if nbytes > 200000 else 0.3), lat
    if eng == "pe":
        if m.name == "transpose":
            src = m.args[1] if len(m.args) > 1 else m.kw.get("in_")
            c = 0.07 + (2.0 if src.dtype == F32 else 1.0) * n / 2400.0
        else:
            lhsT = m.kw.get("lhsT")
            c = 0.06 + (2.5 if lhsT.dtype == F32 else 1.0) * n / 2400.0
        return c, c + 0.25
    if eng == "act":
        c = 0.22 + n / 1000.0
    elif eng == "dve":
        c = 0.15 + n / 960.0
    else:
        c = 0.3 + n / 450.0
    return c, c + 0.1


_SCHED_DEBUG = False
_SCHED_WINDOW = 3000
_SCHED_EPS = 0.0
_SEG_EPS = (0.1, 0.05)


def _schedule(ops, window=None):
    window = window or _SCHED_WINDOW
    n = len(ops)
    deps = [set() for _ in range(n)]
    lastw = {}
    readers = {}
    for i, (eng, fn, r, w, dma) in enumerate(ops):
        for k in r:
            if k in lastw:
                deps[i].add(lastw[k])
        for k in w:
            if k in lastw:
                deps[i].add(lastw[k])
            for j in readers.get(k, ()):
                deps[i].add(j)
        for k in r:
            readers.setdefault(k, []).append(i)
        for k in w:
            lastw[k] = i
            readers[k] = []
        deps[i].discard(i)
    users = [[] for _ in range(n)]
    ndep = [len(d) for d in deps]
    for i, d in enumerate(deps):
        for j in d:
            users[j].append(i)
    cost = [_op_cost(e, f, d) for (e, f, r, w, d) in ops]
    finish = [0.0] * n
    ready = [0.0] * n
    avail = {e: [] for e in ENGS}
    done = [False] * n
    efree = {e: 0.0 for e in ENGS}
    lo = 0
    hi = 0

    def admit(upto):
        nonlocal hi
        while hi < min(upto, n):
            if ndep[hi] == 0:
                avail[ops[hi][0]].append(hi)
            hi += 1
    blev = [0.0] * n
    if _SCHED_EPS > 0:
        for i in range(n - 1, -1, -1):
            m = 0.0
            for u in users[i]:
                if blev[u] > m:
                    m = blev[u]
            blev[i] = m + cost[i][1]
    order = []
    bind = [None] * n
    elast = {e: None for e in ENGS}
    rsrc = [None] * n
    admit(window)
    while len(order) < n:
        best = None
        for e in ENGS:
            be = None
            if _SCHED_EPS > 0 and avail[e]:
                smin = min(max(ready[i], efree[e]) for i in avail[e])
                for i in avail[e]:
                    st_ = max(ready[i], efree[e])
                    if st_ <= smin + _SCHED_EPS:
                        key2 = (-blev[i], i)
                        if be is None or key2 < be[2]:
                            be = ((st_, i), i, key2)
            else:
                for i in avail[e]:
                    st_ = max(ready[i], efree[e])
                    key = (st_, i)
                    if be is None or key < be[0]:
                        be = (key, i)
            if be is not None and (best is None or be[0] < best[0]):
                best = be
        if best is None:
            admit(hi + window)
            continue
        (st_, i) = best[0]
        e = ops[i][0]
        avail[e].remove(i)
        busy, lat = cost[i]
        bind[i] = elast[e] if (efree[e] >= ready[i] and elast[e] is not None) else rsrc[i]
        elast[e] = i
        efree[e] = st_ + busy
        finish[i] = st_ + lat
        done[i] = True
        order.append(i)
        for u in users[i]:
            ndep[u] -= 1
            if finish[i] >= ready[u]:
                rsrc[u] = i
            ready[u] = max(ready[u], finish[i])
            if ndep[u] == 0 and u < hi:
                avail[ops[u][0]].append(u)
        while lo < n and done[lo]:
            lo += 1
        admit(lo + window)
    if _SCHED_DEBUG:
        import collections
        agg = collections.Counter()
        i = max(range(n), key=lambda j: finish[j]) if n else None
        while i is not None:
            j = bind[i]
            t_prev = finish[j] if j is not None else 0.0
            agg[(ops[i][0], ops[i][1].__code__.co_firstlineno)] += finish[i] - t_prev
            i = j
        print('CRIT', [(k, round(v, 1)) for k, v in agg.most_common(40)])
        print('SCHED segment n=%d makespan=%.1f us busy=%s' % (n, max(finish) if n else 0.0, {e: round(sum(cost[i][0] for i in range(n) if ops[i][0] == e), 1) for e in ENGS}))
    return [ops[i] for i in order]


def hs(h):
    return slice(h * 128, (h + 1) * 128)


def build_nc(tiles1=None, tiles2=None, dbg=False, skip=(), stage=99, sched=True):
    tiles1 = list(range(NT)) if tiles1 is None else tiles1
    tiles2 = list(range(NT)) if tiles2 is None else tiles2
    nc = bass.Bass("TRN2", target_bir_lowering=False)

    def din(name, shape, dt=F32):
        return nc.dram_tensor(name, list(shape), dt, kind="ExternalInput").ap()

    def dout(name, shape):
        return nc.dram_tensor(name, list(shape), F32, kind="ExternalOutput").ap()

    x_d = din("x", [NT * 128, D])
    sret_d = din("sret", [16, 4, 128, 128])
    sgdn_d = din("sgdn", [16, 4, 128, 128])
    sconv_d = din("sconv", [48, 1536])
    win_d = din("w_in", [D, INW])
    wout_d = din("w_out", [D, D])
    wup_d = din("w_up", [D, 4096])
    wdn_d = din("w_down", [4096, D])
    cols_d = din("cols", [128, 32])
    rows_d = din("rows", [128, 2048])
    cw_d = din("cw", [128, 48])
    ad_d = din("ad", [128, 8])
    rope_d = din("rope", [NT * 128, 256])
    cm_d = din("cmask", [128, 2 * 4 * 128 + 2 * 2 * 512 + 128 + 128 + 8 + 16])
    y_d = dout("y", [NT * 128, D])
    sretp_d = dout("sret_p", [4, 128, 128])
    sgdnp_d = dout("sgdn_p", [4, 128, 128])
    convp_d = dout("conv_p", [3, 1536])
    srets_d = dout("sret_s", [16, 4, 128, 128])
    sgdns_d = dout("sgdn_s", [16, 4, 128, 128])
    convs_d = dout("conv_s", [48, 1536])
    x1_d = (nc.dram_tensor("x1scr", [NT * 128, D], F32, kind="ExternalOutput").ap() if dbg
            else nc.dram_tensor("x1scr", [NT * 128, D], F32).ap())

    wup16_d = nc.dram_tensor("wup16", [D, 4096], BF16).ap()
    wdn16_d = nc.dram_tensor("wdn16", [4096, D], BF16).ap()

    P = Prog()
    P.collect = []
    st = ExitStack()
    with st:
        def sb(name, shape, dt=F32):
            return st.enter_context(nc.sbuf_tensor("sb_" + name, list(shape), dt))

        WA = sb("WA", [128, 41024], BF16)
        WB = sb("WB", [128, 12288], F32)
        WBb = WB.bitcast(BF16)
        cmask = sb("cmask", [128, cm_d.shape[1]])
        cols = sb("cols", [128, 32])
        rows = sb("rows", [128, 1024])
        gsilB = sb("gsilB", [128, 1024])
        cwt = sb("cwt", [128, 48])
        adt = sb("adt", [128, 8])
        negA = sb("negA", [128, 4])
        Sp = {"r": sb("SpR", [128, 512]), "g": sb("SpG", [128, 512])}
        Sp16 = {"r": sb("Sp16R", [128, 512], BF16), "g": sb("Sp16G", [128, 512], BF16)}
        ident16 = sb("ident16", [128, 128], BF16)
        ones16 = sb("ones16", [128, 128], BF16)
        xt = sb("xt", [128, 1024])
        xtB = sb("xtB", [128, 1024])
        qrkr16 = sb("qrkr16", [128, 1024], BF16)
        gsil = sb("gsil", [128, 1024])
        v_r = sb("v_r", [128, 512])
        convbuf = sb("convbuf", [128, 12 * 176])
        mixT = sb("mixT", [128, 1024], BF16)
        ropet = sb("ropet", [128, 256])
        xnT = sb("xnT", [128, 1024], BF16)
        x1t = sb("x1t", [128, 1024])
        SS = sb("SS", [128, 16 * 128])
        smallc = sb("smallc", [128, 168])
        smallo = sb("smallo", [128, 8])
        smalla = sb("smalla", [128, 16])
        smallg = sb("smallg", [128, 32])
        small2 = sb("small2", [128, 32])
        xgB = sb("xgB", [128, 2048])
        pb = [st.enter_context(nc.psum_tensor("pb%d" % i, [128, 512], F32)) for i in range(8)]
        PB = ["pb%d" % i for i in range(8)]

        o = 0
        U = {}; MG = {}; CMI = {}; CMS = {}; DTR = {}; QSC = {}
        for ti, typ in enumerate("PS"):
            U[typ] = cmask[:, o:o + 128]; o += 128
            MG[typ] = cmask[:, o:o + 128]; o += 128
            CMI[typ] = cmask[:, o:o + 128]; o += 128
            CMS[typ] = cmask[:, o:o + 128]; o += 128
        for typ in "PS":
            DTR[typ] = cmask[:, o:o + 512]; o += 512
            QSC[typ] = cmask[:, o:o + 512]; o += 512
        ident = cmask[:, o:o + 128]; o += 128
        ones = cmask[:, o:o + 128]; o += 128
        KTC = {"P": cmask[:, o:o + 4], "S": cmask[:, o + 4:o + 8]}; o += 8
        segm = cmask[:, o:o + 16]; o += 16

        def SL(i, n=1):
            return WB[:, i * 512:(i + n) * 512]

        def SK(i, n=1):
            return ["S%d" % j for j in range(i, i + n)]

        xn32 = WB[:, 10240:11264]
        XN32 = ["xn32"]

        GAM = [1.0 - 2.0 ** (-5.0 - h) for h in range(4)]

        P.op("sp", lambda e: e.dma_start(out=cmask[:], in_=cm_d), w=["cmask"], dma="cmask")
        P.op("sp", lambda e: e.dma_start(out=cols[:], in_=cols_d), w=["cols"], dma="cols")
        P.op("sp", lambda e: e.dma_start(out=rows[:], in_=rows_d[:, 0:1024]), w=["rows"], dma="rows")
        P.op("sp", lambda e: e.dma_start(out=cwt[:], in_=cw_d), w=["cwt"], dma="cwt")
        P.op("sp", lambda e: e.dma_start(out=adt[:], in_=ad_d), w=["adt"], dma="adt")
        for k in range(0 if 'wload1' not in skip else 99, 8):
            P.op("pool", lambda e, k=k: e.dma_start(out=WA[:, k * INW:(k + 1) * INW], in_=win_d[k * 128:(k + 1) * 128, :]),
                 w=["win%d" % k], dma="win")
        for k in range(0 if 'wload1' not in skip else 99, 8):
            P.op("pool", lambda e, k=k: e.dma_start(out=WA[:, 32832 + k * 1024:32832 + (k + 1) * 1024], in_=wout_d[k * 128:(k + 1) * 128, :]),
                 w=["wout%d" % k], dma="wout")
        WIN = ["win%d" % k for k in range(8)]
        WOUT = ["wout%d" % k for k in range(8)]
        if "wload1" not in skip:
            P.regroup(WIN, "win")
            P.regroup(WOUT, "wout")
        P.op("act", lambda e: e.activation(out=negA[:], in_=adt[:, 0:4], func=AF.Exp), r=["adt"], w=["negA"])
        P.op("dve", lambda e: e.tensor_scalar(out=negA[:], in0=negA[:], scalar1=-1.0, scalar2=None, op0=ALU.mult), r=["negA"], w=["negA"])
        P.op("dve", lambda e: e.memset(Sp["r"][:], 0.0), w=["SpR"])
        P.op("dve", lambda e: e.memset(Sp["g"][:], 0.0), w=["SpG"])
        P.op("dve", lambda e: e.memset(Sp16["r"][:], 0.0), w=["Sp16R"])
        P.op("dve", lambda e: e.memset(Sp16["g"][:], 0.0), w=["Sp16G"])
        P.op("act", lambda e: e.activation(out=ident16[:], in_=ident, func=AF.Copy), r=["cmask"], w=["c16"])
        P.op("act", lambda e: e.activation(out=ones16[:], in_=ones, func=AF.Copy), r=["cmask"], w=["c16"])
        P.op("pool", lambda e: e.memset(convbuf[:], 0.0), w=["convbuf"])

        def win(k, c0, c1):
            return WA[:, k * INW + c0:k * INW + c1]

        def rms_cols(src_ap, srckeys, n_inv, junk_ap, junkkeys, outcol, tmpc):
            P.op("act", lambda e: e.activation(out=junk_ap, in_=src_ap, func=AF.Square, accum_out=smalla[:, tmpc:tmpc + 1]),
                 r=srckeys, w=junkkeys + ["smalla"])
            P.op("act", lambda e: e.activation(out=smalla[:, tmpc + 1:tmpc + 2], in_=smalla[:, tmpc:tmpc + 1], func=AF.Ln, scale=n_inv, bias=EPS),
                 r=["smalla"], w=["smalla"])
            P.op("act", lambda e: e.activation(out=smalla[:, outcol:outcol + 1], in_=smalla[:, tmpc + 1:tmpc + 2], func=AF.Exp, scale=-0.5),
                 r=["smalla"], w=["smalla"])

        def norm_transpose(src, srckeys, wc0, dstT, dstkeys):
            rms_cols(src[:], srckeys, 1.0 / D, xn32, XN32, 8, 0)
            P.op("dve", lambda e: e.tensor_scalar(out=xn32, in0=src[:], scalar1=smalla[:, 8:9], scalar2=None, op0=ALU.mult),
                 r=srckeys + ["smalla"], w=XN32)
            for k in range(8):
                P.op("pe", lambda e, k=k: e.transpose(pb[k // 4][:, hs(k % 4)], xn32[:, hs(k)], ident),
                     r=XN32 + ["cmask"], w=[PB[k // 4]])
            for hf in range(2):
                P.op("dve", lambda e, hf=hf: e.tensor_tensor(
                    out=dstT[:, hf * 512:(hf + 1) * 512].rearrange("p (k t) -> p k t", k=4),
                    in0=pb[hf][:].rearrange("p (k t) -> p k t", k=4),
                    in1=cols[:, wc0 + hf * 4:wc0 + hf * 4 + 4].unsqueeze(2).to_broadcast([128, 4, 128]), op=ALU.mult),
                    r=[PB[hf], "cols"], w=dstkeys)

        SSB = [SS, xgB]

        def load_states(kind, h, par):
            src = sret_d if kind == "r" else sgdn_d
            P.op("sp", lambda e: e.dma_start(out=SSB[par][:].rearrange("p (s e) -> p s e", s=16), in_=src[:, h].rearrange("s d e -> d s e")),
                 w=["SS%d" % par], dma="SS%d" % par)

        def state_mm(typ, kind, h, bank, rhs_tile, rhskeys, par=0):
            SS = SSB[par]
            if typ == "P":
                P.op("pe", lambda e: e.matmul(pb[bank][:, hs(h)], lhsT=Sp16[kind][:, hs(h)], rhs=rhs_tile[:, hs(h)], start=False, stop=True),
                     r=["Sp16" + kind.upper()] + rhskeys, w=[PB[bank]])
            else:
                for s in range(16):
                    c0 = h * 128 + s * 8
                    P.op("pe", lambda e, s=s, c0=c0: e.matmul(pb[bank][:, c0:c0 + 8], lhsT=SS[:, s * 128:(s + 1) * 128], rhs=rhs_tile[:, c0:c0 + 8], start=False, stop=(s == 15)),
                         r=["SS%d" % par] + rhskeys, w=[PB[bank]])

        def state_update(typ, kind, h, kt_tile, ktkeys, v_tile, vkeys, dec_f, dec_ap, out_d, par=0, ks=9):
            SS = SSB[par]
            if typ == "P":
                P.op("pe", lambda e: e.matmul(pb[4][:, hs(h)], lhsT=kt_tile[:, hs(h)], rhs=v_tile[:, hs(h)], start=True, stop=True),
                     r=ktkeys + vkeys, w=[PB[4]])
                key = "Sp" + kind.upper()
                sc = dec_f if dec_ap is None else dec_ap[:, h:h + 1]
                rk = [PB[4], key] + ([] if dec_ap is None else ["smallc"])
                P.op("dve", lambda e: e.scalar_tensor_tensor(out=Sp[kind][:, hs(h)], in0=Sp[kind][:, hs(h)], scalar=sc, in1=pb[4][:, hs(h)], op0=ALU.mult, op1=ALU.add),
                     r=rk, w=[key])
                P.op("pool", lambda e: e.tensor_copy(out=Sp16[kind][:, hs(h)], in_=Sp[kind][:, hs(h)]), r=[key], w=["Sp16" + kind.upper()])
            else:
                kexp = WB[:, ks * 512:(ks + 4) * 512]
                KEXP = SK(ks, 4)
                SoS = WB[:, 15 * 512:19 * 512]
                SOS = SK(15, 4)
                P.op("dve", lambda e: e.tensor_tensor(out=kexp.rearrange("p (s d) -> p s d", s=16),
                                                     in0=kt_tile[:, hs(h)].unsqueeze(1).to_broadcast([128, 16, 128]),
                                                     in1=segm.unsqueeze(2).to_broadcast([128, 16, 128]), op=ALU.mult),
                     r=ktkeys + ["cmask"], w=KEXP)
                for g4 in range(4):
                    bank = 4 + (g4 % 2)
                    for q in range(4):
                        s = g4 * 4 + q
                        P.op("pe", lambda e, s=s, q=q, bank=bank: e.matmul(pb[bank][:, hs(q)], lhsT=kexp[:, s * 128:(s + 1) * 128], rhs=v_tile[:, hs(h)], start=True, stop=True),
                             r=[KEXP[g4]] + vkeys, w=[PB[bank]])
                    if dec_ap is None:
                        P.op("dve", lambda e, g4=g4, bank=bank: e.scalar_tensor_tensor(out=SoS[:, g4 * 512:(g4 + 1) * 512], in0=SS[:, g4 * 512:(g4 + 1) * 512], scalar=dec_f, in1=pb[bank][:], op0=ALU.mult, op1=ALU.add),
                             r=[PB[bank], "SS%d" % par], w=[SOS[g4]])
                    for q in (range(4) if dec_ap is not None else ()):
                        s = g4 * 4 + q
                        sc = dec_f if dec_ap is None else dec_ap[:, s * 4 + h:s * 4 + h + 1]
                        rk = [PB[bank], "SS%d" % par] + ([] if dec_ap is None else ["smallc"])
                        P.op("dve", lambda e, s=s, q=q, sc=sc, bank=bank: e.scalar_tensor_tensor(out=SoS[:, s * 128:(s + 1) * 128], in0=SS[:, s * 128:(s + 1) * 128], scalar=sc, in1=pb[bank][:, hs(q)], op0=ALU.mult, op1=ALU.add),
                             r=rk, w=[SOS[g4]])
                    P.op("sp", lambda e, g4=g4: e.dma_start(out=out_d[g4 * 4:(g4 + 1) * 4, h].rearrange("s d e -> d s e"), in_=SoS[:, g4 * 512:(g4 + 1) * 512].rearrange("p (s e) -> p s e", s=4)),
                         r=[SOS[g4]], dma="sout")

        wup_done = [False]
        pending = [[]]

        cast_next = [0]

        def emit_casts(n):
            if 'wload2' in skip:
                return
            while n > 0 and cast_next[0] < 16:
                i = cast_next[0]
                cast_next[0] += 1
                n -= 1
                if i < 8:
                    P.op("pool", lambda e, i=i: e.dma_start(out=wup16_d[i * 128:(i + 1) * 128, :], in_=wup_d[i * 128:(i + 1) * 128, :]),
                         w=["wup16_%d" % i], dma="cast")
                else:
                    j = i - 8
                    P.op("pool", lambda e, j=j: e.dma_start(out=wdn16_d[j * 512:(j + 1) * 512, :], in_=wdn_d[j * 512:(j + 1) * 512, :]),
                         w=["wdn16_%d" % j], dma="cast")

        def emit_wup():
            wup_done[0] = True
            if 'wload2' in skip:
                return
            emit_casts(99)
            for k in range(8):
                P.op("sp", lambda e, k=k: e.dma_start(out=WA[:, k * 4096:(k + 1) * 4096], in_=wup16_d[k * 128:(k + 1) * 128, :]),
                     r=["wup16_%d" % k], w=WIN + ["wup%d" % k], dma="wup")

        def sweep1_tile(T):
            typ = "P" if T < 16 else "S"
            lp = (typ == "P")
            gsl = [gsil, gsilB][T % 2]
            GSLK = "gsil%d" % (T % 2)

            def V(ap):
                if not lp:
                    return ap
                v = ap.bitcast(BF16)
                return v[:, 0:v.shape[1] // 2]

            def TP(bank):
                return pb[bank][:].bitcast(BF16)[:, 0:512] if lp else pb[bank][:]
            identV = ident16[:] if lp else ident
            onesV = ones16[:] if lp else ones
            vr = V(v_r[:])
            CH32 = True

            def VC(ap):
                return ap if CH32 else V(ap)

            def TPC(bank):
                return pb[bank][:] if CH32 else TP(bank)
            identC = ident if CH32 else identV
            nseg = 1 if typ == "P" else 16
            L = 128 // nseg
            r0 = T * 128
            XT = [xt, xtB]
            xcur = XT[T % 2]
            XK = "xt%d" % (T % 2)

            def load_x(Tn):
                buf = XT[Tn % 2]
                P.op("sp", lambda e: e.dma_start(out=buf[:], in_=x_d[Tn * 128:(Tn + 1) * 128, :]), w=["xt%d" % (Tn % 2)], dma="xt")
            if T == tiles1[0]:
                load_x(T)
            P.op("sp", lambda e, r0=r0: e.dma_start(out=ropet[:], in_=rope_d[r0:r0 + 128, :]), w=["ropet"], dma="ropet")
            norm_transpose(xcur, [XK], 0, xnT, ["xnT"])

            if typ == "P":
                cb4 = convbuf[:, 0:12 * 131].rearrange("p (c s t) -> p c s t", c=12, s=1)
            else:
                cb4 = convbuf[:, 0:12 * 176].rearrange("p (c s t) -> p c s t", c=12, s=16)
                sct = WB[0:48, 15 * 512:18 * 512]
                P.op("sp", lambda e: e.dma_start(out=sct, in_=sconv_d), w=SK(15, 3), dma="sct")
                for c in range(12):
                    bank = 5 + c // 8
                    cc = c % 8
                    P.op("pe", lambda e, c=c, bank=bank, cc=cc: e.transpose(pb[bank][:, cc * 48:(cc + 1) * 48], sct[:, hs(c)], ident[0:48, 0:48]),
                         r=SK(15, 3) + ["cmask"], w=[PB[bank]])
                P.op("act", lambda e: e.activation(out=cb4[:, 0:8, :, 0:3], in_=pb[5][:, 0:384].rearrange("p (c s t) -> p c s t", c=8, s=16), func=AF.Copy),
                     r=[PB[5]], w=["convbuf"])
                P.op("act", lambda e: e.activation(out=cb4[:, 8:12, :, 0:3], in_=pb[6][:, 0:192].rearrange("p (c s t) -> p c s t", c=4, s=16), func=AF.Copy),
                     r=[PB[6]], w=["convbuf"])

            pend, pending[0] = pending[0], []
            TMB = (4, 0, 1) if lp else (2, 3, 4)
            NSLOT = 23
            pslot = [0]

            def play_pend():
                if pend:
                    i = pslot[0]
                    P.play(pend[(len(pend) * i) // NSLOT:(len(pend) * (i + 1)) // NSLOT])
                pslot[0] += 1
            for c in range(12):
                bank = 5 + c // 4
                for k in range(8):
                    P.op("pe", lambda e, c=c, k=k, bank=bank: e.matmul(pb[bank][:, hs(c % 4)], lhsT=win(k, 2048 + c * 128, 2048 + (c + 1) * 128), rhs=xnT[:, hs(k)], start=(k == 0), stop=(k == 7)),
                         r=[WIN[k], "xnT"], w=[PB[bank]])
                play_pend()
            for g in range(3):
                P.op("act", lambda e, g=g: e.activation(out=cb4[:, g * 4:(g + 1) * 4, :, 3:3 + L], in_=pb[5 + g][:].rearrange("p (c s t) -> p c s t", c=4, s=nseg), func=AF.Copy),
                     r=[PB[5 + g]], w=["convbuf"])
            c01 = WB[:, 0:1536]; c23 = WB[:, 1536:3072]; cq = WB[:, 3072:4608]
            C01 = SK(0, 3); C23 = SK(3, 3); CQ = SK(6, 3)
            cqv = V(SL(15)) if lp else cq[:, 1024:1536]
            def cview(buf):
                return buf.rearrange("p (c s t) -> p c s t", c=12, s=nseg)
            def cwb(t):
                return cwt[:].rearrange("p (c t) -> p c t", t=4)[:, :, t:t + 1].unsqueeze(3).to_broadcast([128, 12, nseg, L])

            def do_conv(parts=(0, 1)):
                if 0 in parts:
                    c9 = WB[:, 9 * 512:12 * 512]
                    C9 = SK(9, 3)
                    P.op("pool", lambda e: e.tensor_tensor(out=cview(c01), in0=cb4[:, :, :, 0:L], in1=cwb(0), op=ALU.mult), r=["convbuf", "cwt"], w=C01)
                    P.op("pool", lambda e: e.tensor_tensor(out=cview(c9), in0=cb4[:, :, :, 1:1 + L], in1=cwb(1), op=ALU.mult), r=["convbuf", "cwt"], w=C9)
                    P.op("dve", lambda e: e.tensor_tensor(out=cview(c23), in0=cb4[:, :, :, 2:2 + L], in1=cwb(2), op=ALU.mult), r=["convbuf", "cwt"], w=C23)
                    P.op("dve", lambda e: e.tensor_tensor(out=cview(cq), in0=cb4[:, :, :, 3:3 + L], in1=cwb(3), op=ALU.mult), r=["convbuf", "cwt"], w=CQ)
                    P.op("dve", lambda e: e.tensor_tensor(out=c23, in0=c23, in1=cq, op=ALU.add), r=C23 + CQ, w=C23)
                    P.op("dve", lambda e: e.tensor_tensor(out=c23, in0=c23, in1=c9, op=ALU.add), r=C23 + C9, w=C23)
                    P.op("dve", lambda e: e.tensor_tensor(out=c01, in0=c01, in1=c23, op=ALU.add), r=C01 + C23, w=C01)
                if 1 in parts:
                    P.op("act", lambda e: e.activation(out=cq[:, 0:1024], in_=c01[:, 0:1024], func=AF.Silu), r=C01, w=CQ)
                    P.op("act", lambda e: e.activation(out=cqv, in_=c01[:, 1024:1536], func=AF.Silu), r=C01, w=CQ + (["S15"] if lp else []))
                if typ == "P" and 0 in parts:
                    P.op("pool", lambda e: e.tensor_copy(out=cb4[:, :, :, 0:3], in_=cb4[:, :, :, 128:131]), r=["convbuf"], w=["convbuf"])
            if lp:
                do_conv((0,))
            for c in range(8):
                col0 = 1536 + c * 128 if c < 4 else 3584 + (c - 4) * 128
                for k in range(8):
                    P.op("pe", lambda e, c=c, k=k, col0=col0: e.matmul(pb[c // 4][:, hs(c % 4)], lhsT=win(k, col0, col0 + 128), rhs=xnT[:, hs(k)], start=(k == 0), stop=(k == 7)),
                         r=[WIN[k], "xnT"], w=[PB[c // 4]])
                play_pend()
            for hf in range(2):
                P.op("act", lambda e, hf=hf: e.activation(out=gsl[:, hf * 512:(hf + 1) * 512], in_=pb[hf][:], func=AF.Silu), r=[PB[hf]], w=[GSLK])
            for c in range(3):
                for k in range(8):
                    P.op("pe", lambda e, c=c, k=k: e.matmul(pb[TMB[c]][:], lhsT=xnT[:, hs(k)], rhs=win(k, c * 512, (c + 1) * 512), start=(k == 0), stop=(k == 7)),
                         r=[WIN[k], "xnT"], w=[PB[TMB[c]]])
                play_pend()
            P.op("act", lambda e: e.activation(out=vr, in_=pb[TMB[2]][:], func=AF.Copy), r=[PB[TMB[2]]], w=["v_r"])
            for k in range(8):
                P.op("pe", lambda e, k=k: e.matmul(pb[5][:, 0:8], lhsT=xnT[:, hs(k)], rhs=win(k, 4096, 4104), start=(k == 0), stop=(k == 7)),
                     r=[WIN[k], "xnT"], w=[PB[5]])
            g0 = 16 * (T % 2)
            SMG = "smg%d" % (T % 2)
            gcol = smallg[:, g0:g0 + 4]
            bcol = smallg[:, g0 + 4:g0 + 8]
            gtmp = smallg[:, g0 + 8:g0 + 12]
            P.op("dve", lambda e: e.tensor_tensor(out=gtmp, in0=pb[5][:, 0:4], in1=adt[:, 4:8], op=ALU.add), r=[PB[5], "adt"], w=[SMG])
            P.op("act", lambda e: e.activation(out=gtmp, in_=gtmp, func=AF.Exp), r=[SMG], w=[SMG])
            P.op("act", lambda e: e.activation(out=gtmp, in_=gtmp, func=AF.Ln, bias=1.0), r=[SMG], w=[SMG])
            P.op("dve", lambda e: e.tensor_tensor(out=gcol, in0=gtmp, in1=negA[:], op=ALU.mult), r=[SMG, "negA"], w=[SMG])
            P.op("act", lambda e: e.activation(out=bcol, in_=pb[5][:, 4:8], func=AF.Sigmoid), r=[PB[5]], w=[SMG])

            if T >= 15:
                rawtm = WB[:, 9 * 512:12 * 512]
                for c in range(3):
                    for k in range(8):
                        P.op("pe", lambda e, c=c, k=k: e.matmul(pb[5 + c][:], lhsT=xnT[:, hs(k)], rhs=win(k, 2048 + c * 512, 2048 + (c + 1) * 512), start=(k == 0), stop=(k == 7)),
                             r=[WIN[k], "xnT"], w=[PB[5 + c]])
                    P.op("act", lambda e, c=c: e.activation(out=rawtm[:, c * 512:(c + 1) * 512], in_=pb[5 + c][:], func=AF.Copy), r=[PB[5 + c]], w=SK(9, 3))
                if T == 15:
                    P.op("sp", lambda e: e.dma_start(out=convp_d, in_=rawtm[125:128, :]), r=SK(9, 3), dma="convo")
                else:
                    for s in range(16):
                        P.op("sp", lambda e, s=s: e.dma_start(out=convs_d[s * 3:s * 3 + 3, :], in_=rawtm[s * 8 + 5:s * 8 + 8, :]), r=SK(9, 3), dma="convo")

            if T == NT - 1:
                emit_wup()
            if lp:
                qr, kr = qrkr16[:, 0:512], qrkr16[:, 512:1024]
                QRK, KRK = "qrkr", "qrkr"
                tmp12, tmp34 = xn32[:, 0:512], xn32[:, 512:1024]
                T12K, T34K = "xn32", "xn32"
                H = lambda i, j: SL(i).bitcast(BF16)[:, j * 512:(j + 1) * 512]
                ktl, qT, kT, qTs, scm = H(3, 0), H(3, 1), H(4, 0), H(4, 1), H(5, 0)
                KTLK, QTK, KTK, QTSK, SCMK = "S3", "S3", "S4", "S4", "S5"
                tpq = pb[4][:].bitcast(BF16)[:, 0:512]
                tpk = pb[4][:].bitcast(BF16)[:, 512:1024]
                TPQB, TPKB, SCB = 4, 4, 4
            else:
                qr, kr, ktl, qT, kT, qTs, scm = [SL(i) for i in (0, 1, 4, 5, 6, 7, 8)]
                QRK, KRK, KTLK, QTK, KTK, QTSK, SCMK = "S0", "S1", "S4", "S5", "S6", "S7", "S8"
                tmp12, tmp34 = SL(2), SL(3)
                T12K, T34K = "S2", "S3"
                tpq, tpk = pb[0][:], pb[1][:]
                TPQB, TPKB, SCB = 0, 1, 2

            def do_rope():
                for (src, dst, dk, co) in ((TMB[0], qr, QRK, 0), (TMB[1], kr, KRK, 128)):
                    s4 = pb[src][:].rearrange("p (h two f) -> p h two f", h=4, two=2)
                    d4 = dst.rearrange("p (h two f) -> p h two f", h=4, two=2)
                    cosb = ropet[:, co:co + 64].unsqueeze(1).to_broadcast([128, 4, 64])
                    sinb = ropet[:, co + 64:co + 128].unsqueeze(1).to_broadcast([128, 4, 64])
                    t1 = tmp12[:, 0:256].rearrange("p (h f) -> p h f", h=4)
                    t2 = tmp12[:, 256:512].rearrange("p (h f) -> p h f", h=4)
                    t3 = tmp34[:, 0:256].rearrange("p (h f) -> p h f", h=4)
                    t4 = tmp34[:, 256:512].rearrange("p (h f) -> p h f", h=4)
                    P.op("dve", lambda e, s4=s4, t1=t1, cosb=cosb: e.tensor_tensor(out=t1, in0=s4[:, :, 0, :], in1=cosb, op=ALU.mult), r=[PB[src], "ropet"], w=[T12K])
                    P.op("dve", lambda e, s4=s4, t2=t2, sinb=sinb: e.tensor_tensor(out=t2, in0=s4[:, :, 1, :], in1=sinb, op=ALU.mult), r=[PB[src], "ropet"], w=[T12K])
                    P.op("dve", lambda e, s4=s4, t3=t3, sinb=sinb: e.tensor_tensor(out=t3, in0=s4[:, :, 0, :], in1=sinb, op=ALU.mult), r=[PB[src], "ropet"], w=[T34K])
                    P.op("dve", lambda e, s4=s4, t4=t4, cosb=cosb: e.tensor_tensor(out=t4, in0=s4[:, :, 1, :], in1=cosb, op=ALU.mult), r=[PB[src], "ropet"], w=[T34K])
                    P.op("pool", lambda e, d4=d4, t1=t1, t2=t2: e.tensor_tensor(out=d4[:, :, 0, :], in0=t1, in1=t2, op=ALU.subtract), r=[T12K], w=[dk])
                    P.op("pool", lambda e, d4=d4, t3=t3, t4=t4: e.tensor_tensor(out=d4[:, :, 1, :], in0=t3, in1=t4, op=ALU.add), r=[T34K], w=[dk])

            def do_ret():
                P.op("pool", lambda e: e.tensor_tensor(out=ktl.rearrange("p (h d) -> p h d", h=4), in0=kr.rearrange("p (h d) -> p h d", h=4),
                                                       in1=KTC[typ].unsqueeze(2).to_broadcast([128, 4, 128]), op=ALU.mult), r=[KRK, "cmask"], w=[KTLK])
                for h in range(4):
                    P.op("pe", lambda e, h=h: e.transpose(tpq[:, hs(h)], qr[:, hs(h)], identV), r=[QRK, "cmask", "c16"], w=[PB[TPQB]])
                    P.op("pe", lambda e, h=h: e.transpose(tpk[:, hs(h)], kr[:, hs(h)], identV), r=[KRK, "cmask", "c16"], w=[PB[TPKB]])
                P.op("act", lambda e: e.activation(out=qT, in_=tpq, func=AF.Copy), r=[PB[TPQB]], w=[QTK])
                P.op("dve", lambda e: e.tensor_tensor(out=qTs, in0=tpq, in1=QSC[typ], op=ALU.mult), r=[PB[TPQB], "cmask"], w=[QTSK])
                P.op("act", lambda e: e.activation(out=kT, in_=tpk, func=AF.Copy), r=[PB[TPKB]], w=[KTK])
                for h in range(4):
                    P.op("pe", lambda e, h=h: e.matmul(pb[SCB][:, hs(h)], lhsT=kT[:, hs(h)], rhs=qT[:, hs(h)], start=True, stop=True), r=[QTK, KTK], w=[PB[SCB]])
                P.op("dve", lambda e: e.tensor_tensor(out=scm, in0=pb[SCB][:], in1=DTR[typ], op=ALU.mult), r=[PB[SCB], "cmask"], w=[SCMK])
                if typ == "P":
                    for h in range(4):
                        P.op("pe", lambda e, h=h: e.matmul(pb[6][:, hs(h)], lhsT=vr[:, hs(h)], rhs=scm[:, hs(h)], start=True, stop=False), r=["v_r", SCMK], w=[PB[6]])
                        state_mm(typ, "r", h, 6, qTs, [QTSK])
                    for h in range(4):
                        state_update(typ, "r", h, ktl, [KTLK], vr, ["v_r"], GAM[h] ** L, None, srets_d)
                else:
                    load_states("r", 0, 0)
                    for h in range(4):
                        if h + 1 < 4:
                            load_states("r", h + 1, (h + 1) % 2)
                        P.op("pe", lambda e, h=h: e.matmul(pb[6][:, hs(h)], lhsT=vr[:, hs(h)], rhs=scm[:, hs(h)], start=True, stop=False), r=["v_r", SCMK], w=[PB[6]])
                        state_mm(typ, "r", h, 6, qTs, [QTSK], par=h % 2)
                        state_update(typ, "r", h, ktl, [KTLK], vr, ["v_r"], GAM[h] ** L, None, srets_d, par=h % 2, ks=9)

            if lp:
                do_conv((1,))
            do_rope()
            if lp:
                ret_ops = P.record(do_ret)
            else:
                do_ret()
                ret_ops = []

            if not lp:
                do_conv()
            sq = V(c01[:, 0:1024]); rinv = c23[:, 0:1024]
            P.op("act", lambda e: e.activation(out=sq, in_=cq[:, 0:1024], func=AF.Square), r=CQ, w=C01)
            for hf in range(2):
                P.op("pe", lambda e, hf=hf: e.matmul(pb[hf][:], lhsT=onesV, rhs=sq[:, hf * 512:(hf + 1) * 512], start=True, stop=True), r=C01 + ["cmask", "c16"], w=[PB[hf]])
                P.op("act", lambda e, hf=hf: e.activation(out=rinv[:, hf * 512:(hf + 1) * 512], in_=pb[hf][:], func=AF.Ln, bias=EPS), r=[PB[hf]], w=C23)
            P.op("act", lambda e: e.activation(out=rinv[:, 0:512], in_=rinv[:, 0:512], func=AF.Exp, scale=-0.5, bias=float(np.log(128.0 ** -0.5))), r=C23, w=C23)
            P.op("act", lambda e: e.activation(out=rinv[:, 512:1024], in_=rinv[:, 512:1024], func=AF.Exp, scale=-0.5), r=C23, w=C23)
            qkn = V(WB[:, 9 * 512:11 * 512]); QKN = SK(9, 2)
            qTn = qkn[:, 0:512]; kTn = qkn[:, 512:1024]
            P.op("dve", lambda e: e.tensor_tensor(out=qkn, in0=cq[:, 0:1024], in1=rinv, op=ALU.mult), r=CQ + C23, w=QKN)
            gU, dmatT, dmats, ecumB = SL(11), SL(12), SL(13), SL(14)
            P.op("pool", lambda e: e.tensor_tensor(out=gU.rearrange("p (h i) -> p h i", h=4), in0=U[typ].unsqueeze(1).to_broadcast([128, 4, 128]),
                                                   in1=gcol.unsqueeze(2).to_broadcast([128, 4, 128]), op=ALU.mult), r=["cmask", "smallc", SMG], w=["S11"])
            P.op("pe", lambda e: e.matmul(pb[4][:, 0:4], lhsT=U[typ], rhs=gcol, start=True, stop=True), r=["cmask", "smallc", SMG], w=[PB[4]])
            P.op("pe", lambda e: e.matmul(pb[4][:, 4:8], lhsT=MG[typ], rhs=gcol, start=True, stop=True), r=["cmask", "smallc", SMG], w=[PB[4]])
            if typ == "P":
                P.op("pe", lambda e: e.matmul(pb[4][:, 8:12], lhsT=ones, rhs=gcol, start=True, stop=True), r=["cmask", "smallc", SMG], w=[PB[4]])
            else:
                gseg = smallc[:, 100:164]
                P.op("dve", lambda e: e.tensor_tensor(out=gseg.rearrange("p (s h) -> p s h", s=16), in0=gcol.unsqueeze(1).to_broadcast([128, 16, 4]),
                                                     in1=segm.unsqueeze(2).to_broadcast([128, 16, 4]), op=ALU.mult), r=["smallc", "cmask", SMG], w=["smallc"])
                P.op("pe", lambda e: e.matmul(pb[4][:, 8:72], lhsT=ones, rhs=gseg, start=True, stop=True), r=["cmask", "smallc"], w=[PB[4]])
            ncol = 8 + 4 * nseg
            ecols = smallc[:, 28:28 + ncol]
            P.op("act", lambda e: e.activation(out=ecols, in_=pb[4][:, 0:ncol], func=AF.Exp), r=[PB[4]], w=["smallc"])
            edec = smallc[:, 36:36 + 4 * nseg]
            P.op("dve", lambda e: e.tensor_tensor(out=smallc[:, 9:13], in0=bcol, in1=smallc[:, 28:32], op=ALU.mult), r=["smallc", SMG], w=["smallc"])
            P.op("dve", lambda e: e.tensor_scalar(out=smallc[:, 2:6], in0=bcol, scalar1=-1.0, scalar2=None, op0=ALU.mult), r=["smallc", SMG], w=["smallc"])
            becol = smallc[:, 9:13]
            negb = smallc[:, 2:6]
            etail = smallc[:, 32:36]
            for h in range(4):
                P.op("pe", lambda e, h=h: e.matmul(pb[5][:, hs(h)], lhsT=MG[typ], rhs=gU[:, hs(h)], start=True, stop=True), r=["cmask", "S11"], w=[PB[5]])
                P.op("pe", lambda e, h=h: e.matmul(pb[0][:, hs(h)], lhsT=gU[:, hs(h)], rhs=MG[typ], start=True, stop=True), r=["cmask", "S11"], w=[PB[0]])
                P.op("pe", lambda e, h=h: e.matmul(pb[1][:, hs(h)], lhsT=ones, rhs=gU[:, hs(h)], start=True, stop=True), r=["cmask", "S11"], w=[PB[1]])
            P.op("act", lambda e: e.activation(out=dmatT, in_=pb[5][:], func=AF.Exp), r=[PB[5]], w=["S12"])
            P.op("act", lambda e: e.activation(out=dmats, in_=pb[0][:], func=AF.Exp), r=[PB[0]], w=["S13"])
            P.op("act", lambda e: e.activation(out=ecumB, in_=pb[1][:], func=AF.Exp), r=[PB[1]], w=["S14"])
            P.op("pool", lambda e: e.tensor_tensor(out=dmatT.rearrange("p (h i) -> p h i", h=4), in0=dmatT.rearrange("p (h i) -> p h i", h=4),
                                                   in1=CMI[typ].unsqueeze(1).to_broadcast([128, 4, 128]), op=ALU.mult), r=["S12", "cmask"], w=["S12"])
            P.op("pool", lambda e: e.tensor_tensor(out=dmats.rearrange("p (h i) -> p h i", h=4), in0=dmats.rearrange("p (h i) -> p h i", h=4),
                                                   in1=CMS[typ].unsqueeze(1).to_broadcast([128, 4, 128]), op=ALU.mult), r=["S13", "cmask"], w=["S13"])
            for h in range(4):
                P.op("pe", lambda e, h=h: e.transpose(TP(2)[:, hs(h)], kTn[:, hs(h)], identV), r=QKN + ["cmask", "c16"], w=[PB[2]])
                P.op("pe", lambda e, h=h: e.transpose(TP(3)[:, hs(h)], cqv[:, hs(h)], identV), r=CQ + ["cmask", "c16", "S15"], w=[PB[3]])
            kb, ktg, vb = VC(SL(0)), V(SL(1)), VC(SL(2))
            P.op("dve", lambda e: e.tensor_tensor(out=kb.rearrange("p (h d) -> p h d", h=4), in0=TP(2).rearrange("p (h d) -> p h d", h=4),
                                                  in1=becol.unsqueeze(2).to_broadcast([128, 4, 128]), op=ALU.mult), r=[PB[2], "smallc"] + C01, w=["S0"])
            P.op("dve", lambda e: e.tensor_tensor(out=ktg.rearrange("p (h d) -> p h d", h=4), in0=TP(2).rearrange("p (h d) -> p h d", h=4),
                                                  in1=etail.unsqueeze(2).to_broadcast([128, 4, 128]), op=ALU.mult), r=[PB[2], "smallc"], w=["S1"])
            P.op("dve", lambda e: e.tensor_tensor(out=vb.rearrange("p (h d) -> p h d", h=4), in0=TP(3).rearrange("p (h d) -> p h d", h=4),
                                                  in1=bcol.unsqueeze(2).to_broadcast([128, 4, 128]), op=ALU.mult), r=[PB[3], "smallc", SMG], w=["S2"])
            for h in range(4):
                P.op("pe", lambda e, h=h: e.matmul(pb[4][:, hs(h)], lhsT=kTn[:, hs(h)], rhs=kTn[:, hs(h)], start=True, stop=True), r=QKN, w=[PB[4]])
            NM = [(VC(SL(15)), VC(SL(16)), "S15", "S16"), (VC(SL(17)), VC(SL(18)), "S17", "S18")]
            Pc = VC(SL(19))
            Nc, Mc, NK, MK = NM[0]
            for h in range(4):
                P.op("dve", lambda e, h=h, Nc=Nc: e.scalar_tensor_tensor(out=Nc[:, hs(h)], in0=pb[4][:, hs(h)], scalar=negb[:, h:h + 1], in1=dmats[:, hs(h)], op0=ALU.mult, op1=ALU.mult),
                     r=[PB[4], "smallc", "S13"], w=[NK])
            for h in range(4):
                P.op("pe", lambda e, h=h, Nc=Nc: e.transpose(TPC(5)[:, hs(h)], Nc[:, hs(h)], identC), r=[NK, "cmask", "c16"], w=[PB[5]])
            P.op("act", lambda e, Mc=Mc: e.activation(out=Mc, in_=TPC(5), func=AF.Copy), r=[PB[5]], w=[MK])
            P.op("dve", lambda e: e.tensor_tensor(out=Pc.rearrange("p (h i) -> p h i", h=4), in0=TPC(5).rearrange("p (h i) -> p h i", h=4),
                                                  in1=ident.unsqueeze(1).to_broadcast([128, 4, 128]), op=ALU.add), r=[PB[5], "cmask"], w=["S19"])
            nlev = 7 if typ == "P" else 3
            if lp and T >= 1:
                emit_casts(2)
            HALF = [(0, (0, 1), (0, 1, 2)), (1, (2, 3), (3, 5, 7))]
            for lv in range(nlev):
                Nc, Mc, NK, MK = NM[lv % 2]
                Nn, Mn, NKn, MKn = NM[(lv + 1) % 2]
                last = (lv == nlev - 1)
                for (hi, heads, (bM, bN, bP)) in HALF:
                    sf = "ab"[hi]
                    for h in heads:
                        if not last:
                            P.op("pe", lambda e, h=h, Nc=Nc, Mc=Mc, bM=bM: e.matmul(pb[bM][:, hs(h)], lhsT=Nc[:, hs(h)], rhs=Mc[:, hs(h)], start=True, stop=True), r=[NK + sf, MK + sf], w=[PB[bM]])
                            P.op("pe", lambda e, h=h, Nc=Nc, Mc=Mc, bN=bN: e.matmul(pb[bN][:, hs(h)], lhsT=Mc[:, hs(h)], rhs=Nc[:, hs(h)], start=True, stop=True), r=[NK + sf, MK + sf], w=[PB[bN]])
                        if lv > 0:
                            P.op("pe", lambda e, h=h, Nc=Nc, bP=bP: e.matmul(pb[bP][:, hs(h)], lhsT=Nc[:, hs(h)], rhs=Pc[:, hs(h)], start=True, stop=True), r=[NK + sf, "S19" + sf], w=[PB[bP]])
                for (hi, heads, (bM, bN, bP)) in HALF:
                    sf = "ab"[hi]
                    cs = slice(heads[0] * 128, (heads[-1] + 1) * 128)
                    if not last:
                        P.op("act", lambda e, Mn=Mn, bM=bM, cs=cs: e.activation(out=Mn[:, cs], in_=pb[bM][:, cs], func=AF.Copy), r=[PB[bM]], w=[MKn + sf])
                        P.op("act", lambda e, Nn=Nn, bN=bN, cs=cs: e.activation(out=Nn[:, cs], in_=pb[bN][:, cs], func=AF.Copy), r=[PB[bN]], w=[NKn + sf])
                    if lv > 0:
                        P.op("dve", lambda e, bP=bP, cs=cs: e.tensor_tensor(out=Pc[:, cs], in0=Pc[:, cs], in1=pb[bP][:, cs], op=ALU.add), r=[PB[bP], "S19" + sf], w=["S19" + sf])
                if ret_ops:
                    n0 = (len(ret_ops) * lv) // nlev
                    n1 = (len(ret_ops) * (lv + 1)) // nlev
                    P.play(ret_ops[n0:n1])
            wTn, vnT, vnew = V(SL(3)), V(SL(4)), V(SL(5))
            for h in range(4):
                P.op("pe", lambda e, h=h: e.matmul(pb[0][:, hs(h)], lhsT=kb[:, hs(h)], rhs=Pc[:, hs(h)], start=True, stop=True), r=["S0", "S19"], w=[PB[0]])
            P.op("act", lambda e: e.activation(out=wTn, in_=pb[0][:], func=AF.Copy, scale=-1.0), r=[PB[0]] + C23, w=["S3"])
            scg, qTg = V(SL(11)), V(SL(13))
            if typ == "S":
                load_states("g", 0, 0)
            for h in range(4):
                P.op("pe", lambda e, h=h: e.matmul(pb[3][:, hs(h)], lhsT=kTn[:, hs(h)], rhs=qTn[:, hs(h)], start=True, stop=True), r=QKN, w=[PB[3]])
            P.op("dve", lambda e: e.tensor_tensor(out=scg, in0=pb[3][:], in1=dmatT, op=ALU.mult), r=[PB[3], "S12"], w=["S11"])
            P.op("pool", lambda e: e.tensor_tensor(out=qTg, in0=qTn, in1=ecumB, op=ALU.mult), r=QKN + ["S14"], w=["S13"])
            if typ == "P":
                for h in range(4):
                    P.op("pe", lambda e, h=h: e.matmul(pb[1][:, hs(h)], lhsT=vb[:, hs(h)], rhs=Pc[:, hs(h)], start=True, stop=False), r=["S2", "S19"], w=[PB[1]])
                    state_mm(typ, "g", h, 1, wTn, ["S3"])
                P.op("act", lambda e: e.activation(out=vnT, in_=pb[1][:], func=AF.Copy), r=[PB[1]], w=["S4"])
                for h in range(4):
                    P.op("pe", lambda e, h=h: e.transpose(TP(2)[:, hs(h)], vnT[:, hs(h)], identV), r=["S4", "cmask", "c16"], w=[PB[2]])
                P.op("act", lambda e: e.activation(out=vnew, in_=TP(2), func=AF.Copy), r=[PB[2]], w=["S5"])
                for h in range(4):
                    P.op("pe", lambda e, h=h: e.matmul(pb[7][:, hs(h)], lhsT=vnew[:, hs(h)], rhs=scg[:, hs(h)], start=True, stop=False), r=["S5", "S11"], w=[PB[7]])
                    state_mm(typ, "g", h, 7, qTg, ["S13"])
                for h in range(4):
                    state_update(typ, "g", h, ktg, ["S1"], vnew, ["S5"], None, edec, sgdns_d)
            else:
                for h in range(4):
                    par = h % 2
                    if h + 1 < 4:
                        load_states("g", h + 1, (h + 1) % 2)
                    P.op("pe", lambda e, h=h: e.matmul(pb[1][:, hs(h)], lhsT=vb[:, hs(h)], rhs=Pc[:, hs(h)], start=True, stop=False), r=["S2", "S19"], w=[PB[1]])
                    state_mm(typ, "g", h, 1, wTn, ["S3"], par=par)
                    P.op("act", lambda e, h=h: e.activation(out=vnT[:, hs(h)], in_=pb[1][:, hs(h)], func=AF.Copy), r=[PB[1]], w=["S4"])
                    P.op("pe", lambda e, h=h: e.transpose(TP(2)[:, hs(h)], vnT[:, hs(h)], identV), r=["S4", "cmask", "c16"], w=[PB[2]])
                    P.op("act", lambda e, h=h: e.activation(out=vnew[:, hs(h)], in_=TP(2)[:, hs(h)], func=AF.Copy), r=[PB[2]], w=["S5"])
                    P.op("pe", lambda e, h=h: e.matmul(pb[7][:, hs(h)], lhsT=vnew[:, hs(h)], rhs=scg[:, hs(h)], start=True, stop=False), r=["S5", "S11"], w=[PB[7]])
                    state_mm(typ, "g", h, 7, qTg, ["S13"], par=par)
                    state_update(typ, "g", h, ktg, ["S1"], vnew, ["S5"], None, edec, sgdns_d, par=par, ks=6)

            if T in tiles1 and tiles1.index(T) + 1 < len(tiles1):
                load_x(tiles1[tiles1.index(T) + 1])
            oS = WB[:, 13 * 512:15 * 512]; OSK = SK(13, 2)
            sqo = WB[:, 9 * 512:11 * 512]; SQO = SK(9, 2)
            rso = WB[:, 11 * 512:13 * 512]; RSO = SK(11, 2)
            gsC = WB[:, 18 * 512:20 * 512]; GSK = SK(18, 2)
            ptmp = WB[:, 16 * 512:18 * 512]; PTK = SK(16, 2)
            for hf in range(2):
                P.op("act", lambda e, hf=hf: e.activation(out=oS[:, hf * 512:(hf + 1) * 512], in_=pb[6 + hf][:], func=AF.Copy), r=[PB[6 + hf]], w=OSK)

            def do_out():
                for hf in range(2):
                    P.op("act", lambda e, hf=hf: e.activation(out=V(sqo)[:, hf * 512:(hf + 1) * 512], in_=oS[:, hf * 512:(hf + 1) * 512], func=AF.Square), r=OSK, w=SQO)
                    P.op("pe", lambda e, hf=hf: e.matmul(pb[2 + hf][:], lhsT=onesV, rhs=V(sqo)[:, hf * 512:(hf + 1) * 512], start=True, stop=True), r=SQO + ["cmask", "c16"], w=[PB[2 + hf]])
                    P.op("act", lambda e, hf=hf: e.activation(out=rso[:, hf * 512:(hf + 1) * 512], in_=pb[2 + hf][:], func=AF.Ln, scale=1.0 / 128, bias=EPS), r=[PB[2 + hf]], w=RSO)
                P.op("act", lambda e: e.activation(out=rso, in_=rso, func=AF.Exp, scale=-0.5), r=RSO, w=RSO)
                for hf in range(2):
                    P.op("dve", lambda e, hf=hf: e.tensor_tensor(out=sqo[:, hf * 512:(hf + 1) * 512], in0=oS[:, hf * 512:(hf + 1) * 512], in1=rso[:, hf * 512:(hf + 1) * 512], op=ALU.mult), r=OSK + RSO, w=SQO)
                    P.op("dve", lambda e, hf=hf: e.scalar_tensor_tensor(out=mixT[:, hf * 512:(hf + 1) * 512], in0=sqo[:, hf * 512:(hf + 1) * 512], scalar=cols[:, 16 + hf:17 + hf], in1=gsl[:, hf * 512:(hf + 1) * 512], op0=ALU.mult, op1=ALU.mult),
                         r=SQO + ["cols", GSLK], w=["mixT"])
                for hf in range(2):
                    for k in range(8):
                        P.op("pe", lambda e, hf=hf, k=k: e.matmul(pb[2 + hf][:], lhsT=mixT[:, hs(k)], rhs=WA[:, 32832 + k * 1024 + hf * 512:32832 + k * 1024 + (hf + 1) * 512], start=(k == 0), stop=(k == 7)),
                             r=["mixT", WOUT[k]], w=[PB[2 + hf]])
                for hf in range(2):
                    P.op("act", lambda e, hf=hf: e.activation(out=ptmp[:, hf * 512:(hf + 1) * 512], in_=pb[2 + hf][:], func=AF.Square, accum_out=smallo[:, hf:hf + 1]), r=[PB[2 + hf]], w=PTK + ["smallo"])
                P.op("dve", lambda e: e.tensor_tensor(out=smallo[:, 2:3], in0=smallo[:, 0:1], in1=smallo[:, 1:2], op=ALU.add), r=["smallo"], w=["smallo"])
                P.op("act", lambda e: e.activation(out=smallo[:, 3:4], in_=smallo[:, 2:3], func=AF.Ln, scale=1.0 / D, bias=EPS), r=["smallo"], w=["smallo"])
                P.op("act", lambda e: e.activation(out=smallo[:, 4:5], in_=smallo[:, 3:4], func=AF.Exp, scale=-0.5), r=["smallo"], w=["smallo"])
                for hf in range(2):
                    P.op("dve", lambda e, hf=hf: e.scalar_tensor_tensor(out=ptmp[:, hf * 512:(hf + 1) * 512], in0=pb[2 + hf][:], scalar=smallo[:, 4:5], in1=rows[:, hf * 512:(hf + 1) * 512], op0=ALU.mult, op1=ALU.mult),
                         r=[PB[2 + hf], "smallo", "rows"], w=PTK)
                P.op("pool", lambda e: e.tensor_tensor(out=x1t[:], in0=ptmp, in1=xcur[:], op=ALU.add), r=PTK + [XK], w=["x1t"])
                P.op("sp", lambda e, r0=r0: e.dma_start(out=x1_d[r0:r0 + 128, :], in_=x1t[:]), r=["x1t"], w=["x1d%d" % T], dma="x1o")

            nxt = tiles1[tiles1.index(T) + 1] if tiles1.index(T) + 1 < len(tiles1) else None
            if lp and nxt is not None and nxt == T + 1 and nxt <= 14:
                pending[0] = P.record(do_out)
            else:
                do_out()

        for T in tiles1:
            sweep1_tile(T)

        if 'spo' not in skip:
          P.op("sp", lambda e: e.dma_start(out=sretp_d.rearrange("h d e -> d h e"), in_=Sp["r"][:].rearrange("p (h e) -> p h e", h=4)), r=["SpR"], dma="spo")
        if 'spo' not in skip:
          P.op("sp", lambda e: e.dma_start(out=sgdnp_d.rearrange("h d e -> d h e"), in_=Sp["g"][:].rearrange("p (h e) -> p h e", h=4)), r=["SpG"], dma="spo")

        if not wup_done[0]:
            emit_wup()
        xg = [SS, xgB]
        XG = ["xg0", "xg1"]
        cbb = convbuf.bitcast(BF16)
        hT = [cbb[:, 0:2048], cbb[:, 2048:4096]]
        HT = ["hT0", "hT1"]
        h32 = gsil
        post32 = xt
        r32 = [v_r[:, 0:256], v_r[:, 256:512]]
        R32 = ["r32_0", "r32_1"]
        actr = [mixT[:, i * 256:(i + 1) * 256] for i in range(4)]
        ACTR = ["act%d" % i for i in range(4)]
        P.barrier_keys(XG + HT + ["h32", "post32", "sm2", "rows"] + R32 + ACTR)
        P.op("sp", lambda e: e.dma_start(out=rows[:], in_=rows_d[:, 1024:2048]), w=["rows"], dma="rows")
        groups = []
        t2 = sorted(tiles2)
        i = 0
        while i < len(t2):
            if i + 1 < len(t2) and t2[i + 1] == t2[i] + 1 and t2[i] < 15 and t2[i + 1] <= 15:
                groups.append([t2[i], t2[i + 1]]); i += 2
            else:
                groups.append([t2[i]]); i += 1

        def wdn(fc, hf):
            if fc < 8:
                return WA[:, 32768 + fc * 1024 + hf * 512:32768 + fc * 1024 + (hf + 1) * 512]
            return WBb[:, (fc - 8) * 1024 + hf * 512:(fc - 8) * 1024 + (hf + 1) * 512]

        def prep(g):
            p = g % 2
            tl = groups[g]
            nt = len(tl)
            NTOK = nt * 128
            r0 = tl[0] * 128
            P.op("sp", lambda e: e.dma_start(out=xg[p][:, 0:nt * 1024].rearrange("p (t d) -> p t d", t=nt),
                                             in_=x1_d[r0:r0 + NTOK, :].rearrange("(t p) d -> p t d", p=128)),
                 r=["x1d%d" % T for T in tl], w=[XG[p]], dma="xg%d" % p)
            for ti in range(nt):
                xs = xg[p][:, ti * 1024:(ti + 1) * 1024]
                c0 = 8 * ti
                P.op("act", lambda e, xs=xs, c0=c0: e.activation(out=h32[:], in_=xs, func=AF.Square, accum_out=small2[:, c0:c0 + 1]), r=[XG[p]], w=["h32", "sm2"])
                P.op("act", lambda e, c0=c0: e.activation(out=small2[:, c0 + 1:c0 + 2], in_=small2[:, c0:c0 + 1], func=AF.Ln, scale=1.0 / D, bias=EPS), r=["sm2"], w=["sm2"])
                P.op("act", lambda e, c0=c0: e.activation(out=small2[:, c0 + 2:c0 + 3], in_=small2[:, c0 + 1:c0 + 2], func=AF.Exp, scale=-0.5), r=["sm2"], w=["sm2"])
                P.op("dve", lambda e, xs=xs, c0=c0: e.tensor_scalar(out=h32[:], in0=xs, scalar1=small2[:, c0 + 2:c0 + 3], scalar2=None, op0=ALU.mult), r=[XG[p], "sm2"], w=["h32"])
                hv = hT[p][:, 0:8 * NTOK].rearrange("p (k t) -> p k t", k=8)
                for hf in range(2):
                    for k in range(hf * 4, hf * 4 + 4):
                        P.op("pe", lambda e, k=k: e.transpose(pb[7][:, hs(k % 4)], h32[:, hs(k)], ident), r=["h32", "cmask"], w=[PB[7]])
                    P.op("dve", lambda e, hf=hf, hv=hv, ti=ti: e.tensor_tensor(
                        out=hv[:, hf * 4:hf * 4 + 4, ti * 128:(ti + 1) * 128],
                        in0=pb[7][:].rearrange("p (k t) -> p k t", k=4),
                        in1=cols[:, 8 + hf * 4:8 + hf * 4 + 4].unsqueeze(2).to_broadcast([128, 4, 128]), op=ALU.mult),
                        r=[PB[7], "cols"], w=[HT[p]])

        def down(g, fc):
            tl = groups[g]
            for ti in range(len(tl)):
                for hf in range(2):
                    P.op("pe", lambda e, ti=ti, hf=hf: e.matmul(pb[ti * 2 + hf][:], lhsT=actr[fc % 4][:, ti * 128:(ti + 1) * 128], rhs=wdn(fc, hf), start=(fc == 0), stop=(fc == 31)),
                         r=[ACTR[fc % 4], "wdn%d" % fc], w=[PB[ti * 2 + hf]])

        def main(g):
            p = g % 2
            tl = groups[g]
            NTOK = len(tl) * 128
            for fc in range(32):
                bank = 4 + fc % 3
                for k in range(8):
                    P.op("pe", lambda e, fc=fc, k=k, bank=bank: e.matmul(pb[bank][:, 0:NTOK], lhsT=WA[:, k * 4096 + fc * 128:k * 4096 + (fc + 1) * 128], rhs=hT[p][:, k * NTOK:(k + 1) * NTOK], start=(k == 0), stop=(k == 7)),
                         r=[WUP[k], HT[p]], w=[PB[bank]])
                P.op("act", lambda e, fc=fc, bank=bank: e.activation(out=r32[fc % 2][:, 0:NTOK], in_=pb[bank][:, 0:NTOK], func=AF.Relu), r=[PB[bank]], w=[R32[fc % 2]])
                P.op("pool", lambda e, fc=fc: e.tensor_tensor(out=actr[fc % 4][:, 0:NTOK], in0=r32[fc % 2][:, 0:NTOK], in1=r32[fc % 2][:, 0:NTOK], op=ALU.mult), r=[R32[fc % 2]], w=[ACTR[fc % 4]])
                if fc >= 1:
                    down(g, fc - 1)
                if fc == 12 and g + 1 < len(groups):
                    prep(g + 1)
            down(g, 31)

        def post(g):
            p = g % 2
            tl = groups[g]
            for ti, T in enumerate(tl):
                xs = xg[p][:, ti * 1024:(ti + 1) * 1024]
                c0 = 16 + 8 * ti
                for hf in range(2):
                    P.op("act", lambda e, hf=hf, ti=ti, c0=c0: e.activation(out=post32[:, hf * 512:(hf + 1) * 512], in_=pb[ti * 2 + hf][:], func=AF.Square, accum_out=small2[:, c0 + hf:c0 + hf + 1]), r=[PB[ti * 2 + hf]], w=["post32", "sm2"])
                P.op("dve", lambda e, c0=c0: e.tensor_tensor(out=small2[:, c0 + 2:c0 + 3], in0=small2[:, c0:c0 + 1], in1=small2[:, c0 + 1:c0 + 2], op=ALU.add), r=["sm2"], w=["sm2"])
                P.op("act", lambda e, c0=c0: e.activation(out=small2[:, c0 + 3:c0 + 4], in_=small2[:, c0 + 2:c0 + 3], func=AF.Ln, scale=1.0 / D, bias=EPS), r=["sm2"], w=["sm2"])
                P.op("act", lambda e, c0=c0: e.activation(out=small2[:, c0 + 4:c0 + 5], in_=small2[:, c0 + 3:c0 + 4], func=AF.Exp, scale=-0.5), r=["sm2"], w=["sm2"])
                for hf in range(2):
                    P.op("dve", lambda e, hf=hf, ti=ti, c0=c0: e.scalar_tensor_tensor(out=post32[:, hf * 512:(hf + 1) * 512], in0=pb[ti * 2 + hf][:], scalar=small2[:, c0 + 4:c0 + 5], in1=rows[:, hf * 512:(hf + 1) * 512], op0=ALU.mult, op1=ALU.mult),
                         r=[PB[ti * 2 + hf], "sm2", "rows"], w=["post32"])
                P.op("pool", lambda e, xs=xs: e.tensor_tensor(out=xs, in0=post32[:], in1=xs, op=ALU.add), r=["post32", XG[p]], w=[XG[p]])
                P.op("sp", lambda e, xs=xs, T=T: e.dma_start(out=y_d[T * 128:(T + 1) * 128, :], in_=xs), r=[XG[p]], dma="yo")

        if groups:
            prep(0)
        ALLW = WIN + WOUT
        ALLS = SK(0, 20) + XN32
        for fc in range(0 if 'wload2' not in skip else 99, 32):
            if fc < 8:
                dst = WA[:, 32768 + fc * 1024:32768 + (fc + 1) * 1024]
                wk = ALLW
            else:
                dst = WBb[:, (fc - 8) * 1024:(fc - 7) * 1024]
                wk = ALLS
            P.op("sp", lambda e, fc=fc, dst=dst: e.dma_start(out=dst, in_=wdn16_d[fc * 128:(fc + 1) * 128, :]),
                 r=["wdn16_%d" % (fc // 4)], w=wk + ["wdn%d" % fc], dma="wdn")
        WUP = ["wup%d" % k for k in range(8)]
        if "wload2" not in skip:
            for gq in range(4):
                P.regroup(["wdn%d" % fc for fc in range(gq * 8, gq * 8 + 8)], "wdn%d" % gq)
        for g in range(len(groups)):
            main(g)
            post(g)

        allops, P.collect = P.collect, None
        seg = []
        segi = 0
        for o in allops + [("barrier", None, (), (), None)]:
            if o[0] == "barrier":
                global _SCHED_EPS
                _SCHED_EPS = _SEG_EPS[min(segi, len(_SEG_EPS) - 1)]
                segi += 1
                for (eng, fn, r, w, dma) in (_schedule(seg) if sched else seg):
                    P.op(eng, fn, r, w, dma)
                seg = []
                if o[2]:
                    P.barrier_keys(o[2])
            else:
                seg.append(o)
        P.emit(nc, st)
    return nc


def _consts():
    f = np.float32
    out = []
    idx = np.arange(128)
    for typ in "PS":
        L = 128 if typ == "P" else 8
        seg = idx // L
        same = seg[:, None] == seg[None, :]
        U = ((idx[:, None] <= idx[None, :]) & same).astype(f)
        MG = ((idx[:, None] > idx[None, :]) & same).astype(f)
        CMI = ((idx[None, :] >= idx[:, None]) & same).astype(f)
        CMS = ((idx[:, None] > idx[None, :]) & same).astype(f)
        out += [U, MG, CMI, CMS]
    gam = 1.0 - 2.0 ** (-5.0 - np.arange(4, dtype=np.float64))
    ktc = []
    for typ in "PS":
        L = 128 if typ == "P" else 8
        seg = idx // L
        pos = idx % L
        same = seg[:, None] == seg[None, :]
        dt = np.zeros((128, 512), f)
        qs = np.zeros((128, 512), f)
        for h in range(4):
            dmat = np.where((idx[None, :] >= idx[:, None]) & same, gam[h] ** (idx[None, :] - idx[:, None]).clip(0), 0.0)
            dt[:, h * 128:(h + 1) * 128] = dmat
            qs[:, h * 128:(h + 1) * 128] = np.broadcast_to((gam[h] ** (pos + 1))[None, :], (128, 128))
        out += [dt, qs]
        ktc.append(np.stack([gam[h] ** (L - 1 - pos) for h in range(4)], 1).astype(f))
    out.append(np.eye(128, dtype=f))
    out.append(np.ones((128, 128), f))
    out += ktc
    segm = (idx[:, None] // 8 == np.arange(16)[None, :]).astype(f)
    out.append(segm)
    cm = np.concatenate([np.asarray(a, f) for a in out], axis=1)
    inv_freq = (10000.0 ** (-np.arange(0, 128, 2, dtype=np.float32) / np.float32(128))).astype(f)
    pos = np.concatenate([np.arange(2048, dtype=f), (16384 + (np.arange(128) % 8)).astype(f)])
    ang = (pos[:, None] * inv_freq[None, :]).astype(f).astype(np.float64)
    c, s = np.cos(ang), np.sin(ang)
    sc = 128.0 ** -0.5
    rope = np.concatenate([c, s, c * sc, s * sc], 1).astype(f)
    return np.ascontiguousarray(cm), np.ascontiguousarray(rope)


_NC = None


def kernel(x_prompt, x_sample, state_ret, state_gdn, state_conv, pre_mix_w, w_in, conv_w, A_log,
           dt_bias, ret_norm_w, gdn_norm_w, w_out, post_mix_w, pre_mlp_w, w_up, w_down, post_mlp_w):
    global _NC
    f = np.float32
    a = lambda v: np.ascontiguousarray(np.asarray(v, dtype=f))
    x_prompt, x_sample = a(x_prompt), a(x_sample)
    state_ret, state_gdn, state_conv = a(state_ret), a(state_gdn), a(state_conv)
    cm, rope = _consts()
    colsv = np.zeros((128, 32), f)
    colsv[:, 0:8] = a(pre_mix_w)[0].reshape(8, 128).T
    colsv[:, 8:16] = a(pre_mlp_w)[0].reshape(8, 128).T
    colsv[:, 16] = a(ret_norm_w)[0]
    colsv[:, 17] = a(gdn_norm_w)[0]
    rowsv = np.ascontiguousarray(np.broadcast_to(np.concatenate([a(post_mix_w)[0], a(post_mlp_w)[0]])[None, :], (128, 2048)))
    cwv = np.ascontiguousarray(a(conv_w)[0].reshape(4, 12, 128).transpose(2, 1, 0).reshape(128, 48))
    adv = np.ascontiguousarray(np.broadcast_to(np.concatenate([a(A_log)[0], a(dt_bias)[0]])[None, :], (128, 8)))
    shared = {"w_in": a(w_in)[0], "w_out": a(w_out)[0], "w_up": a(w_up)[0], "w_down": a(w_down)[0],
              "cols": colsv, "rows": rowsv, "cw": cwv, "ad": adv, "rope": rope, "cmask": cm}
    in_maps = []
    for b in range(8):
        m = dict(shared)
        m["x"] = np.ascontiguousarray(np.concatenate([x_prompt[b], x_sample[16 * b:16 * b + 16].reshape(128, D)], 0))
        m["sret"] = np.ascontiguousarray(state_ret[0, 16 * b:16 * b + 16])
        m["sgdn"] = np.ascontiguousarray(state_gdn[0, 16 * b:16 * b + 16])
        m["sconv"] = np.ascontiguousarray(state_conv[0, 16 * b:16 * b + 16].reshape(48, 1536))
        in_maps.append(m)
    if _NC is None:
        _NC = build_nc()
    res = run_bass_kernel_spmd(_NC, in_maps, core_ids=list(range(8)))
    R = res.results
    yp = np.stack([R[b]["y"][:2048] for b in range(8)], 0)
    ys = np.concatenate([R[b]["y"][2048:].reshape(16, 8, D) for b in range(8)], 0)
    rp = np.stack([R[b]["sret_p"] for b in range(8)], 0)[None]
    gp = np.stack([R[b]["sgdn_p"] for b in range(8)], 0)[None]
    cp = np.stack([R[b]["conv_p"] for b in range(8)], 0)[None]
    rs = np.concatenate([R[b]["sret_s"] for b in range(8)], 0)[None]
    gs = np.concatenate([R[b]["sgdn_s"] for b in range(8)], 0)[None]
    cs = np.concatenate([R[b]["conv_s"].reshape(16, 3, 1536) for b in range(8)], 0)[None]
    return tuple(np.ascontiguousarray(v.astype(f)) for v in (yp, ys, rp, gp, cp, rs, gs, cs))
```
